# Optimizing a Trainium2 kernel written in Bass

```python
import jax, jax.numpy as jnp
from jax import lax
import numpy as np

D_MODEL = 1024
BATCH = 4
SEQ = 4096
DEPTH = 2
DEC_BATCH = 128
DEC_SEQ = 4
PAST_LEN = 2048
PAGE_SIZE = 128

N_MIXERS = 2
N_CONV_LAYERS = (DEPTH + N_MIXERS - 1) // N_MIXERS
N_ATTN_LAYERS = DEPTH // N_MIXERS
CONV_WIDTH = 31
D_CONV = D_MODEL
N_HEADS = 16
HEAD_DIM = D_MODEL // N_HEADS
Q_BLOCK = 128
D_FF = 2816
FFN_CONV_WIDTH = 3
LOGIT_BIAS_INIT = -6.0
EPS = 1e-6

kernel_name = "hybrid_conformer_conv_stick_breaking_step"


def rms_norm(x, g):
    xf = x.astype(jnp.float32)
    y = xf * lax.rsqrt(jnp.mean(xf * xf, axis=-1, keepdims=True) + EPS)
    return (y * g.astype(jnp.float32)).astype(x.dtype)


def layer_norm(x, g, b):
    xf = x.astype(jnp.float32)
    mu = jnp.mean(xf, axis=-1, keepdims=True)
    var = jnp.mean(jnp.square(xf - mu), axis=-1, keepdims=True)
    y = (xf - mu) * lax.rsqrt(var + EPS)
    return (y * g.astype(jnp.float32) + b.astype(jnp.float32)).astype(x.dtype)


def causal_dwconv(x_ext, w, b):
    c = x_ext.shape[-1]
    y = lax.conv_general_dilated(x_ext, w[:, None, :].astype(x_ext.dtype), window_strides=(1,), padding='VALID',
                                 dimension_numbers=('NWC', 'WIO', 'NWC'), feature_group_count=c)
    return y + b.astype(y.dtype)


def conformer_conv(h, past, w_in, b_in, w_dw, b_dw, ln_g, ln_b, w_out, b_out):
    u = h @ w_in + b_in
    a, gate = jnp.split(u, 2, axis=-1)
    glu = a * jax.nn.sigmoid(gate)
    ext = jnp.concatenate([past, glu], axis=1)
    y = causal_dwconv(ext, w_dw, b_dw)
    y = jax.nn.silu(layer_norm(y, ln_g, ln_b))
    return y @ w_out + b_out, ext[:, -(CONV_WIDTH - 1):]


def conv_ffn(h, past, w_gate, w_up, w_dw, b_dw, w_down):
    g = h @ w_gate
    ext = jnp.concatenate([past, g], axis=1)
    g_conv = causal_dwconv(ext, w_dw, b_dw)
    out = (jax.nn.silu(g_conv) * (h @ w_up)) @ w_down
    return out, ext[:, -(FFN_CONV_WIDTH - 1):]


def stick_breaking_block(q, k, v, bias, q_start):
    tq, tk = q.shape[1], k.shape[1]
    z = jnp.einsum('bqhd,bkhd->bhqk', q.astype(jnp.float32), k.astype(jnp.float32)) * (HEAD_DIM ** -0.5)
    z = z + bias.astype(jnp.float32)[None, :, None, None]
    q_pos = q_start + jnp.arange(tq)
    k_pos = jnp.arange(tk)
    valid = k_pos[None, :] < q_pos[:, None]
    log_fail = jnp.where(valid, jax.nn.log_sigmoid(-z), 0.0)
    after = lax.cumsum(log_fail, axis=3, reverse=True) - log_fail
    weights = jnp.where(valid, jnp.exp(jax.nn.log_sigmoid(z) + after), 0.0)
    out = jnp.einsum('bhqk,bkhd->bqhd', weights, v.astype(jnp.float32))
    return out.astype(v.dtype)


def stick_breaking_attention(q, k, v, bias, q_start):
    tq = q.shape[1]
    outs = []
    for b0 in range(0, tq, Q_BLOCK):
        b1 = min(b0 + Q_BLOCK, tq)
        k_end = q_start + b1
        outs.append(stick_breaking_block(q[:, b0:b1], k[:, :k_end], v[:, :k_end], bias, q_start + b0))
    return jnp.concatenate(outs, axis=1)


def run_trunk(x, conv_past, ffn_past, k_past, v_past, past_len,
              norm_mix, norm_ffn, norm_final, conv_w_in, conv_b_in, conv_w_dw, conv_b_dw, conv_ln_g,
              conv_ln_b, conv_w_out, conv_b_out, attn_w_qkv, attn_w_o, attn_b_logit, ffn_w_gate, ffn_w_up,
              ffn_w_dw, ffn_b_dw, ffn_w_down):
    b, t, _ = x.shape
    new_conv, new_ffn, new_k, new_v = [], [], [], []
    for i in range(DEPTH):
        h = rms_norm(x, norm_mix[i])
        if i % N_MIXERS == 0:
            c = i // N_MIXERS
            out, st = conformer_conv(h, conv_past[c], conv_w_in[c], conv_b_in[c], conv_w_dw[c], conv_b_dw[c],
                                     conv_ln_g[c], conv_ln_b[c], conv_w_out[c], conv_b_out[c])
            new_conv.append(st)
        else:
            a = i // N_MIXERS
            qkv = (h @ attn_w_qkv[a]).reshape(b, t, 3, N_HEADS, HEAD_DIM)
            q, k, v = qkv[:, :, 0], qkv[:, :, 1], qkv[:, :, 2]
            k_all = jnp.concatenate([k_past[a], k], axis=1)
            v_all = jnp.concatenate([v_past[a], v], axis=1)
            att = stick_breaking_attention(q, k_all, v_all, attn_b_logit[a], past_len)
            out = att.reshape(b, t, N_HEADS * HEAD_DIM) @ attn_w_o[a]
            new_k.append(k)
            new_v.append(v)
        x = x + out
        h = rms_norm(x, norm_ffn[i])
        out, st = conv_ffn(h, ffn_past[i], ffn_w_gate[i], ffn_w_up[i], ffn_w_dw[i], ffn_b_dw[i], ffn_w_down[i])
        new_ffn.append(st)
        x = x + out
    y = rms_norm(x, norm_final)
    return y, jnp.stack(new_conv), jnp.stack(new_ffn), jnp.stack(new_k), jnp.stack(new_v)


def setup_inputs(seed: int = 0) -> dict:
    key = jax.random.key(seed)
    ks = jax.random.split(key, 32)
    n_pages = PAST_LEN // PAGE_SIZE
    n_used = DEC_BATCH * n_pages
    n_pool = (5 * n_used + 3) // 4
    f32 = jnp.float32

    def nrm(k, shape, scale):
        return jax.random.normal(k, shape, f32) * scale

    page_table = jax.random.permutation(ks[0], n_pool)[:n_used].reshape(DEC_BATCH, n_pages).astype(jnp.int32)
    return {
        "x_prompt": nrm(ks[1], (BATCH, SEQ, D_MODEL), 1.0),
        "x_sample": nrm(ks[2], (DEC_BATCH, DEC_SEQ, D_MODEL), 1.0),
        "state_conv": nrm(ks[3], (N_CONV_LAYERS, DEC_BATCH, CONV_WIDTH - 1, D_CONV), 1.0),
        "state_ffn": nrm(ks[4], (DEPTH, DEC_BATCH, FFN_CONV_WIDTH - 1, D_FF), 1.0),
        "cache_k": nrm(ks[5], (N_ATTN_LAYERS, n_pool, PAGE_SIZE, N_HEADS, HEAD_DIM), 1.0),
        "cache_v": nrm(ks[6], (N_ATTN_LAYERS, n_pool, PAGE_SIZE, N_HEADS, HEAD_DIM), 1.0),
        "page_table": page_table,
        "norm_mix": 1.0 + nrm(ks[7], (DEPTH, D_MODEL), 0.02),
        "norm_ffn": 1.0 + nrm(ks[8], (DEPTH, D_MODEL), 0.02),
        "norm_final": 1.0 + nrm(ks[9], (D_MODEL,), 0.02),
        "conv_w_in": nrm(ks[10], (N_CONV_LAYERS, D_MODEL, 2 * D_CONV), D_MODEL ** -0.5),
        "conv_b_in": nrm(ks[11], (N_CONV_LAYERS, 2 * D_CONV), 0.02),
        "conv_w_dw": nrm(ks[12], (N_CONV_LAYERS, CONV_WIDTH, D_CONV), CONV_WIDTH ** -0.5),
        "conv_b_dw": nrm(ks[13], (N_CONV_LAYERS, D_CONV), 0.02),
        "conv_ln_g": 1.0 + nrm(ks[14], (N_CONV_LAYERS, D_CONV), 0.02),
        "conv_ln_b": nrm(ks[15], (N_CONV_LAYERS, D_CONV), 0.02),
        "conv_w_out": nrm(ks[16], (N_CONV_LAYERS, D_CONV, D_MODEL), D_CONV ** -0.5),
        "conv_b_out": nrm(ks[17], (N_CONV_LAYERS, D_MODEL), 0.02),
        "attn_w_qkv": nrm(ks[18], (N_ATTN_LAYERS, D_MODEL, 3 * N_HEADS * HEAD_DIM), D_MODEL ** -0.5),
        "attn_w_o": nrm(ks[19], (N_ATTN_LAYERS, N_HEADS * HEAD_DIM, D_MODEL), (N_HEADS * HEAD_DIM) ** -0.5),
        "attn_b_logit": LOGIT_BIAS_INIT + nrm(ks[25], (N_ATTN_LAYERS, N_HEADS), 0.5),
        "ffn_w_gate": nrm(ks[20], (DEPTH, D_MODEL, D_FF), D_MODEL ** -0.5),
        "ffn_w_up": nrm(ks[21], (DEPTH, D_MODEL, D_FF), D_MODEL ** -0.5),
        "ffn_w_dw": nrm(ks[22], (DEPTH, FFN_CONV_WIDTH, D_FF), FFN_CONV_WIDTH ** -0.5),
        "ffn_b_dw": nrm(ks[23], (DEPTH, D_FF), 0.02),
        "ffn_w_down": nrm(ks[24], (DEPTH, D_FF, D_MODEL), D_FF ** -0.5),
    }


def reference(x_prompt, x_sample, state_conv, state_ffn, cache_k, cache_v, page_table,
              norm_mix, norm_ffn, norm_final, conv_w_in, conv_b_in, conv_w_dw, conv_b_dw, conv_ln_g,
              conv_ln_b, conv_w_out, conv_b_out, attn_w_qkv, attn_w_o, attn_b_logit, ffn_w_gate, ffn_w_up,
              ffn_w_dw, ffn_b_dw, ffn_w_down):
    weights = (norm_mix, norm_ffn, norm_final, conv_w_in, conv_b_in, conv_w_dw, conv_b_dw, conv_ln_g,
               conv_ln_b, conv_w_out, conv_b_out, attn_w_qkv, attn_w_o, attn_b_logit, ffn_w_gate, ffn_w_up,
               ffn_w_dw, ffn_b_dw, ffn_w_down)
    dt = x_prompt.dtype
    b = x_prompt.shape[0]
    conv0 = jnp.zeros((N_CONV_LAYERS, b, CONV_WIDTH - 1, D_CONV), dt)
    ffn0 = jnp.zeros((DEPTH, b, FFN_CONV_WIDTH - 1, D_FF), dt)
    kv0 = [jnp.zeros((b, 0, N_HEADS, HEAD_DIM), dt) for _ in range(N_ATTN_LAYERS)]
    y_p, sc_p, sf_p, k_p, v_p = run_trunk(x_prompt, conv0, ffn0, kv0, kv0, 0, *weights)

    db = x_sample.shape[0]
    past_len = page_table.shape[1] * PAGE_SIZE
    k_past = [jnp.take(cache_k[a], page_table, axis=0).reshape(db, past_len, N_HEADS, HEAD_DIM)
              for a in range(N_ATTN_LAYERS)]
    v_past = [jnp.take(cache_v[a], page_table, axis=0).reshape(db, past_len, N_HEADS, HEAD_DIM)
              for a in range(N_ATTN_LAYERS)]
    y_s, sc_s, sf_s, k_s, v_s = run_trunk(x_sample, state_conv, state_ffn, k_past, v_past, past_len, *weights)
    return (y_p, y_s, sc_p, sf_p, k_p, v_p, sc_s, sf_s, k_s, v_s)
```

```python
import numpy as np
import ml_dtypes
import concourse.bass as bass
import concourse.mybir as mybir
from concourse.bass_utils import run_bass_kernel_spmd

F32 = mybir.dt.float32
BF16 = mybir.dt.bfloat16
I32 = mybir.dt.int32
AF = mybir.ActivationFunctionType
ALU = mybir.AluOpType
NEG = -30000.0
EPS = 1e-6
CONV_W = 31
FFN_W = 3
DH = 64
TQ = 4
PAGE = 128


class Trk:
    __slots__ = ("w", "r")

    def __init__(self):
        self.w = None
        self.r = {}


class Prog:
    ENG = ("pe", "act", "dve", "pool", "sp")

    def __init__(self, nc, ndma=24):
        self.nc = nc
        self.h = {"pe": nc.tensor, "act": nc.scalar, "dve": nc.vector, "pool": nc.gpsimd, "sp": nc.sync}
        self.sem = {e: nc.alloc_semaphore("s_" + e) for e in self.ENG}
        self.ndma = ndma
        for k in range(ndma):
            self.sem["d%d" % k] = nc.alloc_semaphore("s_d%d" % k)
        self.cnt = {e: 0 for e in self.ENG}
        self.dcnt = [0] * ndma
        self.dnext = 0
        self.lists = {e: [] for e in self.ENG}
        self.waited = {e: {} for e in self.ENG}

    def _deps(self, eng, reads, writes):
        need = {}
        for t in reads:
            if t.w is not None and need.get(t.w[0], 0) < t.w[1]:
                need[t.w[0]] = t.w[1]
        for t in writes:
            if t.w is not None and need.get(t.w[0], 0) < t.w[1]:
                need[t.w[0]] = t.w[1]
            for s, v in t.r.items():
                if need.get(s, 0) < v:
                    need[s] = v
        if eng == "pe":
            need.pop("pe", None)
        w = self.waited[eng]
        out = []
        for s, v in need.items():
            if w.get(s, 0) < v:
                w[s] = v
                out.append((s, v))
        return out

    def _mark(self, tok, reads, writes):
        s, v = tok
        for t in reads:
            if t.r.get(s, 0) < v:
                t.r[s] = v
        for t in writes:
            t.w = tok
            t.r = {}

    def op(self, eng, fn, reads=(), writes=()):
        waits = self._deps(eng, reads, writes)
        self.cnt[eng] += 1
        self.lists[eng].append((waits, fn, eng, 1))
        self._mark((eng, self.cnt[eng]), reads, writes)

    def dma(self, eng, fn, reads=(), writes=()):
        waits = self._deps(eng, reads, writes)
        k = self.dnext
        self.dnext = (k + 1) % self.ndma
        s = "d%d" % k
        prev = self.dcnt[k]
        if prev and self.waited[eng].get(s, 0) < prev:
            self.waited[eng][s] = prev
            waits.append((s, prev))
        self.dcnt[k] += 16
        self.lists[eng].append((waits, fn, s, 16))
        self._mark((s, self.dcnt[k]), reads, writes)

    def barrier(self):
        for e in self.ENG:
            waits = []
            w = self.waited[e]
            for f in self.ENG:
                if f != e and self.cnt[f] > w.get(f, 0):
                    w[f] = self.cnt[f]
                    waits.append((f, self.cnt[f]))
            if self.cnt[e] > w.get(e, 0):
                w[e] = self.cnt[e]
                waits.append((e, self.cnt[e]))
            for k in range(self.ndma):
                s = "d%d" % k
                if self.dcnt[k] > w.get(s, 0):
                    w[s] = self.dcnt[k]
                    waits.append((s, self.dcnt[k]))
            if waits:
                self.lists[e].append((waits, None, None, 0))

    def emit(self):
        nc = self.nc
        sem = self.sem
        lists = self.lists

        def run(e, lst):
            for waits, fn, s, inc in lst:
                for ws, wv in waits:
                    e.wait_ge(sem[ws], wv)
                if fn is not None:
                    fn(e).then_inc(sem[s], inc)

        with nc.Block() as block:
            @block.tensor
            def _(e):
                run(e, lists["pe"])

            @block.scalar
            def _(e):
                run(e, lists["act"])

            @block.vector
            def _(e):
                run(e, lists["dve"])

            @block.gpsimd
            def _(e):
                run(e, lists["pool"])

            @block.sync
            def _(e):
                run(e, lists["sp"])


class Ring:
    def __init__(self, aps):
        self.items = [(a, Trk()) for a in aps]
        self.i = 0

    def next(self):
        it = self.items[self.i]
        self.i = (self.i + 1) % len(self.items)
        return it


class Region:
    def __init__(self, nc, start, size, name):
        self.nc, self.start, self.size, self.name = nc, start, size, name
        self.cur = start
        self.n = 0

    def reset(self):
        self.cur = self.start

    def alloc(self, name, shape, dt):
        esz = 4 if dt in (F32, I32) else 2
        n = 1
        for s in shape[1:]:
            n *= s
        nbytes = (n * esz + 63) // 64 * 64
        off = self.cur
        self.cur += nbytes
        assert self.cur <= self.start + self.size, (self.name, name, self.cur - self.start, self.size)
        self.n += 1
        return self.nc.alloc_sbuf_tensor_at("%s_%s_%d" % (self.name, name, self.n), list(shape), dt, offset=off).ap()


def build(cfg):
    D, DFF, H = cfg["D"], cfg["DFF"], cfg["H"]
    WIN, NSEQ, NPG, NPOOL = cfg["WIN"], cfg["NSEQ"], cfg["NPG"], cfg["NPOOL"]
    KC, FC = D // 128, DFF // 128
    assert D == H * DH and KC * 2 == H
    TS = 512
    NT = WIN // TS
    NB = WIN // 128
    OWN_T0 = NT // 2
    NOWN = WIN // 2
    NTS = NSEQ * TQ
    CW = 512
    CWD = 256
    SCALE = DH ** -0.5
    HC = H * TQ
    G = max(1, min(NSEQ, 512 // HC))

    nc = bass.Bass("TRN2", target_bir_lowering=False)
    P = Prog(nc)

    def din(name, shape, dt=F32):
        return nc.dram_tensor(name, list(shape), dt, kind="ExternalInput").ap()

    def dout(name, shape, dt=F32):
        return nc.dram_tensor(name, list(shape), dt, kind="ExternalOutput").ap()

    def dscr(name, shape, dt=BF16):
        return nc.dram_tensor(name, list(shape), dt, kind="Internal").ap()

    xwin = din("xwin", [WIN, D])
    nullb = din("nullb", [128, NB])
    valid3 = din("valid3", [1, TS])
    x_s = din("x_s", [NTS, D])
    st_conv = din("st_conv", [NSEQ * 30, D])
    st_ffn = din("st_ffn", [2, NSEQ * 2, DFF])
    cache_k = din("cache_k", [NPOOL * PAGE, D])
    cache_v = din("cache_v", [NPOOL * PAGE, D])
    ptab = din("ptab", [1, NSEQ * NPG], I32)
    consts = din("consts", [128, 5 * 128])
    mask_new = din("mask_new", [NTS, NSEQ // G, G * HC])
    norm_mix = din("norm_mix", [2, D])
    norm_ffn = din("norm_ffn", [2, D])
    norm_final = din("norm_final", [1, D])
    conv_w_in = din("conv_w_in", [D, 2 * D])
    conv_b_in = din("conv_b_in", [1, 2 * D])
    conv_w_dw = din("conv_w_dw", [CONV_W, D])
    conv_b_dw = din("conv_b_dw", [1, D])
    conv_ln_g = din("conv_ln_g", [1, D])
    conv_ln_b = din("conv_ln_b", [1, D])
    conv_w_out = din("conv_w_out", [D, D])
    conv_b_out = din("conv_b_out", [1, D])
    attn_w_qkv = din("attn_w_qkv", [D, 3 * D])
    attn_w_o = din("attn_w_o", [D, D])
    attn_b = din("attn_b", [1, H])
    ffn_w_gate = din("ffn_w_gate", [2, D, DFF])
    ffn_w_up = din("ffn_w_up", [2, D, DFF])
    ffn_w_dw = din("ffn_w_dw", [2, FFN_W, DFF])
    ffn_b_dw = din("ffn_b_dw", [2, 1, DFF])
    ffn_w_down = din("ffn_w_down", [2, DFF, D])

    y_p = dout("y_p", [NOWN, D])
    k_p = dout("k_p", [NOWN, D])
    v_p = dout("v_p", [NOWN, D])
    sc_p = dout("sc_p", [30, D])
    sf_p = dout("sf_p", [2, 2, DFF])
    y_s = dout("y_s", [NTS, D])
    k_s = dout("k_s", [NTS, D])
    v_s = dout("v_s", [NTS, D])
    sc_s = dout("sc_s", [NSEQ, 30, D])
    sf_s = dout("sf_s", [2, NSEQ * 2, DFF])

    w_in_fm = dscr("w_in_fm", [2 * KC, 128, KC * 128])
    w_out_tm = dscr("w_out_tm", [D // CW, 128, KC * CW])
    w_gate_fm = [dscr("w_gate_fm%d" % l, [FC, 128, KC * 128]) for l in range(2)]
    w_up_fm = [dscr("w_up_fm%d" % l, [FC, 128, KC * 128]) for l in range(2)]
    w_down_tm = [dscr("w_down_tm%d" % l, [D // CWD, 128, FC * CWD]) for l in range(2)]
    w_qk_fm = dscr("w_qk_fm", [2 * KC, 128, KC * 128])
    w_kv_tm = dscr("w_kv_tm", [2 * D // CW, 128, KC * CW])
    w_o_tm = dscr("w_o_tm", [D // CW, 128, KC * CW])
    kt_scr = dscr("kt_scr", [KC, 128, WIN])
    v_scr = dscr("v_scr", [WIN, D])
    trk_kt = Trk()
    trk_v = Trk()
    trk_w = Trk()

    base = (nc.sbuf_base + 63) // 64 * 64
    top = nc.sbuf_top
    PR = Region(nc, base, top - base, "pers")
    sb = PR.alloc

    cst_f = sb("cst_f", [128, 5 * 128], F32)
    ident_f = cst_f[:, 0:128]
    ident_b = sb("ident_b", [128, 128], BF16)
    negtri = sb("negtri", [128, 128], BF16)
    negones = sb("negones", [128, 128], BF16)
    maskneg = sb("maskneg", [128, 128], BF16)
    posones = sb("posones", [128, 128], BF16)
    ones_row = sb("ones_row", [1, 512], BF16)
    zero_row = sb("zero_row", [1, 512], BF16)
    b_in_c = sb("b_in_c", [128, 2 * KC], F32)
    w_dw_c = sb("w_dw_c", [128, KC, CONV_W], F32)
    b_dw_c = sb("b_dw_c", [128, KC], F32)
    ln_g_c = sb("ln_g_c", [128, KC], F32)
    ln_b_c = sb("ln_b_c", [128, KC], F32)
    fw_dw_c = sb("fw_dw_c", [128, 2, FC, FFN_W], F32)
    fb_dw_c = sb("fb_dw_c", [128, 2, FC], F32)
    gain_c = sb("gain_c", [128, 4, KC], F32)
    b_out_r = sb("b_out_r", [1, D], BF16)
    b_out_f = sb("b_out_f", [1, D], F32)
    gfin = sb("gfin", [128, D], F32)
    nullb_c = sb("nullb_c", [128, NB], F32)
    bh_bc = sb("bh_bc", [128, H], F32)
    kbias = sb("kbias", [128, NB, H], F32)
    cvec = sb("cvec", [128, KC], F32)
    valid_c = sb("valid_c", [128, TS], F32)
    hist_g = [sb("hist_g%d" % l, [128, FC, 2], F32) for l in range(2)]
    hist_glu = sb("hist_glu", [128, KC, 30], F32)
    t_hglu = Trk()
    epsc = sb("epsc", [128, 1], F32)
    stat = sb("stat", [128, 16], F32)
    lnv = sb("lnv", [128, 16], F32)
    rstd = sb("rstd", [128, 16], F32)
    t_const = Trk()
    t_stat = Trk()
    t_statb = [Trk() for _ in range(8)]
    t_hist = [Trk(), Trk()]

    xt = sb("xt", [128, 4, D], F32)
    t_xt = [Trk() for _ in range(4)]
    xhalo = sb("xhalo", [128, D], F32)
    t_xhalo = Trk()
    hT = sb("hT", [128, KC, TS], BF16)
    t_hT = Trk()
    xn_ring_aps = [sb("xn%d" % i, [128, D], BF16) for i in range(2)]
    junk = sb("junk", [128, D], BF16)
    t_junk = Trk()
    xn_ring = Ring(xn_ring_aps)
    wfm_ring = Ring([sb("wfm%d" % i, [128, KC, 128], BF16) for i in range(5)])
    wtm_ring = Ring([sb("wtm%d" % i, [128, max(FC * CWD, KC * CW)], BF16) for i in range(2)])

    R1_SZ = 36 * 1024
    R2_SZ = FC * TS * 2
    R3_SZ = 13 * 1024
    r1s = PR.cur
    R1 = Region(nc, r1s, R1_SZ, "r1")
    R2 = Region(nc, r1s + R1_SZ, R2_SZ, "r2")
    R12 = Region(nc, r1s, R1_SZ + R2_SZ, "r12")
    R3 = Region(nc, r1s + R1_SZ + R2_SZ, R3_SZ, "r3")
    r4s = r1s + R1_SZ + R2_SZ + R3_SZ
    assert r4s < top, (r4s, top)
    R4 = Region(nc, r4s, top - r4s, "r4")

    EXTW = 30 + TS
    EW = 30 + TQ
    EXTA = max(EXTW, NSEQ * EW)
    ext_full = R1.alloc("ext", [128, KC, EXTA], F32)
    ext = ext_full[:, :, 0:EXTW]
    t_ext = [Trk() for _ in range(KC)]
    yacc = R1.alloc("yacc", [128, KC, TS], F32)
    t_yacc = [Trk() for _ in range(KC)]
    sc22 = R2.alloc("sc22", [128, FC, TS], BF16)
    t_sc22 = [Trk() for _ in range(FC)]
    mean_t = R3.alloc("mean_t", [128, TS], F32)
    rstd_t = R3.alloc("rstd_t", [128, TS], F32)
    tmpa_ring = Ring([R3.alloc("tmp_a%d" % i, [128, TS], F32) for i in range(4)])
    t_mean = Trk()
    t_rstd = Trk()
    R3.reset()
    extg_ring = Ring([R3.alloc("extg%d" % i, [128, 2 + TS], F32) for i in range(2)])
    gc_ring = Ring([R3.alloc("gc%d" % i, [128, TS], F32) for i in range(2)])
    sil_ring = Ring([R3.alloc("sil%d" % i, [128, TS], F32) for i in range(2)])
    R2.reset()
    QT = R2.alloc("QT", [128, KC, TS], BF16)
    t_QT = Trk()
    attT = R2.alloc("attT", [128, KC, TS], BF16)
    t_attT = [Trk() for _ in range(KC)]
    R1.reset()
    R3.reset()
    KTt = R3.alloc("KTt", [128, KC, TS], BF16)
    t_KTt = Trk()
    kvtm_ring = Ring([R1.alloc("kvtm%d" % i, [128, 2 * D], F32) for i in range(4)])
    vbf_ring = Ring([R3.alloc("vbf%d" % i, [128, D], BF16) for i in range(2)])
    R1.reset()
    Kp_ring = Ring([R1.alloc("Kp%d" % i, [128, WIN], BF16) for i in range(2)])
    Vp_ring = Ring([R1.alloc("Vp%d" % i, [128, NB, 128], BF16) for i in range(2)])
    NPEC = KC // 2
    dg_bytes = NPEC * CONV_W * 128 * 2
    eb_bytes = (NPEC * EXTW * 2 + 63) // 64 * 64
    RDG = Region(nc, (top - dg_bytes - eb_bytes) // 64 * 64, dg_bytes + eb_bytes, "rdg")
    DG = RDG.alloc("DG", [128, NPEC * CONV_W, 128], BF16)
    extb = RDG.alloc("extb", [128, NPEC, EXTW], BF16)
    t_extb = [Trk() for _ in range(NPEC)]
    R34 = Region(nc, R3.start, RDG.start - R3.start, "r34")
    e_ring = Ring([R34.alloc("e%d" % i, [128, 2, TS], F32) for i in range(2)])
    sp_ring = Ring([R34.alloc("sp%d" % i, [128, 2, TS], BF16) for i in range(3)])
    S_ring = Ring([R34.alloc("S%d" % i, [128, 2, TS], BF16) for i in range(2)])
    W_ring = Ring([R34.alloc("W%d" % i, [128, 2, TS], BF16) for i in range(3)])
    R1.reset()
    yout_ring = Ring([R1.alloc("yout%d" % i, [128, D], F32) for i in range(2)])
    R1.reset()
    R2.reset()
    stg_f = Ring([R1.alloc("stg_f%d" % i, [128, 3 * D], F32) for i in range(3)])
    stg_b = Ring([R2.alloc("stg_b%d" % i, [128, 3 * D], BF16) for i in range(3)])

    ps_all = nc.alloc_psum_tensor("ps_all", [128, 8 * 512], F32).ap()
    banks = [(ps_all[:, i * 512:(i + 1) * 512], Trk()) for i in range(8)]
    bank_i = {}
    pzpair_ring = Ring([ps_all[:, 2 * i * 512:(2 * i + 2) * 512].rearrange("p (b n) -> p b n", b=2) for i in range(3)])

    def nbank(lo=0, hi=8):
        i = bank_i.get((lo, hi), lo)
        bank_i[(lo, hi)] = i + 1 if i + 1 < hi else lo
        return banks[i]

    def mm(out, lhsT, rhs, start, stop, reads, writes):
        P.op("pe", lambda e: e.matmul(out, lhsT, rhs, start=start, stop=stop), reads, writes)

    def act(out, in_, func, reads, writes, bias=None, scale=None, accum=None):
        kw = {}
        if bias is not None:
            kw["bias"] = bias
        if scale is not None:
            kw["scale"] = scale
        if accum is not None:
            kw["accum_out"] = accum
        P.op("act", lambda e: e.activation(out, in_, func, **kw), reads, writes)

    def tcopy(eng, out, in_, reads, writes):
        if eng == "act":
            P.op(eng, lambda e: e.activation(out, in_, AF.Copy), reads, writes)
        else:
            P.op(eng, lambda e: e.tensor_copy(out, in_), reads, writes)

    def tt(eng, out, a, b, op, reads, writes):
        P.op(eng, lambda e: e.tensor_tensor(out, a, b, op), reads, writes)

    def ts(eng, out, a, s1, s2, op0, op1, reads, writes):
        if op1 is None:
            P.op(eng, lambda e: e.tensor_scalar(out, a, s1, None, op0), reads, writes)
        else:
            P.op(eng, lambda e: e.tensor_scalar(out, a, s1, s2, op0, op1), reads, writes)

    def stt(eng, out, a, s, b, op0, op1, reads, writes):
        P.op(eng, lambda e: e.scalar_tensor_tensor(out, a, s, b, op0, op1), reads, writes)

    def dma(eng, out, in_, reads, writes, **kw):
        P.dma(eng, lambda e: e.dma_start(out=out, in_=in_, **kw), reads, writes)

    def memset(eng, ap, val, writes):
        P.op(eng, lambda e: e.memset(ap, val), (), writes)

    dma("sp", cst_f, consts, (), (t_const,))
    tcopy("dve", ident_b, cst_f[:, 0:128], (t_const,), (t_const,))
    tcopy("dve", negtri, cst_f[:, 128:256], (t_const,), (t_const,))
    tcopy("dve", negones, cst_f[:, 256:384], (t_const,), (t_const,))
    tcopy("dve", maskneg, cst_f[:, 384:512], (t_const,), (t_const,))
    tcopy("dve", posones, cst_f[:, 512:640], (t_const,), (t_const,))
    memset("dve", ones_row, 1.0, (t_const,))
    memset("dve", zero_row, 0.0, (t_const,))
    memset("dve", epsc, EPS, (t_const,))

    def colload(dst, src_row):
        dma("sp", dst, src_row.rearrange("o (c p) -> p (o c)", p=128), (), (t_const,),
            allow_slow_non_contiguous=True)

    colload(b_in_c, conv_b_in)
    colload(b_dw_c, conv_b_dw)
    colload(ln_g_c, conv_ln_g)
    colload(ln_b_c, conv_ln_b)
    for j in range(CONV_W):
        colload(w_dw_c[:, :, j], conv_w_dw[j:j + 1, :])
    for l in range(2):
        colload(fb_dw_c[:, l, :], ffn_b_dw[l])
        for j in range(FFN_W):
            colload(fw_dw_c[:, l, :, j], ffn_w_dw[l, j:j + 1, :])
    colload(gain_c[:, 0, :], norm_mix[0:1, :])
    colload(gain_c[:, 1, :], norm_ffn[0:1, :])
    colload(gain_c[:, 2, :], norm_mix[1:2, :])
    colload(gain_c[:, 3, :], norm_ffn[1:2, :])
    dma("sp", b_out_f, conv_b_out, (), (t_const,))
    tcopy("dve", b_out_r, b_out_f, (t_const,), (t_const,))
    dma("sp", gfin, norm_final.partition_broadcast(128), (), (t_const,))
    dma("sp", nullb_c, nullb, (), (t_const,))
    dma("sp", bh_bc, attn_b.partition_broadcast(128), (), (t_const,))
    dma("sp", valid_c, valid3.partition_broadcast(128), (), (t_const,))
    bhp = bh_bc.rearrange("p (k t) -> p k t", t=2)
    tcopy("dve", cvec[0:64, :], bhp[0:64, :, 0], (t_const,), (t_const,))
    tcopy("dve", cvec[64:128, :], bhp[64:128, :, 1], (t_const,), (t_const,))
    act(cvec, cvec, AF.Exp, (t_const,), (t_const,))
    for kb in range(NB):
        ts("dve", kbias[:, kb, :], bh_bc, nullb_c[:, kb:kb + 1], None, ALU.add, None, (t_const,), (t_const,))
    for l in range(2):
        memset("dve", hist_g[l], 0.0, (t_hist[l],))
    for c in range(KC - NPEC, KC):
        for j in range(CONV_W):
            ts(("dve", "pool")[j % 2], DG[:, (c - (KC - NPEC)) * CONV_W + j, :], ident_f, w_dw_c[:, c, j:j + 1], None,
               ALU.mult, None, (t_const,), (t_const,))

    cast_rr = [0]

    def convert(src, kin, n, gain_idx, dests):
        kin_c = kin // 128
        for kc in range(kin_c):
            sf, tf = stg_f.next()
            sbf, tb = stg_b.next()
            dma("sp", sf[:, 0:n], src[kc * 128:(kc + 1) * 128, :], (), (tf,))
            if gain_idx is not None:
                eng = ("act", "act", "dve")[cast_rr[0] % 3]
            else:
                eng = ("dve", "act", "pool")[cast_rr[0] % 3]
            cast_rr[0] += 1
            if gain_idx is None:
                if eng == "act":
                    act(sbf[:, 0:n], sf[:, 0:n], AF.Copy, (tf,), (tb,))
                else:
                    tcopy(eng, sbf[:, 0:n], sf[:, 0:n], (tf,), (tb,))
            else:
                gsc = gain_c[:, gain_idx, kc:kc + 1]
                if eng == "act":
                    act(sbf[:, 0:n], sf[:, 0:n], AF.Copy, (tf, t_const), (tb,), scale=gsc)
                else:
                    ts(eng, sbf[:, 0:n], sf[:, 0:n], gsc, None, ALU.mult, None, (tf, t_const), (tb,))
            for kind, scr, c0, ncols in dests:
                w = 128 if kind == "fm" else kind
                dst = scr.rearrange("s p (k n) -> p s k n", k=kin_c)[:, :, kc, :]
                srcv = sbf[:, c0:c0 + ncols].rearrange("p (s n) -> p s n", n=w)
                dma("act", dst, srcv, (tb,), (trk_w,))

    convert(conv_w_in, D, 2 * D, 0, [("fm", w_in_fm, 0, 2 * D)])
    convert(conv_w_out, D, D, None, [(CW, w_out_tm, 0, D)])
    for l in range(2):
        convert(ffn_w_gate[l], D, DFF, 1 + 2 * l, [("fm", w_gate_fm[l], 0, DFF)])
        convert(ffn_w_up[l], D, DFF, 1 + 2 * l, [("fm", w_up_fm[l], 0, DFF)])
        convert(ffn_w_down[l], DFF, D, None, [(CWD, w_down_tm[l], 0, D)])
    convert(attn_w_qkv, D, 3 * D, 2, [("fm", w_qk_fm, 0, 2 * D), (CW, w_kv_tm, D, 2 * D)])
    convert(attn_w_o, D, D, None, [(CW, w_o_tm, 0, D)])
    P.barrier()

    CW_DEFAULT = CW
    def load_fm(scr, oc):
        w, tw = wfm_ring.next()
        dma("sp", w, scr[oc].rearrange("p (k n) -> p k n", k=KC), (trk_w,), (tw,))
        return w, tw

    def load_tm(scr, cs, kin_c, cw):
        w, tw = wtm_ring.next()
        wv = w[:, 0:kin_c * cw].rearrange("p (k n) -> p k n", k=kin_c)
        dma("sp", wv, scr[cs].rearrange("p (k n) -> p k n", k=kin_c), (trk_w,), (tw,))
        return wv, tw

    def row_stats(xa, tx, n, b):
        tsb = t_statb[b]
        act(junk[:n, :], xa, AF.Square, (tx,), (t_junk, tsb), accum=stat[:n, b:b + 1])
        act(lnv[:n, b:b + 1], stat[:n, b:b + 1], AF.Ln, (tsb, t_const), (tsb,), bias=epsc[:n, :], scale=1.0 / D)
        act(rstd[:n, b:b + 1], lnv[:n, b:b + 1], AF.Exp, (tsb,), (tsb,), scale=-0.5)

    def rms_to_hT(blocks):
        for b, (xa, tx, n) in enumerate(blocks):
            row_stats(xa, tx, n, b)
            xb, txn = xn_ring.next()
            ts("dve", xb[:n, :], xa, rstd[:n, b:b + 1], None, ALU.mult, None, (tx, t_statb[b]), (txn,))
            pb, tpb = nbank()
            pbv = pb.bitcast(BF16)
            for kc in range(KC):
                P.op("pe", lambda e, kc=kc, n=n, xb=xb, pbv=pbv: e.transpose(
                    pbv[:, kc * 128:kc * 128 + n], xb[:n, kc * 128:(kc + 1) * 128], ident_b[:n, :n]),
                    (txn, t_const), (tpb,))
            src = pbv[:, 0:KC * 128].rearrange("p (k n) -> p k n", k=KC)[:, :, 0:n]
            tcopy("dve", hT[:, :, b * 128:b * 128 + n], src, (tpb,), (t_hT,))

    def linear_fm(scr, oc_list, N, consume, rhs=None, trhs=None):
        rhs = hT if rhs is None else rhs
        trhs = t_hT if trhs is None else trhs
        for i, oc in enumerate(oc_list):
            w, tw = load_fm(scr, oc)
            pb, tpb = nbank()
            for kc in range(KC):
                mm(pb[:, 0:N], w[:, kc, :], rhs[:, kc, 0:N], kc == 0, kc == KC - 1, (tw, trhs), (tpb,))
            consume(i, oc, pb[:, 0:N], tpb)

    def linear_tm(scr, ncols, kin_c, lhs, tlhs_fn, blocks, consume, bias_row=None, cw=None):
        CW = cw or CW_DEFAULT
        for cs in range(ncols // CW):
            w, tw = load_tm(scr, cs, kin_c, CW)
            for b, n in blocks:
                pb, tpb = nbank()
                for kc in range(kin_c):
                    mm(pb[:n, 0:CW], lhs[:, kc, b * 128:b * 128 + n], w[:, kc, :], kc == 0,
                       kc == kin_c - 1 and bias_row is None, (tw, tlhs_fn(kc)), (tpb,))
                if bias_row is not None:
                    mm(pb[:n, 0:CW], ones_row[0:1, 0:n], bias_row[0:1, cs * CW:(cs + 1) * CW], False, True,
                       (t_const,), (tpb,))
                consume(cs * CW, CW, b, n, pb[:n, 0:CW], tpb)

    def resid_add(xblocks):
        def f(c0, cw, b, n, pb, tpb):
            xa, tx, _ = xblocks[b]
            tt("dve", xa[:, c0:c0 + cw], xa[:, c0:c0 + cw], pb, ALU.add, (tx, tpb), (tx,))
        return f

    def blist(xblocks):
        return [(b, n) for b, (_, _, n) in enumerate(xblocks)]

    def conv_module(xblocks, N, t_extv, glu_dst, tap_src, view, valid=None, use_pe=False):
        rms_to_hT(xblocks)
        for c in range(KC):
            res = {}

            def cons(i, oc, pb, tpb, res=res):
                res[i] = (pb, tpb)
            linear_fm(w_in_fm, [c, c + KC], N, cons)
            (pa, tpa), (pg, tpg) = res[0], res[1]
            sg, tsg = tmpa_ring.next()
            act(sg[:, 0:N], pg, AF.Sigmoid, (tpg, t_const), (tsg,), bias=b_in_c[:, KC + c:KC + c + 1])
            stt("dve", glu_dst(c), view(pa), b_in_c[:, c:c + 1], view(sg[:, 0:N]), ALU.add, ALU.mult,
                (tpa, tsg, t_const), (t_extv[c],))
            if valid is not None:
                tt("dve", glu_dst(c), glu_dst(c), view(valid[:, 0:N]), ALU.mult, (t_extv[c], t_const), (t_extv[c],))
        ndve = KC if not use_pe else KC - NPEC
        for c in range(ndve, KC):
            tcopy("act" if c % 2 == 0 else "pool", extb[:, c - ndve, :], ext[:, c, :], (t_extv[c],), (t_extb[c - ndve],))
        for j in range(CONV_W):
            for c in range(ndve):
                yv = view(yacc[:, c, 0:N])
                if j == 0:
                    ts("dve", yv, tap_src(c, 0), w_dw_c[:, c, 0:1], b_dw_c[:, c:c + 1], ALU.mult, ALU.add,
                       (t_extv[c], t_const), (t_yacc[c],))
                else:
                    stt("dve", yv, tap_src(c, j), w_dw_c[:, c, j:j + 1], yv, ALU.mult, ALU.add,
                        (t_extv[c], t_const, t_yacc[c]), (t_yacc[c],))
        for c in range(ndve, KC):
            py, tpy = nbank()
            for j in range(CONV_W):
                mm(py[:, 0:N], DG[:, (c - ndve) * CONV_W + j, :], extb[:, c - ndve, j:j + N], j == 0, j == CONV_W - 1,
                   (t_const, t_extb[c - ndve]), (tpy,))
            act(yacc[:, c, 0:N], py[:, 0:N], AF.Identity, (tpy, t_const), (t_yacc[c],), bias=b_dw_c[:, c:c + 1])
        for c in range(KC):
            tcopy("dve" if c % 2 == 0 else "pool", sc22[:, c, 0:N], yacc[:, c, 0:N], (t_yacc[c],), (t_sc22[c],))
            act(sc22[:, KC + c, 0:N], yacc[:, c, 0:N], AF.Square, (t_yacc[c],), (t_sc22[KC + c],))
        pm, tpm = nbank()
        pq, tpq = nbank()
        for c in range(KC):
            mm(pm[:, 0:N], posones, sc22[:, c, 0:N], c == 0, c == KC - 1, (t_const, t_sc22[c]), (tpm,))
        for c in range(KC):
            mm(pq[:, 0:N], posones, sc22[:, KC + c, 0:N], c == 0, c == KC - 1, (t_const, t_sc22[KC + c]), (tpq,))
        ts("dve", mean_t[:, 0:N], pm[:, 0:N], 1.0 / D, None, ALU.mult, None, (tpm,), (t_mean,))
        m2, tm2 = tmpa_ring.next()
        tt("dve", m2[:, 0:N], mean_t[:, 0:N], mean_t[:, 0:N], ALU.mult, (t_mean,), (tm2,))
        stt("dve", m2[:, 0:N], pq[:, 0:N], 1.0 / D, m2[:, 0:N], ALU.mult, ALU.subtract, (tpq, tm2), (tm2,))
        act(m2[:, 0:N], m2[:, 0:N], AF.Ln, (tm2, t_const), (tm2,), bias=epsc[:, :])
        act(rstd_t[:, 0:N], m2[:, 0:N], AF.Exp, (tm2,), (t_rstd,), scale=-0.5)
        for c in range(KC):
            d1, td1 = tmpa_ring.next()
            tt("dve", d1[:, 0:N], yacc[:, c, 0:N], mean_t[:, 0:N], ALU.subtract, (t_yacc[c], t_mean), (td1,))
            tt("dve", d1[:, 0:N], d1[:, 0:N], rstd_t[:, 0:N], ALU.mult, (td1, t_rstd), (td1,))
            act(hT[:, c, 0:N], d1[:, 0:N], AF.Silu, (td1, t_const), (t_hT,),
                bias=ln_b_c[:, c:c + 1], scale=ln_g_c[:, c:c + 1])
        linear_tm(w_out_tm, D, KC, hT, lambda kc: t_hT, blist(xblocks), resid_add(xblocks),
                  bias_row=b_out_r)

    def ffn(l, xblocks, N, hist_src, gate_only=False, valid=None, hist_out=None):
        rms_to_hT(xblocks)
        for c in range(FC):
            eg, teg = extg_ring.next()
            res = {}

            def consg(i, oc, pb, tpb, res=res):
                res["g"] = (pb, tpb)
            linear_fm(w_gate_fm[l], [c], N, consg)
            pg, tpg = res["g"]
            hist_src(c, eg, teg)
            if valid is not None:
                tt("dve", eg[:, 2:2 + N], pg, valid[:, 0:N], ALU.mult, (tpg, t_const), (teg,))
            else:
                act(eg[:, 2:2 + N], pg, AF.Copy, (tpg,), (teg,))
            if hist_out is not None:
                hist_out(c, eg, teg)
            if gate_only:
                continue

            def consu(i, oc, pb, tpb, res=res):
                res["u"] = (pb, tpb)
            linear_fm(w_up_fm[l], [c], N, consu)
            pu, tpu = res["u"]
            gc, tgc = gc_ring.next()
            for j in range(FFN_W):
                if j == 0:
                    ts("dve", gc[:, 0:N], eg[:, 0:N], fw_dw_c[:, l, c, 0:1], None, ALU.mult, None,
                       (teg, t_const), (tgc,))
                else:
                    stt("dve", gc[:, 0:N], eg[:, j:j + N], fw_dw_c[:, l, c, j:j + 1], gc[:, 0:N], ALU.mult, ALU.add,
                        (teg, t_const, tgc), (tgc,))
            sl, tsl = sil_ring.next()
            act(sl[:, 0:N], gc[:, 0:N], AF.Silu, (tgc, t_const), (tsl,), bias=fb_dw_c[:, l, c:c + 1])
            tt("dve", sc22[:, c, 0:N], sl[:, 0:N], pu, ALU.mult, (tsl, tpu), (t_sc22[c],))
        if gate_only:
            return
        linear_tm(w_down_tm[l], D, FC, sc22, lambda kc: t_sc22[kc], blist(xblocks),
                  resid_add(xblocks), cw=CWD)

    def final_out(xblocks, out_rows):
        for b, (xa, tx, n) in enumerate(xblocks):
            row_stats(xa, tx, n, b)
            yo, tyo = yout_ring.next()
            stt("dve", yo[:n, :], xa, rstd[:n, b:b + 1], gfin[:n, :], ALU.mult, ALU.mult, (tx, t_statb[b], t_const), (tyo,))
            dma("sp", out_rows(b, n), yo[:n, :], (tyo,), ())

    def attention(nq, nkb_full, ndiag):
        nkb = nkb_full + ndiag
        nkeys = nkb * 128
        for pr in range(KC):
            Kp, tKp = Kp_ring.next()
            Vp, tVp = Vp_ring.next()
            dma("sp", Kp[:, 0:nkeys], kt_scr[pr, :, 0:nkeys], (trk_kt,), (tKp,))
            dma("sp", Vp[:, 0:nkb, :], v_scr[0:nkeys, pr * 128:(pr + 1) * 128].rearrange("(k p) d -> p k d", p=128),
                (trk_v,), (tVp,))
            po, tpo = nbank(6, 8)
            state = dict(S=None, first=True)

            def stage_a(kb):
                jd = kb - nkb_full
                c0 = 128 * jd if jd >= 0 else 0
                pz2, tpz = pzpair_ring.next()
                e2, te = e_ring.next()
                for hh in range(2):
                    r0 = hh * 64
                    mm(pz2[:, hh, c0:nq], Kp[r0:r0 + 64, kb * 128:(kb + 1) * 128], QT[r0:r0 + 64, pr, c0:nq],
                       True, False, (tKp, t_QT), (tpz,))
                    if jd >= 0:
                        mm(pz2[:, hh, c0:c0 + 128], ident_b, maskneg, False, False, (t_const,), (tpz,))
                for hh in range(2):
                    act(e2[:, hh, c0:nq], pz2[:, hh, c0:nq], AF.Exp, (tpz, t_const), (te,),
                        bias=kbias[:, kb, 2 * pr + hh:2 * pr + hh + 1])
                sp2, tsp = sp_ring.next()
                act(sp2[:, :, c0:nq], e2[:, :, c0:nq], AF.Ln, (te,), (tsp,), bias=1.0)
                return (pz2, tpz, sp2, tsp, c0)

            def stage_b(kb, a):
                pz2, tpz, sp2, tsp, c0 = a
                S_prev = state["S"]
                for hh in range(2):
                    mm(pz2[:, hh, c0:nq], negtri, sp2[:, hh, c0:nq], False, S_prev is None, (t_const, tsp), (tpz,))
                    if S_prev is not None:
                        Sa, tSa, s_c0 = S_prev
                        mm(pz2[:, hh, s_c0:nq], negones, Sa[:, hh, s_c0:nq], False, True, (t_const, tSa), (tpz,))
                W2, tW = W_ring.next()
                act(W2[:, :, c0:nq], pz2[:, :, c0:nq], AF.Exp, (tpz, t_const), (tW,), bias=nullb_c[:, kb:kb + 1])
                if kb > 0:
                    Sn, tSn = S_ring.next()
                    if S_prev is None:
                        tcopy("pool", Sn[:, :, c0:nq], sp2[:, :, c0:nq], (tsp,), (tSn,))
                    else:
                        Sa, tSa, s_c0 = S_prev
                        if s_c0 > c0:
                            tcopy("pool", Sn[:, :, c0:s_c0], sp2[:, :, c0:s_c0], (tsp,), (tSn,))
                        tt("pool", Sn[:, :, s_c0:nq], Sa[:, :, s_c0:nq], sp2[:, :, s_c0:nq], ALU.add, (tSa, tsp), (tSn,))
                    state["S"] = (Sn, tSn, c0)
                return (W2, tW, c0)

            def stage_c(kb, wv):
                W2, tW, c0 = wv
                for hh in range(2):
                    r0 = hh * 64
                    if state["first"] and c0 > 0:
                        mm(po[r0:r0 + 64, 0:nq], zero_row[0:1, 0:64], zero_row[0:1, 0:nq], True, False,
                           (t_const,), (tpo,))
                    mm(po[r0:r0 + 64, c0:nq], Vp[:, kb, r0:r0 + 64], W2[:, hh, c0:nq],
                       state["first"] and c0 == 0, kb == 0, (tVp, tW), (tpo,))
                state["first"] = False

            kbs = list(range(nkb - 1, -1, -1))
            pa, pw = {}, {}
            ns = len(kbs)
            for i in range(ns + 2):
                if i < ns:
                    pa[i] = stage_a(kbs[i])
                j = i - 1
                if 0 <= j < ns:
                    pw[j] = stage_b(kbs[j], pa.pop(j))
                k = j - 1
                if 0 <= k < ns:
                    stage_c(kbs[k], pw.pop(k))
            ts("dve", attT[:, pr, 0:nq], po[:, 0:nq], cvec[:, pr:pr + 1], None, ALU.mult, None,
               (tpo, t_const), (t_attT[pr],))

    def layer1(xblocks, N, q0, nkb_full, ndiag, own_tile, hist_src, hist_out, halo=False):
        rms_to_hT(xblocks)

        def consq(i, oc, pb, tpb):
            if oc < KC:
                act(QT[:, oc, 0:N], pb, AF.Copy, (tpb,), (t_QT,), scale=SCALE)
            else:
                tcopy("dve", KTt[:, oc - KC, 0:N], pb, (tpb,), (t_KTt,))
        if halo:
            linear_fm(w_qk_fm, list(range(KC)), N, consq)
        else:
            ocs = list(range(2 * KC)) if own_tile is not None else list(range(KC, 2 * KC))
            linear_fm(w_qk_fm, ocs, N, consq)
            dma("sp", kt_scr[:, :, q0:q0 + N].rearrange("c p n -> p c n"), KTt[:, :, 0:N], (t_KTt,), (trk_kt,))
            kvb = {b: kvtm_ring.next() for b, n in blist(xblocks)}

            def conskv(c0, cw, b_, n_, pb, tpb):
                kv, tkv = kvb[b_]
                tcopy("dve" if (c0 // cw) % 2 == 0 else "act", kv[:n_, c0:c0 + cw], pb, (tpb,), (tkv,))
            linear_tm(w_kv_tm, 2 * D, KC, hT, lambda kc: t_hT, blist(xblocks), conskv)
            for b, n in blist(xblocks):
                kv, tkv = kvb[b]
                vb, tvb = vbf_ring.next()
                tcopy("pool", vb[:n, :], kv[:n, D:2 * D], (tkv,), (tvb,))
                dma("sp", v_scr[q0 + b * 128:q0 + b * 128 + n, :], vb[:n, :], (tvb,), (trk_v,))
                if own_tile is not None:
                    r = own_tile * TS + b * 128
                    dma("sp", k_p[r:r + n, :], kv[:n, 0:D], (tkv,), ())
                    dma("sp", v_p[r:r + n, :], kv[:n, D:2 * D], (tkv,), ())
        if own_tile is None and not halo:
            return
        P.barrier()
        attention(N, nkb_full, ndiag)
        P.barrier()
        linear_tm(w_o_tm, D, KC, attT, lambda kc: t_attT[kc], blist(xblocks), resid_add(xblocks))
        if halo:
            ffn(1, xblocks, N, hist_src, gate_only=True, valid=valid_c[:, TS - 128:TS], hist_out=hist_out)
            return
        ffn(1, xblocks, N, hist_src, hist_out=hist_out)

    memset("pool", hist_glu, 0.0, (t_hglu,))

    def hist_from(l):
        def f(c, eg, teg):
            tcopy("pool", eg[:, 0:2], hist_g[l][:, c, :], (t_hist[l],), (teg,))
        return f

    def hist_to(l, N, export):
        def f(c, eg, teg):
            tcopy("pool", hist_g[l][:, c, :], eg[:, N:N + 2], (teg,), (t_hist[l],))
            if export and c == FC - 1:
                for tt_ in range(2):
                    dma("pool", sf_p[l, tt_:tt_ + 1, :].rearrange("o (c p) -> p (o c)", p=128), hist_g[l][:, :, tt_],
                        (t_hist[l],), (), allow_slow_non_contiguous=True)
        return f

    for t in range(NT):
        own = t - OWN_T0 if t >= OWN_T0 else None
        last = t == NT - 1
        xblocks = [(xt[:, b, :], t_xt[b], 128) for b in range(4)]
        for b in range(4):
            dma("sp", xt[:, b, :], xwin[t * TS + b * 128:t * TS + (b + 1) * 128, :], (), (t_xt[b],))
        vmask = valid_c if t == OWN_T0 - 1 else None
        for c in range(KC):
            tcopy("pool", ext[:, c, 0:30], hist_glu[:, c, :], (t_hglu,), (t_ext[c],))
        conv_module(xblocks, TS, t_ext, lambda c: ext[:, c, 30:30 + TS], lambda c, j: ext[:, c, j:j + TS],
                    lambda a: a, valid=vmask, use_pe=True)
        if last:
            P.barrier()
            pb0, tp0 = nbank()
            pb1, tp1 = nbank()
            hk = KC // 2
            for c in range(KC):
                pbx, tpx = (pb0, tp0) if c < hk else (pb1, tp1)
                cc = c % hk
                P.op("pe", lambda e, c=c, pbx=pbx, cc=cc: e.transpose(
                    pbx[0:30, cc * 128:(cc + 1) * 128], ext[:, c, TS:TS + 30], ident_f),
                    (t_ext[c], t_const), (tpx,))
            tcopy("dve", xhalo[0:30, 0:hk * 128], pb0[0:30, 0:hk * 128], (tp0,), (t_xhalo,))
            tcopy("dve", xhalo[0:30, hk * 128:2 * hk * 128], pb1[0:30, 0:hk * 128], (tp1,), (t_xhalo,))
            dma("sp", sc_p[:, :], xhalo[0:30, :], (t_xhalo,), ())
        for c in range(KC):
            tcopy("pool", hist_glu[:, c, :], ext[:, c, TS:TS + 30], (t_ext[c],), (t_hglu,))
        P.barrier()
        ffn(0, xblocks, TS, hist_from(0), valid=vmask, hist_out=hist_to(0, TS, last))
        P.barrier()
        if t == OWN_T0 - 1:
            tcopy("pool", xhalo, xt[:, 3, :], (t_xt[3],), (t_xhalo,))
        if own is None:
            layer1(xblocks, TS, t * TS, 0, 0, None, None, None)
            P.barrier()
            if t == OWN_T0 - 1:
                hb = [(xhalo, t_xhalo, 128)]
                layer1(hb, 128, OWN_T0 * TS - 128, OWN_T0 * 4 - 1, 1, None, hist_from(1), hist_to(1, 128, False),
                       halo=True)
                P.barrier()
        else:
            layer1(xblocks, TS, t * TS, t * 4, 4, own, hist_from(1), hist_to(1, TS, last))
            P.barrier()
            final_out(xblocks, lambda b, n: y_p[own * TS + b * 128:own * TS + b * 128 + n, :])
            P.barrier()

    N = NTS
    xs = xt[0:N, 0, :]
    t_xs = t_xt[0]
    xsb = [(xs, t_xs, N)]
    stage = xt[:, 1, :]
    t_stage = t_xt[1]
    glu_tm = xt[:, 2, :]
    t_glutm = t_xt[2]
    kvs = xt[:, 2:4, :].rearrange("p a d -> p (a d)")
    t_kvs = t_xt[3]
    dma("sp", xs, x_s, (), (t_xs,))
    ext_s = ext_full[:, :, 0:NSEQ * EW].rearrange("p k (s j) -> p k s j", j=EW)
    t_exts = [Trk() for _ in range(KC)]
    gl_c = R4.alloc("gl_c", [128, KC, N], F32)
    t_glc = Trk()
    SPB = 4
    for s0 in range(0, NSEQ, SPB):
        ns = min(SPB, NSEQ - s0)
        rows = ns * 30
        dma("sp", stage[0:rows, :], st_conv[s0 * 30:s0 * 30 + rows, :], (), (t_stage,))
        for c in range(KC):
            pb, tpb = nbank()
            P.op("pe", lambda e, c=c, pb=pb, rows=rows: e.transpose(
                pb[:, 0:rows], stage[0:rows, c * 128:(c + 1) * 128], ident_f[0:rows, 0:rows]),
                (t_stage, t_const), (tpb,))
            tcopy("dve", ext_s[:, c, s0:s0 + ns, 0:30], pb[:, 0:rows].rearrange("p (s j) -> p s j", j=30),
                  (tpb,), (t_exts[c],))
        for si in range(ns):
            dma("sp", sc_s[s0 + si, 0:26, :], stage[si * 30 + 4:si * 30 + 30, :], (t_stage,), ())

    def view_s(a):
        return a.rearrange("p (s i) -> p s i", i=TQ)

    conv_module(xsb, N, t_exts, lambda c: ext_s[:, c, :, 30:30 + TQ], lambda c, j: ext_s[:, c, :, j:j + TQ], view_s)
    for c in range(KC):
        tcopy("pool", view_s(gl_c[:, c, :]), ext_s[:, c, :, 30:30 + TQ], (t_exts[c],), (t_glc,))
    pb0, tp0 = nbank()
    pb1, tp1 = nbank()
    hk = KC // 2
    for c in range(KC):
        pbx, tpx = (pb0, tp0) if c < hk else (pb1, tp1)
        cc = c % hk
        P.op("pe", lambda e, c=c, pbx=pbx, cc=cc: e.transpose(
            pbx[0:N, cc * 128:(cc + 1) * 128], gl_c[:, c, :], ident_f), (t_glc, t_const), (tpx,))
    tcopy("dve", glu_tm[0:N, 0:hk * 128], pb0[0:N, 0:hk * 128], (tp0,), (t_glutm,))
    tcopy("dve", glu_tm[0:N, hk * 128:2 * hk * 128], pb1[0:N, 0:hk * 128], (tp1,), (t_glutm,))
    for s in range(NSEQ):
        dma("sp", sc_s[s, 26:30, :], glu_tm[s * TQ:(s + 1) * TQ, :], (t_glutm,), ())
    P.barrier()

    hs = [R4.alloc("hs%d" % l, [128, FC, NSEQ, 2], F32) for l in range(2)]
    t_hs = [Trk(), Trk()]
    gnew = [R4.alloc("gnew%d" % l, [128, FC, NSEQ * 2], F32) for l in range(2)]
    t_gnew = [Trk(), Trk()]
    R1.reset()
    stg2 = R1.alloc("stg2", [128, DFF], F32)
    t_stg2 = Trk()
    RS = NSEQ * 2
    for l in range(2):
        dma("sp", stg2[0:RS, :], st_ffn[l], (), (t_stg2,))
        for c in range(FC):
            pb, tpb = nbank()
            P.op("pe", lambda e, c=c, pb=pb: e.transpose(
                pb[:, 0:RS], stg2[0:RS, c * 128:(c + 1) * 128], ident_f[0:RS, 0:RS]), (t_stg2, t_const), (tpb,))
            tcopy("dve", hs[l][:, c, :, :], pb[:, 0:RS].rearrange("p (s j) -> p s j", j=2), (tpb,), (t_hs[l],))
    P.barrier()

    def ffn_s(l):
        rms_to_hT(xsb)
        for c in range(FC):
            eg, teg = extg_ring.next()
            egv = eg[:, 0:NSEQ * 6].rearrange("p (s j) -> p s j", j=6)
            res = {}

            def consg(i, oc, pb, tpb, res=res):
                res["g"] = (pb, tpb)
            linear_fm(w_gate_fm[l], [c], N, consg)
            pg, tpg = res["g"]
            tcopy("pool", egv[:, :, 0:2], hs[l][:, c, :, :], (t_hs[l],), (teg,))
            tcopy("dve", egv[:, :, 2:2 + TQ], view_s(pg), (tpg,), (teg,))
            tcopy("pool", gnew[l][:, c, :].rearrange("p (s j) -> p s j", j=2), egv[:, :, 4:6], (teg,), (t_gnew[l],))

            def consu(i, oc, pb, tpb, res=res):
                res["u"] = (pb, tpb)
            linear_fm(w_up_fm[l], [c], N, consu)
            pu, tpu = res["u"]
            gc, tgc = gc_ring.next()
            dst = view_s(gc[:, 0:N])
            for j in range(FFN_W):
                src = egv[:, :, j:j + TQ]
                if j == 0:
                    ts("dve", dst, src, fw_dw_c[:, l, c, 0:1], None, ALU.mult, None, (teg, t_const), (tgc,))
                else:
                    stt("dve", dst, src, fw_dw_c[:, l, c, j:j + 1], dst, ALU.mult, ALU.add,
                        (teg, t_const, tgc), (tgc,))
            sl, tsl = sil_ring.next()
            act(sl[:, 0:N], gc[:, 0:N], AF.Silu, (tgc, t_const), (tsl,), bias=fb_dw_c[:, l, c:c + 1])
            tt("dve", sc22[:, c, 0:N], sl[:, 0:N], pu, ALU.mult, (tsl, tpu), (t_sc22[c],))
        linear_tm(w_down_tm[l], D, FC, sc22, lambda kc: t_sc22[kc], [(0, N)], resid_add(xsb), cw=CWD)
        for c0 in range(0, FC, 4):
            pb, tpb = nbank()
            nn = min(4, FC - c0)
            for cc in range(nn):
                P.op("pe", lambda e, c=c0 + cc, cc=cc, pb=pb: e.transpose(
                    pb[0:RS, cc * 128:(cc + 1) * 128], gnew[l][:, c, :], ident_f), (t_gnew[l], t_const), (tpb,))
            tcopy("dve", stg2[0:RS, c0 * 128:(c0 + nn) * 128], pb[0:RS, 0:nn * 128], (tpb,), (t_stg2,))
        dma("sp", sf_s[l], stg2[0:RS, :], (t_stg2,), ())

    ffn_s(0)
    P.barrier()

    rms_to_hT(xsb)
    QTs = R4.alloc("QTs", [128, KC, N], BF16)
    KTn = R4.alloc("KTn", [128, KC, N], BF16)
    t_QTs, t_KTn = Trk(), Trk()

    def consq_s(i, oc, pb, tpb):
        if oc < KC:
            act(QTs[:, oc, :], pb, AF.Copy, (tpb,), (t_QTs,), scale=SCALE)
        else:
            tcopy("dve", KTn[:, oc - KC, :], pb, (tpb,), (t_KTn,))
    linear_fm(w_qk_fm, list(range(2 * KC)), N, consq_s)

    def conskv_s(c0, cw, b, n, pb, tpb):
        tcopy("dve", kvs[:n, c0:c0 + cw], pb, (tpb,), (t_kvs,))
    linear_tm(w_kv_tm, 2 * D, KC, hT, lambda kc: t_hT, [(0, N)], conskv_s)
    dma("sp", k_s[:, :], kvs[0:N, 0:D], (t_kvs,), ())
    dma("sp", v_s[:, :], kvs[0:N, D:2 * D], (t_kvs,), ())
    Vn = R4.alloc("Vn", [128, D], BF16)
    t_Vn = Trk()
    tcopy("pool", Vn[0:N, :], kvs[0:N, D:2 * D], (t_kvs,), (t_Vn,))
    Qblk = R4.alloc("Qblk", [128, NSEQ, KC, 2 * TQ], BF16)
    t_Qblk = Trk()
    memset("pool", Qblk, 0.0, (t_Qblk,))
    for pr in range(KC):
        src = QTs[:, pr, :].rearrange("p (s i) -> p s i", i=TQ)
        tcopy("pool", Qblk[0:64, :, pr, 0:TQ], src[0:64], (t_QTs, t_Qblk), (t_Qblk,))
        tcopy("pool", Qblk[64:128, :, pr, TQ:2 * TQ], src[64:128], (t_QTs, t_Qblk), (t_Qblk,))
    NG = NSEQ // G
    GC = G * HC
    brow_f = R4.alloc("brow_f", [1, GC], F32)
    brow = R4.alloc("brow", [1, GC], BF16)
    mnew_f = R4.alloc("mnew_f", [N, NG, GC], F32)
    mnew = R4.alloc("mnew", [N, NG, GC], BF16)
    t_brow = Trk()
    for i in range(TQ):
        dst = brow_f[0:1, :].rearrange("o (s h i) -> o s h i", h=H, i=TQ)[:, :, :, i]
        for s in range(G):
            tcopy("dve", dst[:, s, :], bh_bc[0:1, :], (t_const,), (t_brow,))
    tcopy("dve", brow, brow_f, (t_brow,), (t_brow,))
    dma("sp", mnew_f, mask_new, (), (t_brow,))
    tcopy("dve", mnew, mnew_f, (t_brow,), (t_brow,))
    NI = NSEQ * NPG
    pt_b = R4.alloc("pt_b", [128, NI], I32)
    iot = R4.alloc("iot", [128, NI], I32)
    c128 = R4.alloc("c128", [128, NI], I32)
    idx = R4.alloc("idx", [128, NI], I32)
    t_idx = Trk()
    dma("sp", pt_b, ptab.partition_broadcast(128), (), (t_idx,))
    P.op("pool", lambda e: e.iota(iot, [[0, NI]], base=0, channel_multiplier=1), (), (t_idx,))
    P.op("pool", lambda e: e.iota(c128, [[0, NI]], base=PAGE, channel_multiplier=0), (), (t_idx,))
    tt("pool", idx, pt_b, c128, ALU.mult, (t_idx,), (t_idx,))
    tt("pool", idx, idx, iot, ALU.add, (t_idx,), (t_idx,))
    e_s = Ring([R4.alloc("e_s%d" % i, [128, 512], F32) for i in range(2)])
    sp_s = Ring([R4.alloc("sp_s%d" % i, [128, 512], BF16) for i in range(2)])
    S_s = Ring([R4.alloc("S_s%d" % i, [128, 512], BF16) for i in range(2)])
    W_s = Ring([R4.alloc("W_s%d" % i, [128, 512], BF16) for i in range(2)])
    attTs = R4.alloc("attTs", [128, KC, N], BF16)
    t_attTs = Trk()
    P.barrier()
    R123 = Region(nc, R1.start, R1_SZ + R2_SZ + R3_SZ, "r123")
    kpg_ring = Ring([R123.alloc("kpg%d" % i, [128, D], BF16) for i in range(6)])
    ktp_ring = Ring([R123.alloc("ktp%d" % i, [128, KC, 128], BF16) for i in range(G + 3)])
    vpg_ring = Ring([R123.alloc("vpg%d" % i, [128, D], BF16) for i in range(2 * G)])

    for gi in range(NG):
        s0 = gi * G
        ng = G
        NCOL = ng * HC
        po, tpo = nbank(6, 8)
        mm(po[:, 0:ng * KC * 2 * TQ], zero_row[0:1, 0:128], zero_row[0:1, 0:ng * KC * 2 * TQ], True, False,
           (t_const,), (tpo,))
        S_prev = None
        for blk in range(NPG, -1, -1):
            newb = blk == NPG
            nk = N if newb else 128
            kts, vps = [], []
            if not newb:
                for si in range(ng):
                    s = s0 + si
                    col = s * NPG + blk
                    kp_, tkp = kpg_ring.next()
                    P.dma("pool", lambda e, kp_=kp_, col=col: e.indirect_dma_start(
                        out=kp_, out_offset=None, in_=cache_k,
                        in_offset=bass.IndirectOffsetOnAxis(ap=idx[:, col:col + 1], axis=0)), (t_idx,), (tkp,))
                    vp_, tvp = vpg_ring.next()
                    P.dma("pool", lambda e, vp_=vp_, col=col: e.indirect_dma_start(
                        out=vp_, out_offset=None, in_=cache_v,
                        in_offset=bass.IndirectOffsetOnAxis(ap=idx[:, col:col + 1], axis=0)), (t_idx,), (tvp,))
                    pbT, tpT = nbank(4, 6)
                    pbv = pbT.bitcast(BF16)
                    for c in range(KC):
                        P.op("pe", lambda e, c=c, pbv=pbv, kp_=kp_: e.transpose(
                            pbv[:, c * 128:(c + 1) * 128], kp_[:, c * 128:(c + 1) * 128], ident_b),
                            (tkp, t_const), (tpT,))
                    kt_, tkt = ktp_ring.next()
                    tcopy("dve", kt_, pbv[:, 0:KC * 128].rearrange("p (k n) -> p k n", k=KC), (tpT,), (tkt,))
                    kts.append((kt_, tkt))
                    vps.append((vp_, tvp))
            pz, tpz = nbank(0, 4)
            mm(pz[0:nk, 0:NCOL], ones_row[0:1, 0:nk], brow[0:1, 0:NCOL], True, False, (t_const, t_brow), (tpz,))
            for si in range(ng):
                s = s0 + si
                for pr in range(KC):
                    cols = slice(si * HC + pr * 2 * TQ, si * HC + (pr + 1) * 2 * TQ)
                    if newb:
                        lhsT = KTn[:, pr, 0:N]
                        rd = (t_KTn, t_Qblk)
                    else:
                        lhsT = kts[si][0][:, pr, :]
                        rd = (kts[si][1], t_Qblk)
                    mm(pz[0:nk, cols], lhsT, Qblk[:, s, pr, :], False, False, rd, (tpz,))
            if newb:
                mm(pz[0:nk, 0:NCOL], ident_b[0:N, 0:N], mnew[0:N, gi, 0:NCOL], False, False, (t_const, t_brow), (tpz,))
            e_t, te = e_s.next()
            act(e_t[0:nk, 0:NCOL], pz[0:nk, 0:NCOL], AF.Exp, (tpz,), (te,))
            sp_t, tsp = sp_s.next()
            act(sp_t[0:nk, 0:NCOL], e_t[0:nk, 0:NCOL], AF.Ln, (te,), (tsp,), bias=1.0)
            mm(pz[0:nk, 0:NCOL], negtri[0:nk, 0:nk], sp_t[0:nk, 0:NCOL], False, S_prev is None, (t_const, tsp), (tpz,))
            if S_prev is not None:
                Sa, tSa = S_prev
                mm(pz[0:nk, 0:NCOL], negones[:, 0:nk], Sa[:, 0:NCOL], False, True, (t_const, tSa), (tpz,))
            W_t, tW = W_s.next()
            act(W_t[0:nk, 0:NCOL], pz[0:nk, 0:NCOL], AF.Exp, (tpz,), (tW,))
            if blk > 0:
                Sn, tSn = S_s.next()
                if S_prev is None:
                    memset("pool", Sn[:, 0:NCOL], 0.0, (tSn,))
                    tcopy("pool", Sn[0:nk, 0:NCOL], sp_t[0:nk, 0:NCOL], (tsp, tSn), (tSn,))
                else:
                    Sa, tSa = S_prev
                    tt("pool", Sn[:, 0:NCOL], Sa[:, 0:NCOL], sp_t[:, 0:NCOL], ALU.add, (tSa, tsp), (tSn,))
                S_prev = (Sn, tSn)
            for si in range(ng):
                s = s0 + si
                for pr in range(KC):
                    wc = slice(si * HC + pr * 2 * TQ, si * HC + (pr + 1) * 2 * TQ)
                    oc_ = slice((si * KC + pr) * 2 * TQ, (si * KC + pr + 1) * 2 * TQ)
                    if newb:
                        lhsT = Vn[0:N, pr * 128:(pr + 1) * 128]
                        rd = (t_Vn, tW)
                    else:
                        lhsT = vps[si][0][:, pr * 128:(pr + 1) * 128]
                        rd = (vps[si][1], tW)
                    mm(po[:, oc_], lhsT, W_t[0:nk, wc], False, blk == 0 and si == ng - 1 and pr == KC - 1, rd, (tpo,))
        for si in range(ng):
            s = s0 + si
            pov = po[:, si * KC * 2 * TQ:(si + 1) * KC * 2 * TQ].rearrange("p (k j) -> p k j", j=2 * TQ)
            tcopy("dve", attTs[0:64, :, s * TQ:(s + 1) * TQ], pov[0:64, :, 0:TQ], (tpo,), (t_attTs,))
            tcopy("dve", attTs[64:128, :, s * TQ:(s + 1) * TQ], pov[64:128, :, TQ:2 * TQ], (tpo,), (t_attTs,))
    P.barrier()
    linear_tm(w_o_tm, D, KC, attTs, lambda kc: t_attTs, [(0, N)], resid_add(xsb))
    ffn_s(1)
    P.barrier()
    final_out(xsb, lambda b, n: y_s[0:n, :])

    P.barrier()
    P.emit()
    return nc


_CACHE = {}


def _consts():
    c = np.zeros((128, 5 * 128), np.float32)
    k = np.arange(128)[:, None]
    q = np.arange(128)[None, :]
    c[:, 0:128] = np.eye(128, dtype=np.float32)
    c[:, 128:256] = -(k >= q).astype(np.float32)
    c[:, 256:384] = -1.0
    c[:, 384:512] = np.where(k >= q, NEG, 0.0)
    c[:, 512:640] = 1.0
    return c


def _mask_new(nseq, h):
    hc = h * TQ
    g = max(1, min(nseq, 512 // hc))
    ng = nseq // g
    m = np.full((nseq * TQ, ng, g * hc), NEG, np.float32)
    for s in range(nseq):
        gi, sl = s // g, s % g
        for j in range(TQ):
            for i in range(TQ):
                if j < i:
                    m[s * TQ + j, gi, sl * hc + i:(sl + 1) * hc:TQ] = 0.0
    return m


def kernel(x_prompt, x_sample, state_conv, state_ffn, cache_k, cache_v, page_table,
           norm_mix, norm_ffn, norm_final, conv_w_in, conv_b_in, conv_w_dw, conv_b_dw, conv_ln_g,
           conv_ln_b, conv_w_out, conv_b_out, attn_w_qkv, attn_w_o, attn_b_logit, ffn_w_gate, ffn_w_up,
           ffn_w_dw, ffn_b_dw, ffn_w_down):
    f32 = np.float32
    x_prompt = np.asarray(x_prompt, f32)
    B, SEQ, D = x_prompt.shape
    x_sample = np.asarray(x_sample, f32)
    DB = x_sample.shape[0]
    n_cores = 8
    assert B * 2 == n_cores and DB % n_cores == 0
    NSEQ = DB // n_cores
    page_table = np.asarray(page_table, np.int32)
    NPG = page_table.shape[1]
    cache_k = np.asarray(cache_k, f32)
    cache_v = np.asarray(cache_v, f32)
    NPOOL = cache_k.shape[1]
    H = np.asarray(attn_b_logit).shape[1]
    DFF = np.asarray(ffn_w_gate).shape[2]
    cfg = dict(D=D, DFF=DFF, H=H, WIN=SEQ, NSEQ=NSEQ, NPG=NPG, NPOOL=NPOOL)
    key = tuple(sorted(cfg.items()))
    if key not in _CACHE:
        _CACHE[key] = build(cfg)
    nc = _CACHE[key]
    NB = SEQ // 128
    half = SEQ // 2
    state_conv = np.asarray(state_conv, f32)
    state_ffn = np.asarray(state_ffn, f32)
    ck = np.ascontiguousarray(cache_k[0].reshape(NPOOL * PAGE, D))
    cv = np.ascontiguousarray(cache_v[0].reshape(NPOOL * PAGE, D))
    shared = dict(
        cache_k=ck, cache_v=cv, consts=_consts(), mask_new=_mask_new(NSEQ, H),
        norm_mix=np.asarray(norm_mix, f32), norm_ffn=np.asarray(norm_ffn, f32),
        norm_final=np.asarray(norm_final, f32).reshape(1, D),
        conv_w_in=np.asarray(conv_w_in, f32)[0], conv_b_in=np.asarray(conv_b_in, f32),
        conv_w_dw=np.asarray(conv_w_dw, f32)[0], conv_b_dw=np.asarray(conv_b_dw, f32),
        conv_ln_g=np.asarray(conv_ln_g, f32), conv_ln_b=np.asarray(conv_ln_b, f32),
        conv_w_out=np.asarray(conv_w_out, f32)[0], conv_b_out=np.asarray(conv_b_out, f32),
        attn_w_qkv=np.asarray(attn_w_qkv, f32)[0], attn_w_o=np.asarray(attn_w_o, f32)[0],
        attn_b=np.asarray(attn_b_logit, f32),
        ffn_w_gate=np.asarray(ffn_w_gate, f32), ffn_w_up=np.asarray(ffn_w_up, f32),
        ffn_w_dw=np.asarray(ffn_w_dw, f32), ffn_b_dw=np.asarray(ffn_b_dw, f32).reshape(2, 1, DFF),
        ffn_w_down=np.asarray(ffn_w_down, f32),
    )
    in_maps = []
    for c in range(n_cores):
        b, r = c // 2, c % 2
        if r == 1:
            xwin = x_prompt[b]
            nullb = np.zeros((128, NB), f32)
            valid = np.ones((1, 512), f32)
        else:
            xwin = np.concatenate([np.zeros((half, D), f32), x_prompt[b, :half]], axis=0)
            nullb = np.zeros((128, NB), f32)
            nullb[:, :NB // 2] = NEG
            valid = np.zeros((1, 512), f32)
        sl = slice(c * NSEQ, (c + 1) * NSEQ)
        m = dict(shared)
        m.update(
            xwin=np.ascontiguousarray(xwin), nullb=nullb, valid3=valid,
            x_s=np.ascontiguousarray(x_sample[sl].reshape(NSEQ * TQ, D)),
            st_conv=np.ascontiguousarray(state_conv[0, sl].reshape(NSEQ * 30, D)),
            st_ffn=np.ascontiguousarray(state_ffn[:, sl].reshape(2, NSEQ * 2, DFF)),
            ptab=np.ascontiguousarray(page_table[sl].reshape(1, NSEQ * NPG)),
        )
        in_maps.append(m)
    res = run_bass_kernel_spmd(nc, in_maps, core_ids=list(range(n_cores))).results

    y_p = np.zeros((B, SEQ, D), f32)
    k_p = np.zeros((1, B, SEQ, H, DH), f32)
    v_p = np.zeros((1, B, SEQ, H, DH), f32)
    sc_p = np.zeros((1, B, 30, D), f32)
    sf_p = np.zeros((2, B, 2, DFF), f32)
    y_s = np.zeros((DB, TQ, D), f32)
    k_s = np.zeros((1, DB, TQ, H, DH), f32)
    v_s = np.zeros((1, DB, TQ, H, DH), f32)
    sc_s = np.zeros((1, DB, 30, D), f32)
    sf_s = np.zeros((2, DB, 2, DFF), f32)
    for c in range(n_cores):
        b, r = c // 2, c % 2
        o = res[c]
        rows = slice(r * half, (r + 1) * half)
        y_p[b, rows] = o["y_p"]
        k_p[0, b, rows] = o["k_p"].reshape(half, H, DH)
        v_p[0, b, rows] = o["v_p"].reshape(half, H, DH)
        if r == 1:
            sc_p[0, b] = o["sc_p"]
            sf_p[:, b] = o["sf_p"]
        sl = slice(c * NSEQ, (c + 1) * NSEQ)
        y_s[sl] = o["y_s"].reshape(NSEQ, TQ, D)
        k_s[0, sl] = o["k_s"].reshape(NSEQ, TQ, H, DH)
        v_s[0, sl] = o["v_s"].reshape(NSEQ, TQ, H, DH)
        sc_s[0, sl] = o["sc_s"]
        sf_s[:, sl] = o["sf_s"].reshape(2, NSEQ, 2, DFF)
    return (y_p, y_s, sc_p, sf_p, k_p, v_p, sc_s, sf_s, k_s, v_s)
```

```python
import numpy as np
import ml_dtypes
import concourse.bass as bass
import concourse.mybir as mybir
from concourse.bass_utils import run_bass_kernel_spmd

F32 = mybir.dt.float32
BF16 = mybir.dt.bfloat16
I32 = mybir.dt.int32
AF = mybir.ActivationFunctionType
ALU = mybir.AluOpType
NEG = -30000.0
EPS = 1e-6
CONV_W = 31
FFN_W = 3
DH = 64
TQ = 4
PAGE = 128


class Trk:
    __slots__ = ("w", "r")

    def __init__(self):
        self.w = None
        self.r = {}


class Prog:
    ENG = ("pe", "act", "dve", "pool", "sp")

    def __init__(self, nc, ndma=24):
        self.nc = nc
        self.h = {"pe": nc.tensor, "act": nc.scalar, "dve": nc.vector, "pool": nc.gpsimd, "sp": nc.sync}
        self.sem = {e: nc.alloc_semaphore("s_" + e) for e in self.ENG}
        self.ndma = ndma
        for k in range(ndma):
            self.sem["d%d" % k] = nc.alloc_semaphore("s_d%d" % k)
        self.cnt = {e: 0 for e in self.ENG}
        self.dcnt = [0] * ndma
        self.dnext = 0
        self.lists = {e: [] for e in self.ENG}
        self.waited = {e: {} for e in self.ENG}

    def _deps(self, eng, reads, writes):
        need = {}
        for t in reads:
            if t.w is not None and need.get(t.w[0], 0) < t.w[1]:
                need[t.w[0]] = t.w[1]
        for t in writes:
            if t.w is not None and need.get(t.w[0], 0) < t.w[1]:
                need[t.w[0]] = t.w[1]
            for s, v in t.r.items():
                if need.get(s, 0) < v:
                    need[s] = v
        if eng == "pe":
            need.pop("pe", None)
        w = self.waited[eng]
        out = []
        for s, v in need.items():
            if w.get(s, 0) < v:
                w[s] = v
                out.append((s, v))
        return out

    def _mark(self, tok, reads, writes):
        s, v = tok
        for t in reads:
            if t.r.get(s, 0) < v:
                t.r[s] = v
        for t in writes:
            t.w = tok
            t.r = {}

    def op(self, eng, fn, reads=(), writes=()):
        waits = self._deps(eng, reads, writes)
        self.cnt[eng] += 1
        self.lists[eng].append((waits, fn, eng, 1))
        self._mark((eng, self.cnt[eng]), reads, writes)

    def dma(self, eng, fn, reads=(), writes=()):
        waits = self._deps(eng, reads, writes)
        k = self.dnext
        self.dnext = (k + 1) % self.ndma
        s = "d%d" % k
        prev = self.dcnt[k]
        if prev and self.waited[eng].get(s, 0) < prev:
            self.waited[eng][s] = prev
            waits.append((s, prev))
        self.dcnt[k] += 16
        self.lists[eng].append((waits, fn, s, 16))
        self._mark((s, self.dcnt[k]), reads, writes)

    def barrier(self):
        for e in self.ENG:
            waits = []
            w = self.waited[e]
            for f in self.ENG:
                if f != e and self.cnt[f] > w.get(f, 0):
                    w[f] = self.cnt[f]
                    waits.append((f, self.cnt[f]))
            if self.cnt[e] > w.get(e, 0):
                w[e] = self.cnt[e]
                waits.append((e, self.cnt[e]))
            for k in range(self.ndma):
                s = "d%d" % k
                if self.dcnt[k] > w.get(s, 0):
                    w[s] = self.dcnt[k]
                    waits.append((s, self.dcnt[k]))
            if waits:
                self.lists[e].append((waits, None, None, 0))

    def emit(self):
        nc = self.nc
        sem = self.sem
        lists = self.lists

        def run(e, lst):
            for waits, fn, s, inc in lst:
                for ws, wv in waits:
                    e.wait_ge(sem[ws], wv)
                if fn is not None:
                    fn(e).then_inc(sem[s], inc)

        with nc.Block() as block:
            @block.tensor
            def _(e):
                run(e, lists["pe"])

            @block.scalar
            def _(e):
                run(e, lists["act"])

            @block.vector
            def _(e):
                run(e, lists["dve"])

            @block.gpsimd
            def _(e):
                run(e, lists["pool"])

            @block.sync
            def _(e):
                run(e, lists["sp"])


class Ring:
    def __init__(self, aps):
        self.items = [(a, Trk()) for a in aps]
        self.i = 0

    def next(self):
        it = self.items[self.i]
        self.i = (self.i + 1) % len(self.items)
        return it


class Region:
    def __init__(self, nc, start, size, name):
        self.nc, self.start, self.size, self.name = nc, start, size, name
        self.cur = start
        self.n = 0

    def reset(self):
        self.cur = self.start

    def alloc(self, name, shape, dt):
        esz = 4 if dt in (F32, I32) else 2
        n = 1
        for s in shape[1:]:
            n *= s
        nbytes = (n * esz + 63) // 64 * 64
        off = self.cur
        self.cur += nbytes
        assert self.cur <= self.start + self.size, (self.name, name, self.cur - self.start, self.size)
        self.n += 1
        return self.nc.alloc_sbuf_tensor_at("%s_%s_%d" % (self.name, name, self.n), list(shape), dt, offset=off).ap()


def build(cfg):
    D, DFF, H = cfg["D"], cfg["DFF"], cfg["H"]
    WIN, NSEQ, NPG, NPOOL = cfg["WIN"], cfg["NSEQ"], cfg["NPG"], cfg["NPOOL"]
    KC, FC = D // 128, DFF // 128
    assert D == H * DH and KC * 2 == H
    TS = 512
    NT = WIN // TS
    NB = WIN // 128
    OWN_T0 = NT // 2
    NOWN = WIN // 2
    NTS = NSEQ * TQ
    CW = 512
    CWD = 256
    SCALE = DH ** -0.5
    HC = H * TQ
    G = max(1, min(NSEQ, 512 // HC))

    nc = bass.Bass("TRN2", target_bir_lowering=False)
    P = Prog(nc)

    def din(name, shape, dt=F32):
        return nc.dram_tensor(name, list(shape), dt, kind="ExternalInput").ap()

    def dout(name, shape, dt=F32):
        return nc.dram_tensor(name, list(shape), dt, kind="ExternalOutput").ap()

    def dscr(name, shape, dt=BF16):
        return nc.dram_tensor(name, list(shape), dt, kind="Internal").ap()

    xwin = din("xwin", [WIN, D])
    nullb = din("nullb", [128, NB])
    valid3 = din("valid3", [1, TS])
    x_s = din("x_s", [NTS, D])
    st_conv = din("st_conv", [NSEQ * 30, D])
    st_ffn = din("st_ffn", [2, NSEQ * 2, DFF])
    cache_k = din("cache_k", [NPOOL * PAGE, D])
    cache_v = din("cache_v", [NPOOL * PAGE, D])
    ptab = din("ptab", [1, NSEQ * NPG], I32)
    consts = din("consts", [128, 5 * 128])
    mask_new = din("mask_new", [NTS, NSEQ // G, G * HC])
    pvec_d = din("pvec_d", [9 + CONV_W, D])
    pvec_f = din("pvec_f", [8, DFF])
    norm_mix = din("norm_mix", [2, D])
    norm_ffn = din("norm_ffn", [2, D])
    norm_final = din("norm_final", [1, D])
    conv_w_in = din("conv_w_in", [D, 2 * D])
    conv_b_in = din("conv_b_in", [1, 2 * D])
    conv_w_dw = din("conv_w_dw", [CONV_W, D])
    conv_b_dw = din("conv_b_dw", [1, D])
    conv_ln_g = din("conv_ln_g", [1, D])
    conv_ln_b = din("conv_ln_b", [1, D])
    conv_w_out = din("conv_w_out", [D, D])
    conv_b_out = din("conv_b_out", [1, D])
    attn_w_qkv = din("attn_w_qkv", [D, 3 * D])
    attn_w_o = din("attn_w_o", [D, D])
    attn_b = din("attn_b", [1, H])
    ffn_w_gate = din("ffn_w_gate", [2, D, DFF])
    ffn_w_up = din("ffn_w_up", [2, D, DFF])
    ffn_w_dw = din("ffn_w_dw", [2, FFN_W, DFF])
    ffn_b_dw = din("ffn_b_dw", [2, 1, DFF])
    ffn_w_down = din("ffn_w_down", [2, DFF, D])

    y_p = dout("y_p", [NOWN, D])
    k_p = dout("k_p", [NOWN, D])
    v_p = dout("v_p", [NOWN, D])
    sc_p = dout("sc_p", [30, D])
    sf_p = dout("sf_p", [2, 2, DFF])
    y_s = dout("y_s", [NTS, D])
    k_s = dout("k_s", [NTS, D])
    v_s = dout("v_s", [NTS, D])
    sc_s = dout("sc_s", [NSEQ, 30, D])
    sf_s = dout("sf_s", [2, NSEQ * 2, DFF])

    w_in_fm = dscr("w_in_fm", [2 * KC, 128, KC * 128])
    w_out_tm = dscr("w_out_tm", [D // CW, 128, KC * CW])
    w_gate_fm = [dscr("w_gate_fm%d" % l, [FC, 128, KC * 128]) for l in range(2)]
    w_up_fm = [dscr("w_up_fm%d" % l, [FC, 128, KC * 128]) for l in range(2)]
    w_down_tm = [dscr("w_down_tm%d" % l, [D // CWD, 128, FC * CWD]) for l in range(2)]
    w_qk_fm = dscr("w_qk_fm", [2 * KC, 128, KC * 128])
    w_kv_tm = dscr("w_kv_tm", [2 * D // CW, 128, KC * CW])
    w_o_tm = dscr("w_o_tm", [D // CW, 128, KC * CW])
    kt_scr = dscr("kt_scr", [KC, 128, WIN])
    v_scr = dscr("v_scr", [WIN, D])
    trk_kt = Trk()
    trk_v = Trk()
    trk_w = Trk()

    base = (nc.sbuf_base + 63) // 64 * 64
    top = nc.sbuf_top
    PR = Region(nc, base, top - base, "pers")
    sb = PR.alloc

    cst_f = sb("cst_f", [128, 5 * 128], F32)
    ident_f = cst_f[:, 0:128]
    ident_b = sb("ident_b", [128, 128], BF16)
    negtri = sb("negtri", [128, 128], BF16)
    negones = sb("negones", [128, 128], BF16)
    maskneg = sb("maskneg", [128, 128], BF16)
    posones = sb("posones", [128, 128], BF16)
    ones_row = sb("ones_row", [1, 512], BF16)
    zero_row = sb("zero_row", [1, 512], BF16)
    b_in_c = sb("b_in_c", [128, 2 * KC], F32)
    w_dw_c = sb("w_dw_c", [128, KC, CONV_W], F32)
    b_dw_c = sb("b_dw_c", [128, KC], F32)
    ln_g_c = sb("ln_g_c", [128, KC], F32)
    ln_b_c = sb("ln_b_c", [128, KC], F32)
    fw_dw_c = sb("fw_dw_c", [128, 2, FC, FFN_W], F32)
    fb_dw_c = sb("fb_dw_c", [128, 2, FC], F32)
    gain_c = sb("gain_c", [128, 4, KC], F32)
    b_out_r = sb("b_out_r", [1, D], BF16)
    b_out_f = sb("b_out_f", [1, D], F32)
    gfin = sb("gfin", [128, D], F32)
    nullb_c = sb("nullb_c", [128, NB], F32)
    bh_bc = sb("bh_bc", [128, H], F32)
    kbias = sb("kbias", [128, NB, H], F32)
    cvec = sb("cvec", [128, KC], F32)
    valid_c = sb("valid_c", [128, TS], F32)
    hist_g = [sb("hist_g%d" % l, [128, FC, 2], F32) for l in range(2)]
    hist_glu = sb("hist_glu", [128, KC, 30], F32)
    t_hglu = Trk()
    epsc = sb("epsc", [128, 1], F32)
    stat = sb("stat", [128, 16], F32)
    lnv = sb("lnv", [128, 16], F32)
    rstd = sb("rstd", [128, 16], F32)
    t_const = Trk()
    t_stat = Trk()
    t_statb = [Trk() for _ in range(8)]
    t_hist = [Trk(), Trk()]

    xt = sb("xt", [128, 4, D], F32)
    t_xt = [Trk() for _ in range(4)]
    xhalo = sb("xhalo", [128, D], F32)
    t_xhalo = Trk()
    hT = sb("hT", [128, KC, TS], BF16)
    t_hT = Trk()
    xn_ring_aps = [sb("xn%d" % i, [128, D], BF16) for i in range(2)]
    junk = sb("junk", [128, D], BF16)
    t_junk = Trk()
    xn_ring = Ring(xn_ring_aps)
    wfm_ring = Ring([sb("wfm%d" % i, [128, KC, 128], BF16) for i in range(5)])
    wtm_ring = Ring([sb("wtm%d" % i, [128, max(FC * CWD, KC * CW)], BF16) for i in range(2)])

    R1_SZ = 36 * 1024
    R2_SZ = FC * TS * 2
    R3_SZ = 13 * 1024
    r1s = PR.cur
    R1 = Region(nc, r1s, R1_SZ, "r1")
    R2 = Region(nc, r1s + R1_SZ, R2_SZ, "r2")
    R12 = Region(nc, r1s, R1_SZ + R2_SZ, "r12")
    R3 = Region(nc, r1s + R1_SZ + R2_SZ, R3_SZ, "r3")
    r4s = r1s + R1_SZ + R2_SZ + R3_SZ
    assert r4s < top, (r4s, top)
    R4 = Region(nc, r4s, top - r4s, "r4")

    EXTW = 30 + TS
    EW = 30 + TQ
    EXTA = max(EXTW, NSEQ * EW)
    ext_full = R1.alloc("ext", [128, KC, EXTA], F32)
    ext = ext_full[:, :, 0:EXTW]
    t_ext = [Trk() for _ in range(KC)]
    yacc = R1.alloc("yacc", [128, KC, TS], F32)
    t_yacc = [Trk() for _ in range(KC)]
    sc22 = R2.alloc("sc22", [128, FC, TS], BF16)
    t_sc22 = [Trk() for _ in range(FC)]
    mean_t = R3.alloc("mean_t", [128, TS], F32)
    rstd_t = R3.alloc("rstd_t", [128, TS], F32)
    tmpa_ring = Ring([R3.alloc("tmp_a%d" % i, [128, TS], F32) for i in range(4)])
    t_mean = Trk()
    t_rstd = Trk()
    R3.reset()
    extg_ring = Ring([R3.alloc("extg%d" % i, [128, 2 + TS], F32) for i in range(2)])
    gc_ring = Ring([R3.alloc("gc%d" % i, [128, TS], F32) for i in range(2)])
    sil_ring = Ring([R3.alloc("sil%d" % i, [128, TS], F32) for i in range(2)])
    R2.reset()
    QT = R2.alloc("QT", [128, KC, TS], BF16)
    t_QT = Trk()
    attT = R2.alloc("attT", [128, KC, TS], BF16)
    t_attT = [Trk() for _ in range(KC)]
    R1.reset()
    R3.reset()
    KTt = R3.alloc("KTt", [128, KC, TS], BF16)
    t_KTt = Trk()
    kvtm_ring = Ring([R1.alloc("kvtm%d" % i, [128, 2 * D], F32) for i in range(4)])
    vbf_ring = Ring([R3.alloc("vbf%d" % i, [128, D], BF16) for i in range(2)])
    R1.reset()
    Kp_ring = Ring([R1.alloc("Kp%d" % i, [128, WIN], BF16) for i in range(2)])
    Vp_ring = Ring([R1.alloc("Vp%d" % i, [128, NB, 128], BF16) for i in range(2)])
    NPEC = KC // 2
    dg_bytes = NPEC * CONV_W * 128 * 2
    eb_bytes = (NPEC * EXTW * 2 + 63) // 64 * 64
    RDG = Region(nc, (top - dg_bytes - eb_bytes) // 64 * 64, dg_bytes + eb_bytes, "rdg")
    DG = RDG.alloc("DG", [128, NPEC * CONV_W, 128], BF16)
    extb = RDG.alloc("extb", [128, NPEC, EXTW], BF16)
    t_extb = [Trk() for _ in range(NPEC)]
    R34 = Region(nc, R3.start, RDG.start - R3.start, "r34")
    e_ring = Ring([R34.alloc("e%d" % i, [128, 2, TS], F32) for i in range(2)])
    sp_ring = Ring([R34.alloc("sp%d" % i, [128, 2, TS], BF16) for i in range(3)])
    S_ring = Ring([R34.alloc("S%d" % i, [128, 2, TS], BF16) for i in range(2)])
    W_ring = Ring([R34.alloc("W%d" % i, [128, 2, TS], BF16) for i in range(3)])
    R1.reset()
    yout_ring = Ring([R1.alloc("yout%d" % i, [128, D], F32) for i in range(2)])
    R1.reset()
    R2.reset()
    stg_f = Ring([R1.alloc("stg_f%d" % i, [128, 3 * D], F32) for i in range(3)])
    stg_b = Ring([R2.alloc("stg_b%d" % i, [128, 3 * D], BF16) for i in range(3)])

    ps_all = nc.alloc_psum_tensor("ps_all", [128, 8 * 512], F32).ap()
    banks = [(ps_all[:, i * 512:(i + 1) * 512], Trk()) for i in range(8)]
    bank_i = {}
    pzpair_ring = Ring([ps_all[:, 2 * i * 512:(2 * i + 2) * 512].rearrange("p (b n) -> p b n", b=2) for i in range(3)])

    def nbank(lo=0, hi=8):
        i = bank_i.get((lo, hi), lo)
        bank_i[(lo, hi)] = i + 1 if i + 1 < hi else lo
        return banks[i]

    def mm(out, lhsT, rhs, start, stop, reads, writes):
        P.op("pe", lambda e: e.matmul(out, lhsT, rhs, start=start, stop=stop), reads, writes)

    def act(out, in_, func, reads, writes, bias=None, scale=None, accum=None):
        kw = {}
        if bias is not None:
            kw["bias"] = bias
        if scale is not None:
            kw["scale"] = scale
        if accum is not None:
            kw["accum_out"] = accum
        P.op("act", lambda e: e.activation(out, in_, func, **kw), reads, writes)

    def tcopy(eng, out, in_, reads, writes):
        if eng == "act":
            P.op(eng, lambda e: e.activation(out, in_, AF.Copy), reads, writes)
        else:
            P.op(eng, lambda e: e.tensor_copy(out, in_), reads, writes)

    def tt(eng, out, a, b, op, reads, writes):
        P.op(eng, lambda e: e.tensor_tensor(out, a, b, op), reads, writes)

    def ts(eng, out, a, s1, s2, op0, op1, reads, writes):
        if op1 is None:
            P.op(eng, lambda e: e.tensor_scalar(out, a, s1, None, op0), reads, writes)
        else:
            P.op(eng, lambda e: e.tensor_scalar(out, a, s1, s2, op0, op1), reads, writes)

    def stt(eng, out, a, s, b, op0, op1, reads, writes):
        P.op(eng, lambda e: e.scalar_tensor_tensor(out, a, s, b, op0, op1), reads, writes)

    def dma(eng, out, in_, reads, writes, **kw):
        P.dma(eng, lambda e: e.dma_start(out=out, in_=in_, **kw), reads, writes)

    def memset(eng, ap, val, writes):
        P.op(eng, lambda e: e.memset(ap, val), (), writes)

    dma("sp", cst_f, consts, (), (t_const,))
    tcopy("dve", ident_b, cst_f[:, 0:128], (t_const,), (t_const,))
    tcopy("dve", negtri, cst_f[:, 128:256], (t_const,), (t_const,))
    tcopy("dve", negones, cst_f[:, 256:384], (t_const,), (t_const,))
    tcopy("dve", maskneg, cst_f[:, 384:512], (t_const,), (t_const,))
    tcopy("dve", posones, cst_f[:, 512:640], (t_const,), (t_const,))
    memset("dve", ones_row, 1.0, (t_const,))
    memset("dve", zero_row, 0.0, (t_const,))
    memset("dve", epsc, EPS, (t_const,))

    NPD = 9 + CONV_W
    prow, t_prow = stg_f.items[0]
    prow2, t_prow2 = stg_f.items[1]
    dma("sp", prow[0:NPD, 0:D], pvec_d, (), (t_prow,))
    dma("sp", prow2[0:8, 0:DFF], pvec_f, (), (t_prow2,))
    for kc in range(KC):
        pb, tpb = nbank()
        P.op("pe", lambda e, kc=kc, pb=pb: e.transpose(pb[:, 0:NPD], prow[0:NPD, kc * 128:(kc + 1) * 128],
                                                      ident_f[0:NPD, 0:NPD]), (t_prow, t_const), (tpb,))
        tcopy("dve", b_in_c[:, kc:kc + 1], pb[:, 0:1], (tpb,), (t_const,))
        tcopy("dve", b_in_c[:, KC + kc:KC + kc + 1], pb[:, 1:2], (tpb,), (t_const,))
        tcopy("dve", b_dw_c[:, kc:kc + 1], pb[:, 2:3], (tpb,), (t_const,))
        tcopy("dve", ln_g_c[:, kc:kc + 1], pb[:, 3:4], (tpb,), (t_const,))
        tcopy("dve", ln_b_c[:, kc:kc + 1], pb[:, 4:5], (tpb,), (t_const,))
        tcopy("dve", w_dw_c[:, kc, :], pb[:, 5:5 + CONV_W], (tpb,), (t_const,))
        tcopy("dve", gain_c[:, :, kc], pb[:, 5 + CONV_W:9 + CONV_W], (tpb,), (t_const,))
    for c in range(FC):
        pb, tpb = nbank()
        P.op("pe", lambda e, c=c, pb=pb: e.transpose(pb[:, 0:8], prow2[0:8, c * 128:(c + 1) * 128],
                                                    ident_f[0:8, 0:8]), (t_prow2, t_const), (tpb,))
        tcopy("dve", fb_dw_c[:, :, c], pb[:, 0:2], (tpb,), (t_const,))
        tcopy("dve", fw_dw_c[:, :, c, :], pb[:, 2:8].rearrange("p (l j) -> p l j", j=FFN_W), (tpb,), (t_const,))
    dma("sp", b_out_f, conv_b_out, (), (t_const,))
    tcopy("dve", b_out_r, b_out_f, (t_const,), (t_const,))
    dma("sp", gfin, norm_final.partition_broadcast(128), (), (t_const,))
    dma("sp", nullb_c, nullb, (), (t_const,))
    dma("sp", bh_bc, attn_b.partition_broadcast(128), (), (t_const,))
    dma("sp", valid_c, valid3.partition_broadcast(128), (), (t_const,))
    bhp = bh_bc.rearrange("p (k t) -> p k t", t=2)
    tcopy("dve", cvec[0:64, :], bhp[0:64, :, 0], (t_const,), (t_const,))
    tcopy("dve", cvec[64:128, :], bhp[64:128, :, 1], (t_const,), (t_const,))
    act(cvec, cvec, AF.Exp, (t_const,), (t_const,))
    for kb in range(NB):
        ts("dve", kbias[:, kb, :], bh_bc, nullb_c[:, kb:kb + 1], None, ALU.add, None, (t_const,), (t_const,))
    for l in range(2):
        memset("dve", hist_g[l], 0.0, (t_hist[l],))
    for c in range(KC - NPEC, KC):
        for j in range(CONV_W):
            ts(("dve", "pool")[j % 2], DG[:, (c - (KC - NPEC)) * CONV_W + j, :], ident_f, w_dw_c[:, c, j:j + 1], None,
               ALU.mult, None, (t_const,), (t_const,))

    cast_rr = [0]

    def convert(src, kin, n, gain_idx, dests):
        kin_c = kin // 128
        for kc in range(kin_c):
            sf, tf = stg_f.next()
            sbf, tb = stg_b.next()
            dma("sp", sf[:, 0:n], src[kc * 128:(kc + 1) * 128, :], (), (tf,))
            if gain_idx is not None:
                eng = ("act", "act", "dve")[cast_rr[0] % 3]
            else:
                eng = ("dve", "act", "pool")[cast_rr[0] % 3]
            cast_rr[0] += 1
            if gain_idx is None:
                if eng == "act":
                    act(sbf[:, 0:n], sf[:, 0:n], AF.Copy, (tf,), (tb,))
                else:
                    tcopy(eng, sbf[:, 0:n], sf[:, 0:n], (tf,), (tb,))
            else:
                gsc = gain_c[:, gain_idx, kc:kc + 1]
                if eng == "act":
                    act(sbf[:, 0:n], sf[:, 0:n], AF.Copy, (tf, t_const), (tb,), scale=gsc)
                else:
                    ts(eng, sbf[:, 0:n], sf[:, 0:n], gsc, None, ALU.mult, None, (tf, t_const), (tb,))
            for kind, scr, c0, ncols in dests:
                w = 128 if kind == "fm" else kind
                dst = scr.rearrange("s p (k n) -> p s k n", k=kin_c)[:, :, kc, :]
                srcv = sbf[:, c0:c0 + ncols].rearrange("p (s n) -> p s n", n=w)
                dma("act", dst, srcv, (tb,), (trk_w,))

    convert(conv_w_in, D, 2 * D, 0, [("fm", w_in_fm, 0, 2 * D)])
    convert(conv_w_out, D, D, None, [(CW, w_out_tm, 0, D)])
    for l in range(2):
        convert(ffn_w_gate[l], D, DFF, 1 + 2 * l, [("fm", w_gate_fm[l], 0, DFF)])
        convert(ffn_w_up[l], D, DFF, 1 + 2 * l, [("fm", w_up_fm[l], 0, DFF)])
        convert(ffn_w_down[l], DFF, D, None, [(CWD, w_down_tm[l], 0, D)])
    convert(attn_w_qkv, D, 3 * D, 2, [("fm", w_qk_fm, 0, 2 * D), (CW, w_kv_tm, D, 2 * D)])
    convert(attn_w_o, D, D, None, [(CW, w_o_tm, 0, D)])
    P.barrier()

    CW_DEFAULT = CW
    def load_fm(scr, oc):
        w, tw = wfm_ring.next()
        dma("sp", w, scr[oc].rearrange("p (k n) -> p k n", k=KC), (trk_w,), (tw,))
        return w, tw

    def load_tm(scr, cs, kin_c, cw):
        w, tw = wtm_ring.next()
        wv = w[:, 0:kin_c * cw].rearrange("p (k n) -> p k n", k=kin_c)
        dma("sp", wv, scr[cs].rearrange("p (k n) -> p k n", k=kin_c), (trk_w,), (tw,))
        return wv, tw

    def row_stats(xa, tx, n, b):
        tsb = t_statb[b]
        act(junk[:n, :], xa, AF.Square, (tx,), (t_junk, tsb), accum=stat[:n, b:b + 1])
        act(lnv[:n, b:b + 1], stat[:n, b:b + 1], AF.Ln, (tsb, t_const), (tsb,), bias=epsc[:n, :], scale=1.0 / D)
        act(rstd[:n, b:b + 1], lnv[:n, b:b + 1], AF.Exp, (tsb,), (tsb,), scale=-0.5)

    def rms_to_hT(blocks):
        for b, (xa, tx, n) in enumerate(blocks):
            row_stats(xa, tx, n, b)
            xb, txn = xn_ring.next()
            ts("dve", xb[:n, :], xa, rstd[:n, b:b + 1], None, ALU.mult, None, (tx, t_statb[b]), (txn,))
            pb, tpb = nbank()
            pbv = pb.bitcast(BF16)
            for kc in range(KC):
                P.op("pe", lambda e, kc=kc, n=n, xb=xb, pbv=pbv: e.transpose(
                    pbv[:, kc * 128:kc * 128 + n], xb[:n, kc * 128:(kc + 1) * 128], ident_b[:n, :n]),
                    (txn, t_const), (tpb,))
            src = pbv[:, 0:KC * 128].rearrange("p (k n) -> p k n", k=KC)[:, :, 0:n]
            tcopy("dve", hT[:, :, b * 128:b * 128 + n], src, (tpb,), (t_hT,))

    def linear_fm(scr, oc_list, N, consume, rhs=None, trhs=None):
        rhs = hT if rhs is None else rhs
        trhs = t_hT if trhs is None else trhs
        for i, oc in enumerate(oc_list):
            w, tw = load_fm(scr, oc)
            pb, tpb = nbank()
            for kc in range(KC):
                mm(pb[:, 0:N], w[:, kc, :], rhs[:, kc, 0:N], kc == 0, kc == KC - 1, (tw, trhs), (tpb,))
            consume(i, oc, pb[:, 0:N], tpb)

    def linear_tm(scr, ncols, kin_c, lhs, tlhs_fn, blocks, consume, bias_row=None, cw=None):
        CW = cw or CW_DEFAULT
        for cs in range(ncols // CW):
            w, tw = load_tm(scr, cs, kin_c, CW)
            for b, n in blocks:
                pb, tpb = nbank()
                for kc in range(kin_c):
                    mm(pb[:n, 0:CW], lhs[:, kc, b * 128:b * 128 + n], w[:, kc, :], kc == 0,
                       kc == kin_c - 1 and bias_row is None, (tw, tlhs_fn(kc)), (tpb,))
                if bias_row is not None:
                    mm(pb[:n, 0:CW], ones_row[0:1, 0:n], bias_row[0:1, cs * CW:(cs + 1) * CW], False, True,
                       (t_const,), (tpb,))
                consume(cs * CW, CW, b, n, pb[:n, 0:CW], tpb)

    def resid_add(xblocks):
        def f(c0, cw, b, n, pb, tpb):
            xa, tx, _ = xblocks[b]
            tt("dve", xa[:, c0:c0 + cw], xa[:, c0:c0 + cw], pb, ALU.add, (tx, tpb), (tx,))
        return f

    def blist(xblocks):
        return [(b, n) for b, (_, _, n) in enumerate(xblocks)]

    def conv_module(xblocks, N, t_extv, glu_dst, tap_src, view, valid=None, use_pe=False):
        rms_to_hT(xblocks)
        for c in range(KC):
            res = {}

            def cons(i, oc, pb, tpb, res=res):
                res[i] = (pb, tpb)
            linear_fm(w_in_fm, [c, c + KC], N, cons)
            (pa, tpa), (pg, tpg) = res[0], res[1]
            sg, tsg = tmpa_ring.next()
            act(sg[:, 0:N], pg, AF.Sigmoid, (tpg, t_const), (tsg,), bias=b_in_c[:, KC + c:KC + c + 1])
            stt("dve", glu_dst(c), view(pa), b_in_c[:, c:c + 1], view(sg[:, 0:N]), ALU.add, ALU.mult,
                (tpa, tsg, t_const), (t_extv[c],))
            if valid is not None:
                tt("dve", glu_dst(c), glu_dst(c), view(valid[:, 0:N]), ALU.mult, (t_extv[c], t_const), (t_extv[c],))
        ndve = KC if not use_pe else KC - NPEC
        for c in range(ndve, KC):
            tcopy("act" if c % 2 == 0 else "pool", extb[:, c - ndve, :], ext[:, c, :], (t_extv[c],), (t_extb[c - ndve],))
        for j in range(CONV_W):
            for c in range(ndve):
                yv = view(yacc[:, c, 0:N])
                if j == 0:
                    ts("dve", yv, tap_src(c, 0), w_dw_c[:, c, 0:1], b_dw_c[:, c:c + 1], ALU.mult, ALU.add,
                       (t_extv[c], t_const), (t_yacc[c],))
                else:
                    stt("dve", yv, tap_src(c, j), w_dw_c[:, c, j:j + 1], yv, ALU.mult, ALU.add,
                        (t_extv[c], t_const, t_yacc[c]), (t_yacc[c],))
        for c in range(ndve, KC):
            py, tpy = nbank()
            for j in range(CONV_W):
                mm(py[:, 0:N], DG[:, (c - ndve) * CONV_W + j, :], extb[:, c - ndve, j:j + N], j == 0, j == CONV_W - 1,
                   (t_const, t_extb[c - ndve]), (tpy,))
            act(yacc[:, c, 0:N], py[:, 0:N], AF.Identity, (tpy, t_const), (t_yacc[c],), bias=b_dw_c[:, c:c + 1])
        for c in range(KC):
            tcopy("dve" if c % 2 == 0 else "pool", sc22[:, c, 0:N], yacc[:, c, 0:N], (t_yacc[c],), (t_sc22[c],))
            act(sc22[:, KC + c, 0:N], yacc[:, c, 0:N], AF.Square, (t_yacc[c],), (t_sc22[KC + c],))
        pm, tpm = nbank()
        pq, tpq = nbank()
        for c in range(KC):
            mm(pm[:, 0:N], posones, sc22[:, c, 0:N], c == 0, c == KC - 1, (t_const, t_sc22[c]), (tpm,))
        for c in range(KC):
            mm(pq[:, 0:N], posones, sc22[:, KC + c, 0:N], c == 0, c == KC - 1, (t_const, t_sc22[KC + c]), (tpq,))
        ts("dve", mean_t[:, 0:N], pm[:, 0:N], 1.0 / D, None, ALU.mult, None, (tpm,), (t_mean,))
        m2, tm2 = tmpa_ring.next()
        tt("dve", m2[:, 0:N], mean_t[:, 0:N], mean_t[:, 0:N], ALU.mult, (t_mean,), (tm2,))
        stt("dve", m2[:, 0:N], pq[:, 0:N], 1.0 / D, m2[:, 0:N], ALU.mult, ALU.subtract, (tpq, tm2), (tm2,))
        act(m2[:, 0:N], m2[:, 0:N], AF.Ln, (tm2, t_const), (tm2,), bias=epsc[:, :])
        act(rstd_t[:, 0:N], m2[:, 0:N], AF.Exp, (tm2,), (t_rstd,), scale=-0.5)
        for c in range(KC):
            d1, td1 = tmpa_ring.next()
            tt("dve", d1[:, 0:N], yacc[:, c, 0:N], mean_t[:, 0:N], ALU.subtract, (t_yacc[c], t_mean), (td1,))
            tt("dve", d1[:, 0:N], d1[:, 0:N], rstd_t[:, 0:N], ALU.mult, (td1, t_rstd), (td1,))
            act(hT[:, c, 0:N], d1[:, 0:N], AF.Silu, (td1, t_const), (t_hT,),
                bias=ln_b_c[:, c:c + 1], scale=ln_g_c[:, c:c + 1])
        linear_tm(w_out_tm, D, KC, hT, lambda kc: t_hT, blist(xblocks), resid_add(xblocks),
                  bias_row=b_out_r)

    def ffn(l, xblocks, N, hist_src, gate_only=False, valid=None, hist_out=None):
        rms_to_hT(xblocks)
        for c in range(FC):
            eg, teg = extg_ring.next()
            res = {}

            def consg(i, oc, pb, tpb, res=res):
                res["g"] = (pb, tpb)
            linear_fm(w_gate_fm[l], [c], N, consg)
            pg, tpg = res["g"]
            hist_src(c, eg, teg)
            if valid is not None:
                tt("dve", eg[:, 2:2 + N], pg, valid[:, 0:N], ALU.mult, (tpg, t_const), (teg,))
            else:
                act(eg[:, 2:2 + N], pg, AF.Copy, (tpg,), (teg,))
            if hist_out is not None:
                hist_out(c, eg, teg)
            if gate_only:
                continue

            def consu(i, oc, pb, tpb, res=res):
                res["u"] = (pb, tpb)
            linear_fm(w_up_fm[l], [c], N, consu)
            pu, tpu = res["u"]
            gc, tgc = gc_ring.next()
            for j in range(FFN_W):
                if j == 0:
                    ts("dve", gc[:, 0:N], eg[:, 0:N], fw_dw_c[:, l, c, 0:1], None, ALU.mult, None,
                       (teg, t_const), (tgc,))
                else:
                    stt("dve", gc[:, 0:N], eg[:, j:j + N], fw_dw_c[:, l, c, j:j + 1], gc[:, 0:N], ALU.mult, ALU.add,
                        (teg, t_const, tgc), (tgc,))
            sl, tsl = sil_ring.next()
            act(sl[:, 0:N], gc[:, 0:N], AF.Silu, (tgc, t_const), (tsl,), bias=fb_dw_c[:, l, c:c + 1])
            tt("dve", sc22[:, c, 0:N], sl[:, 0:N], pu, ALU.mult, (tsl, tpu), (t_sc22[c],))
        if gate_only:
            return
        linear_tm(w_down_tm[l], D, FC, sc22, lambda kc: t_sc22[kc], blist(xblocks),
                  resid_add(xblocks), cw=CWD)

    def final_out(xblocks, out_rows):
        for b, (xa, tx, n) in enumerate(xblocks):
            row_stats(xa, tx, n, b)
            yo, tyo = yout_ring.next()
            stt("dve", yo[:n, :], xa, rstd[:n, b:b + 1], gfin[:n, :], ALU.mult, ALU.mult, (tx, t_statb[b], t_const), (tyo,))
            dma("sp", out_rows(b, n), yo[:n, :], (tyo,), ())

    def attention(nq, nkb_full, ndiag):
        nkb = nkb_full + ndiag
        nkeys = nkb * 128
        for pr in range(KC):
            Kp, tKp = Kp_ring.next()
            Vp, tVp = Vp_ring.next()
            dma("sp", Kp[:, 0:nkeys], kt_scr[pr, :, 0:nkeys], (trk_kt,), (tKp,))
            dma("sp", Vp[:, 0:nkb, :], v_scr[0:nkeys, pr * 128:(pr + 1) * 128].rearrange("(k p) d -> p k d", p=128),
                (trk_v,), (tVp,))
            po, tpo = nbank(6, 8)
            state = dict(S=None, first=True)

            def stage_a(kb):
                jd = kb - nkb_full
                c0 = 128 * jd if jd >= 0 else 0
                pz2, tpz = pzpair_ring.next()
                e2, te = e_ring.next()
                for hh in range(2):
                    r0 = hh * 64
                    mm(pz2[:, hh, c0:nq], Kp[r0:r0 + 64, kb * 128:(kb + 1) * 128], QT[r0:r0 + 64, pr, c0:nq],
                       True, False, (tKp, t_QT), (tpz,))
                    if jd >= 0:
                        mm(pz2[:, hh, c0:c0 + 128], ident_b, maskneg, False, False, (t_const,), (tpz,))
                for hh in range(2):
                    act(e2[:, hh, c0:nq], pz2[:, hh, c0:nq], AF.Exp, (tpz, t_const), (te,),
                        bias=kbias[:, kb, 2 * pr + hh:2 * pr + hh + 1])
                sp2, tsp = sp_ring.next()
                act(sp2[:, :, c0:nq], e2[:, :, c0:nq], AF.Ln, (te,), (tsp,), bias=1.0)
                return (pz2, tpz, sp2, tsp, c0)

            def stage_b(kb, a):
                pz2, tpz, sp2, tsp, c0 = a
                S_prev = state["S"]
                for hh in range(2):
                    mm(pz2[:, hh, c0:nq], negtri, sp2[:, hh, c0:nq], False, S_prev is None, (t_const, tsp), (tpz,))
                    if S_prev is not None:
                        Sa, tSa, s_c0 = S_prev
                        mm(pz2[:, hh, s_c0:nq], negones, Sa[:, hh, s_c0:nq], False, True, (t_const, tSa), (tpz,))
                W2, tW = W_ring.next()
                act(W2[:, :, c0:nq], pz2[:, :, c0:nq], AF.Exp, (tpz, t_const), (tW,), bias=nullb_c[:, kb:kb + 1])
                if kb > 0:
                    Sn, tSn = S_ring.next()
                    if S_prev is None:
                        tcopy("pool", Sn[:, :, c0:nq], sp2[:, :, c0:nq], (tsp,), (tSn,))
                    else:
                        Sa, tSa, s_c0 = S_prev
                        if s_c0 > c0:
                            tcopy("pool", Sn[:, :, c0:s_c0], sp2[:, :, c0:s_c0], (tsp,), (tSn,))
                        tt("pool", Sn[:, :, s_c0:nq], Sa[:, :, s_c0:nq], sp2[:, :, s_c0:nq], ALU.add, (tSa, tsp), (tSn,))
                    state["S"] = (Sn, tSn, c0)
                return (W2, tW, c0)

            def stage_c(kb, wv):
                W2, tW, c0 = wv
                for hh in range(2):
                    r0 = hh * 64
                    if state["first"] and c0 > 0:
                        mm(po[r0:r0 + 64, 0:nq], zero_row[0:1, 0:64], zero_row[0:1, 0:nq], True, False,
                           (t_const,), (tpo,))
                    mm(po[r0:r0 + 64, c0:nq], Vp[:, kb, r0:r0 + 64], W2[:, hh, c0:nq],
                       state["first"] and c0 == 0, kb == 0, (tVp, tW), (tpo,))
                state["first"] = False

            kbs = list(range(nkb - 1, -1, -1))
            pa, pw = {}, {}
            ns = len(kbs)
            for i in range(ns + 2):
                if i < ns:
                    pa[i] = stage_a(kbs[i])
                j = i - 1
                if 0 <= j < ns:
                    pw[j] = stage_b(kbs[j], pa.pop(j))
                k = j - 1
                if 0 <= k < ns:
                    stage_c(kbs[k], pw.pop(k))
            ts("dve", attT[:, pr, 0:nq], po[:, 0:nq], cvec[:, pr:pr + 1], None, ALU.mult, None,
               (tpo, t_const), (t_attT[pr],))

    def layer1(xblocks, N, q0, nkb_full, ndiag, own_tile, hist_src, hist_out, halo=False):
        rms_to_hT(xblocks)

        def consq(i, oc, pb, tpb):
            if oc < KC:
                act(QT[:, oc, 0:N], pb, AF.Copy, (tpb,), (t_QT,), scale=SCALE)
            else:
                tcopy("dve", KTt[:, oc - KC, 0:N], pb, (tpb,), (t_KTt,))
        if halo:
            linear_fm(w_qk_fm, list(range(KC)), N, consq)
        else:
            ocs = list(range(2 * KC)) if own_tile is not None else list(range(KC, 2 * KC))
            linear_fm(w_qk_fm, ocs, N, consq)
            dma("sp", kt_scr[:, :, q0:q0 + N].rearrange("c p n -> p c n"), KTt[:, :, 0:N], (t_KTt,), (trk_kt,))
            kvb = {b: kvtm_ring.next() for b, n in blist(xblocks)}

            def conskv(c0, cw, b_, n_, pb, tpb):
                kv, tkv = kvb[b_]
                tcopy("dve" if (c0 // cw) % 2 == 0 else "act", kv[:n_, c0:c0 + cw], pb, (tpb,), (tkv,))
            linear_tm(w_kv_tm, 2 * D, KC, hT, lambda kc: t_hT, blist(xblocks), conskv)
            for b, n in blist(xblocks):
                kv, tkv = kvb[b]
                vb, tvb = vbf_ring.next()
                tcopy("pool", vb[:n, :], kv[:n, D:2 * D], (tkv,), (tvb,))
                dma("sp", v_scr[q0 + b * 128:q0 + b * 128 + n, :], vb[:n, :], (tvb,), (trk_v,))
                if own_tile is not None:
                    r = own_tile * TS + b * 128
                    dma("sp", k_p[r:r + n, :], kv[:n, 0:D], (tkv,), ())
                    dma("sp", v_p[r:r + n, :], kv[:n, D:2 * D], (tkv,), ())
        if own_tile is None and not halo:
            return
        P.barrier()
        attention(N, nkb_full, ndiag)
        P.barrier()
        linear_tm(w_o_tm, D, KC, attT, lambda kc: t_attT[kc], blist(xblocks), resid_add(xblocks))
        if halo:
            ffn(1, xblocks, N, hist_src, gate_only=True, valid=valid_c[:, TS - 128:TS], hist_out=hist_out)
            return
        ffn(1, xblocks, N, hist_src, hist_out=hist_out)

    memset("pool", hist_glu, 0.0, (t_hglu,))

    def hist_from(l):
        def f(c, eg, teg):
            tcopy("pool", eg[:, 0:2], hist_g[l][:, c, :], (t_hist[l],), (teg,))
        return f

    def hist_to(l, N, export):
        def f(c, eg, teg):
            tcopy("pool", hist_g[l][:, c, :], eg[:, N:N + 2], (teg,), (t_hist[l],))
            if export and c == FC - 1:
                for tt_ in range(2):
                    dma("pool", sf_p[l, tt_:tt_ + 1, :].rearrange("o (c p) -> p (o c)", p=128), hist_g[l][:, :, tt_],
                        (t_hist[l],), (), allow_slow_non_contiguous=True)
        return f

    for t in range(NT):
        own = t - OWN_T0 if t >= OWN_T0 else None
        last = t == NT - 1
        xblocks = [(xt[:, b, :], t_xt[b], 128) for b in range(4)]
        for b in range(4):
            dma("sp", xt[:, b, :], xwin[t * TS + b * 128:t * TS + (b + 1) * 128, :], (), (t_xt[b],))
        vmask = valid_c if t == OWN_T0 - 1 else None
        for c in range(KC):
            tcopy("pool", ext[:, c, 0:30], hist_glu[:, c, :], (t_hglu,), (t_ext[c],))
        conv_module(xblocks, TS, t_ext, lambda c: ext[:, c, 30:30 + TS], lambda c, j: ext[:, c, j:j + TS],
                    lambda a: a, valid=vmask, use_pe=True)
        if last:
            P.barrier()
            pb0, tp0 = nbank()
            pb1, tp1 = nbank()
            hk = KC // 2
            for c in range(KC):
                pbx, tpx = (pb0, tp0) if c < hk else (pb1, tp1)
                cc = c % hk
                P.op("pe", lambda e, c=c, pbx=pbx, cc=cc: e.transpose(
                    pbx[0:30, cc * 128:(cc + 1) * 128], ext[:, c, TS:TS + 30], ident_f),
                    (t_ext[c], t_const), (tpx,))
            tcopy("dve", xhalo[0:30, 0:hk * 128], pb0[0:30, 0:hk * 128], (tp0,), (t_xhalo,))
            tcopy("dve", xhalo[0:30, hk * 128:2 * hk * 128], pb1[0:30, 0:hk * 128], (tp1,), (t_xhalo,))
            dma("sp", sc_p[:, :], xhalo[0:30, :], (t_xhalo,), ())
        for c in range(KC):
            tcopy("pool", hist_glu[:, c, :], ext[:, c, TS:TS + 30], (t_ext[c],), (t_hglu,))
        P.barrier()
        ffn(0, xblocks, TS, hist_from(0), valid=vmask, hist_out=hist_to(0, TS, last))
        P.barrier()
        if t == OWN_T0 - 1:
            tcopy("pool", xhalo, xt[:, 3, :], (t_xt[3],), (t_xhalo,))
        if own is None:
            layer1(xblocks, TS, t * TS, 0, 0, None, None, None)
            P.barrier()
            if t == OWN_T0 - 1:
                hb = [(xhalo, t_xhalo, 128)]
                layer1(hb, 128, OWN_T0 * TS - 128, OWN_T0 * 4 - 1, 1, None, hist_from(1), hist_to(1, 128, False),
                       halo=True)
                P.barrier()
        else:
            layer1(xblocks, TS, t * TS, t * 4, 4, own, hist_from(1), hist_to(1, TS, last))
            P.barrier()
            final_out(xblocks, lambda b, n: y_p[own * TS + b * 128:own * TS + b * 128 + n, :])
            P.barrier()

    N = NTS
    xs = xt[0:N, 0, :]
    t_xs = t_xt[0]
    xsb = [(xs, t_xs, N)]
    stage = xt[:, 1, :]
    t_stage = t_xt[1]
    glu_tm = xt[:, 2, :]
    t_glutm = t_xt[2]
    kvs = xt[:, 2:4, :].rearrange("p a d -> p (a d)")
    t_kvs = t_xt[3]
    dma("sp", xs, x_s, (), (t_xs,))
    ext_s = ext_full[:, :, 0:NSEQ * EW].rearrange("p k (s j) -> p k s j", j=EW)
    t_exts = [Trk() for _ in range(KC)]
    gl_c = R4.alloc("gl_c", [128, KC, N], F32)
    t_glc = Trk()
    SPB = 4
    for s0 in range(0, NSEQ, SPB):
        ns = min(SPB, NSEQ - s0)
        rows = ns * 30
        dma("sp", stage[0:rows, :], st_conv[s0 * 30:s0 * 30 + rows, :], (), (t_stage,))
        for c in range(KC):
            pb, tpb = nbank()
            P.op("pe", lambda e, c=c, pb=pb, rows=rows: e.transpose(
                pb[:, 0:rows], stage[0:rows, c * 128:(c + 1) * 128], ident_f[0:rows, 0:rows]),
                (t_stage, t_const), (tpb,))
            tcopy("dve", ext_s[:, c, s0:s0 + ns, 0:30], pb[:, 0:rows].rearrange("p (s j) -> p s j", j=30),
                  (tpb,), (t_exts[c],))
        for si in range(ns):
            dma("sp", sc_s[s0 + si, 0:26, :], stage[si * 30 + 4:si * 30 + 30, :], (t_stage,), ())

    def view_s(a):
        return a.rearrange("p (s i) -> p s i", i=TQ)

    conv_module(xsb, N, t_exts, lambda c: ext_s[:, c, :, 30:30 + TQ], lambda c, j: ext_s[:, c, :, j:j + TQ], view_s)
    for c in range(KC):
        tcopy("pool", view_s(gl_c[:, c, :]), ext_s[:, c, :, 30:30 + TQ], (t_exts[c],), (t_glc,))
    pb0, tp0 = nbank()
    pb1, tp1 = nbank()
    hk = KC // 2
    for c in range(KC):
        pbx, tpx = (pb0, tp0) if c < hk else (pb1, tp1)
        cc = c % hk
        P.op("pe", lambda e, c=c, pbx=pbx, cc=cc: e.transpose(
            pbx[0:N, cc * 128:(cc + 1) * 128], gl_c[:, c, :], ident_f), (t_glc, t_const), (tpx,))
    tcopy("dve", glu_tm[0:N, 0:hk * 128], pb0[0:N, 0:hk * 128], (tp0,), (t_glutm,))
    tcopy("dve", glu_tm[0:N, hk * 128:2 * hk * 128], pb1[0:N, 0:hk * 128], (tp1,), (t_glutm,))
    for s in range(NSEQ):
        dma("sp", sc_s[s, 26:30, :], glu_tm[s * TQ:(s + 1) * TQ, :], (t_glutm,), ())
    P.barrier()

    hs = [R4.alloc("hs%d" % l, [128, FC, NSEQ, 2], F32) for l in range(2)]
    t_hs = [Trk(), Trk()]
    gnew = [R4.alloc("gnew%d" % l, [128, FC, NSEQ * 2], F32) for l in range(2)]
    t_gnew = [Trk(), Trk()]
    R1.reset()
    stg2 = R1.alloc("stg2", [128, DFF], F32)
    t_stg2 = Trk()
    RS = NSEQ * 2
    for l in range(2):
        dma("sp", stg2[0:RS, :], st_ffn[l], (), (t_stg2,))
        for c in range(FC):
            pb, tpb = nbank()
            P.op("pe", lambda e, c=c, pb=pb: e.transpose(
                pb[:, 0:RS], stg2[0:RS, c * 128:(c + 1) * 128], ident_f[0:RS, 0:RS]), (t_stg2, t_const), (tpb,))
            tcopy("dve", hs[l][:, c, :, :], pb[:, 0:RS].rearrange("p (s j) -> p s j", j=2), (tpb,), (t_hs[l],))
    P.barrier()

    def ffn_s(l):
        rms_to_hT(xsb)
        for c in range(FC):
            eg, teg = extg_ring.next()
            egv = eg[:, 0:NSEQ * 6].rearrange("p (s j) -> p s j", j=6)
            res = {}

            def consg(i, oc, pb, tpb, res=res):
                res["g"] = (pb, tpb)
            linear_fm(w_gate_fm[l], [c], N, consg)
            pg, tpg = res["g"]
            tcopy("pool", egv[:, :, 0:2], hs[l][:, c, :, :], (t_hs[l],), (teg,))
            tcopy("dve", egv[:, :, 2:2 + TQ], view_s(pg), (tpg,), (teg,))
            tcopy("pool", gnew[l][:, c, :].rearrange("p (s j) -> p s j", j=2), egv[:, :, 4:6], (teg,), (t_gnew[l],))

            def consu(i, oc, pb, tpb, res=res):
                res["u"] = (pb, tpb)
            linear_fm(w_up_fm[l], [c], N, consu)
            pu, tpu = res["u"]
            gc, tgc = gc_ring.next()
            dst = view_s(gc[:, 0:N])
            for j in range(FFN_W):
                src = egv[:, :, j:j + TQ]
                if j == 0:
                    ts("dve", dst, src, fw_dw_c[:, l, c, 0:1], None, ALU.mult, None, (teg, t_const), (tgc,))
                else:
                    stt("dve", dst, src, fw_dw_c[:, l, c, j:j + 1], dst, ALU.mult, ALU.add,
                        (teg, t_const, tgc), (tgc,))
            sl, tsl = sil_ring.next()
            act(sl[:, 0:N], gc[:, 0:N], AF.Silu, (tgc, t_const), (tsl,), bias=fb_dw_c[:, l, c:c + 1])
            tt("dve", sc22[:, c, 0:N], sl[:, 0:N], pu, ALU.mult, (tsl, tpu), (t_sc22[c],))
        linear_tm(w_down_tm[l], D, FC, sc22, lambda kc: t_sc22[kc], [(0, N)], resid_add(xsb), cw=CWD)
        for c0 in range(0, FC, 4):
            pb, tpb = nbank()
            nn = min(4, FC - c0)
            for cc in range(nn):
                P.op("pe", lambda e, c=c0 + cc, cc=cc, pb=pb: e.transpose(
                    pb[0:RS, cc * 128:(cc + 1) * 128], gnew[l][:, c, :], ident_f), (t_gnew[l], t_const), (tpb,))
            tcopy("dve", stg2[0:RS, c0 * 128:(c0 + nn) * 128], pb[0:RS, 0:nn * 128], (tpb,), (t_stg2,))
        dma("sp", sf_s[l], stg2[0:RS, :], (t_stg2,), ())

    ffn_s(0)
    P.barrier()

    rms_to_hT(xsb)
    QTs = R4.alloc("QTs", [128, KC, N], BF16)
    KTn = R4.alloc("KTn", [128, KC, N], BF16)
    t_QTs, t_KTn = Trk(), Trk()

    def consq_s(i, oc, pb, tpb):
        if oc < KC:
            act(QTs[:, oc, :], pb, AF.Copy, (tpb,), (t_QTs,), scale=SCALE)
        else:
            tcopy("dve", KTn[:, oc - KC, :], pb, (tpb,), (t_KTn,))
    linear_fm(w_qk_fm, list(range(2 * KC)), N, consq_s)

    def conskv_s(c0, cw, b, n, pb, tpb):
        tcopy("dve", kvs[:n, c0:c0 + cw], pb, (tpb,), (t_kvs,))
    linear_tm(w_kv_tm, 2 * D, KC, hT, lambda kc: t_hT, [(0, N)], conskv_s)
    dma("sp", k_s[:, :], kvs[0:N, 0:D], (t_kvs,), ())
    dma("sp", v_s[:, :], kvs[0:N, D:2 * D], (t_kvs,), ())
    Vn = R4.alloc("Vn", [128, D], BF16)
    t_Vn = Trk()
    tcopy("pool", Vn[0:N, :], kvs[0:N, D:2 * D], (t_kvs,), (t_Vn,))
    Qblk = R4.alloc("Qblk", [128, NSEQ, KC, 2 * TQ], BF16)
    t_Qblk = Trk()
    memset("pool", Qblk, 0.0, (t_Qblk,))
    for pr in range(KC):
        src = QTs[:, pr, :].rearrange("p (s i) -> p s i", i=TQ)
        tcopy("pool", Qblk[0:64, :, pr, 0:TQ], src[0:64], (t_QTs, t_Qblk), (t_Qblk,))
        tcopy("pool", Qblk[64:128, :, pr, TQ:2 * TQ], src[64:128], (t_QTs, t_Qblk), (t_Qblk,))
    NG = NSEQ // G
    GC = G * HC
    brow_f = R4.alloc("brow_f", [1, GC], F32)
    brow = R4.alloc("brow", [1, GC], BF16)
    mnew_f = R4.alloc("mnew_f", [N, NG, GC], F32)
    mnew = R4.alloc("mnew", [N, NG, GC], BF16)
    t_brow = Trk()
    for i in range(TQ):
        dst = brow_f[0:1, :].rearrange("o (s h i) -> o s h i", h=H, i=TQ)[:, :, :, i]
        for s in range(G):
            tcopy("dve", dst[:, s, :], bh_bc[0:1, :], (t_const,), (t_brow,))
    tcopy("dve", brow, brow_f, (t_brow,), (t_brow,))
    dma("sp", mnew_f, mask_new, (), (t_brow,))
    tcopy("dve", mnew, mnew_f, (t_brow,), (t_brow,))
    NI = NSEQ * NPG
    pt_b = R4.alloc("pt_b", [128, NI], I32)
    iot = R4.alloc("iot", [128, NI], I32)
    c128 = R4.alloc("c128", [128, NI], I32)
    idx = R4.alloc("idx", [128, NI], I32)
    t_idx = Trk()
    dma("sp", pt_b, ptab.partition_broadcast(128), (), (t_idx,))
    P.op("pool", lambda e: e.iota(iot, [[0, NI]], base=0, channel_multiplier=1), (), (t_idx,))
    P.op("pool", lambda e: e.iota(c128, [[0, NI]], base=PAGE, channel_multiplier=0), (), (t_idx,))
    tt("pool", idx, pt_b, c128, ALU.mult, (t_idx,), (t_idx,))
    tt("pool", idx, idx, iot, ALU.add, (t_idx,), (t_idx,))
    e_s = Ring([R4.alloc("e_s%d" % i, [128, 512], F32) for i in range(2)])
    sp_s = Ring([R4.alloc("sp_s%d" % i, [128, 512], BF16) for i in range(2)])
    S_s = Ring([R4.alloc("S_s%d" % i, [128, 512], BF16) for i in range(2)])
    W_s = Ring([R4.alloc("W_s%d" % i, [128, 512], BF16) for i in range(2)])
    attTs = R4.alloc("attTs", [128, KC, N], BF16)
    t_attTs = Trk()
    P.barrier()
    R123 = Region(nc, R1.start, R1_SZ + R2_SZ + R3_SZ, "r123")
    kpg_ring = Ring([R123.alloc("kpg%d" % i, [128, D], BF16) for i in range(6)])
    ktp_ring = Ring([R123.alloc("ktp%d" % i, [128, KC, 128], BF16) for i in range(G + 3)])
    vpg_ring = Ring([R123.alloc("vpg%d" % i, [128, D], BF16) for i in range(2 * G)])

    for gi in range(NG):
        s0 = gi * G
        ng = G
        NCOL = ng * HC
        po, tpo = nbank(6, 8)
        mm(po[:, 0:ng * KC * 2 * TQ], zero_row[0:1, 0:128], zero_row[0:1, 0:ng * KC * 2 * TQ], True, False,
           (t_const,), (tpo,))
        S_prev = None
        for blk in range(NPG, -1, -1):
            newb = blk == NPG
            nk = N if newb else 128
            kts, vps = [], []
            if not newb:
                for si in range(ng):
                    s = s0 + si
                    col = s * NPG + blk
                    kp_, tkp = kpg_ring.next()
                    P.dma("pool", lambda e, kp_=kp_, col=col: e.indirect_dma_start(
                        out=kp_, out_offset=None, in_=cache_k,
                        in_offset=bass.IndirectOffsetOnAxis(ap=idx[:, col:col + 1], axis=0)), (t_idx,), (tkp,))
                    vp_, tvp = vpg_ring.next()
                    P.dma("pool", lambda e, vp_=vp_, col=col: e.indirect_dma_start(
                        out=vp_, out_offset=None, in_=cache_v,
                        in_offset=bass.IndirectOffsetOnAxis(ap=idx[:, col:col + 1], axis=0)), (t_idx,), (tvp,))
                    pbT, tpT = nbank(4, 6)
                    pbv = pbT.bitcast(BF16)
                    for c in range(KC):
                        P.op("pe", lambda e, c=c, pbv=pbv, kp_=kp_: e.transpose(
                            pbv[:, c * 128:(c + 1) * 128], kp_[:, c * 128:(c + 1) * 128], ident_b),
                            (tkp, t_const), (tpT,))
                    kt_, tkt = ktp_ring.next()
                    tcopy("dve", kt_, pbv[:, 0:KC * 128].rearrange("p (k n) -> p k n", k=KC), (tpT,), (tkt,))
                    kts.append((kt_, tkt))
                    vps.append((vp_, tvp))
            pz, tpz = nbank(0, 4)
            mm(pz[0:nk, 0:NCOL], ones_row[0:1, 0:nk], brow[0:1, 0:NCOL], True, False, (t_const, t_brow), (tpz,))
            for si in range(ng):
                s = s0 + si
                for pr in range(KC):
                    cols = slice(si * HC + pr * 2 * TQ, si * HC + (pr + 1) * 2 * TQ)
                    if newb:
                        lhsT = KTn[:, pr, 0:N]
                        rd = (t_KTn, t_Qblk)
                    else:
                        lhsT = kts[si][0][:, pr, :]
                        rd = (kts[si][1], t_Qblk)
                    mm(pz[0:nk, cols], lhsT, Qblk[:, s, pr, :], False, False, rd, (tpz,))
            if newb:
                mm(pz[0:nk, 0:NCOL], ident_b[0:N, 0:N], mnew[0:N, gi, 0:NCOL], False, False, (t_const, t_brow), (tpz,))
            e_t, te = e_s.next()
            act(e_t[0:nk, 0:NCOL], pz[0:nk, 0:NCOL], AF.Exp, (tpz,), (te,))
            sp_t, tsp = sp_s.next()
            act(sp_t[0:nk, 0:NCOL], e_t[0:nk, 0:NCOL], AF.Ln, (te,), (tsp,), bias=1.0)
            mm(pz[0:nk, 0:NCOL], negtri[0:nk, 0:nk], sp_t[0:nk, 0:NCOL], False, S_prev is None, (t_const, tsp), (tpz,))
            if S_prev is not None:
                Sa, tSa = S_prev
                mm(pz[0:nk, 0:NCOL], negones[:, 0:nk], Sa[:, 0:NCOL], False, True, (t_const, tSa), (tpz,))
            W_t, tW = W_s.next()
            act(W_t[0:nk, 0:NCOL], pz[0:nk, 0:NCOL], AF.Exp, (tpz,), (tW,))
            if blk > 0:
                Sn, tSn = S_s.next()
                if S_prev is None:
                    memset("pool", Sn[:, 0:NCOL], 0.0, (tSn,))
                    tcopy("pool", Sn[0:nk, 0:NCOL], sp_t[0:nk, 0:NCOL], (tsp, tSn), (tSn,))
                else:
                    Sa, tSa = S_prev
                    tt("pool", Sn[:, 0:NCOL], Sa[:, 0:NCOL], sp_t[:, 0:NCOL], ALU.add, (tSa, tsp), (tSn,))
                S_prev = (Sn, tSn)
            for si in range(ng):
                s = s0 + si
                for pr in range(KC):
                    wc = slice(si * HC + pr * 2 * TQ, si * HC + (pr + 1) * 2 * TQ)
                    oc_ = slice((si * KC + pr) * 2 * TQ, (si * KC + pr + 1) * 2 * TQ)
                    if newb:
                        lhsT = Vn[0:N, pr * 128:(pr + 1) * 128]
                        rd = (t_Vn, tW)
                    else:
                        lhsT = vps[si][0][:, pr * 128:(pr + 1) * 128]
                        rd = (vps[si][1], tW)
                    mm(po[:, oc_], lhsT, W_t[0:nk, wc], False, blk == 0 and si == ng - 1 and pr == KC - 1, rd, (tpo,))
        for si in range(ng):
            s = s0 + si
            pov = po[:, si * KC * 2 * TQ:(si + 1) * KC * 2 * TQ].rearrange("p (k j) -> p k j", j=2 * TQ)
            tcopy("dve", attTs[0:64, :, s * TQ:(s + 1) * TQ], pov[0:64, :, 0:TQ], (tpo,), (t_attTs,))
            tcopy("dve", attTs[64:128, :, s * TQ:(s + 1) * TQ], pov[64:128, :, TQ:2 * TQ], (tpo,), (t_attTs,))
    P.barrier()
    linear_tm(w_o_tm, D, KC, attTs, lambda kc: t_attTs, [(0, N)], resid_add(xsb))
    ffn_s(1)
    P.barrier()
    final_out(xsb, lambda b, n: y_s[0:n, :])

    P.barrier()
    P.emit()
    return nc


_CACHE = {}


def _consts():
    c = np.zeros((128, 5 * 128), np.float32)
    k = np.arange(128)[:, None]
    q = np.arange(128)[None, :]
    c[:, 0:128] = np.eye(128, dtype=np.float32)
    c[:, 128:256] = -(k >= q).astype(np.float32)
    c[:, 256:384] = -1.0
    c[:, 384:512] = np.where(k >= q, NEG, 0.0)
    c[:, 512:640] = 1.0
    return c


def _mask_new(nseq, h):
    hc = h * TQ
    g = max(1, min(nseq, 512 // hc))
    ng = nseq // g
    m = np.full((nseq * TQ, ng, g * hc), NEG, np.float32)
    for s in range(nseq):
        gi, sl = s // g, s % g
        for j in range(TQ):
            for i in range(TQ):
                if j < i:
                    m[s * TQ + j, gi, sl * hc + i:(sl + 1) * hc:TQ] = 0.0
    return m


def kernel(x_prompt, x_sample, state_conv, state_ffn, cache_k, cache_v, page_table,
           norm_mix, norm_ffn, norm_final, conv_w_in, conv_b_in, conv_w_dw, conv_b_dw, conv_ln_g,
           conv_ln_b, conv_w_out, conv_b_out, attn_w_qkv, attn_w_o, attn_b_logit, ffn_w_gate, ffn_w_up,
           ffn_w_dw, ffn_b_dw, ffn_w_down):
    f32 = np.float32
    x_prompt = np.asarray(x_prompt, f32)
    B, SEQ, D = x_prompt.shape
    x_sample = np.asarray(x_sample, f32)
    DB = x_sample.shape[0]
    n_cores = 8
    assert B * 2 == n_cores and DB % n_cores == 0
    NSEQ = DB // n_cores
    page_table = np.asarray(page_table, np.int32)
    NPG = page_table.shape[1]
    cache_k = np.asarray(cache_k, f32)
    cache_v = np.asarray(cache_v, f32)
    NPOOL = cache_k.shape[1]
    H = np.asarray(attn_b_logit).shape[1]
    DFF = np.asarray(ffn_w_gate).shape[2]
    cfg = dict(D=D, DFF=DFF, H=H, WIN=SEQ, NSEQ=NSEQ, NPG=NPG, NPOOL=NPOOL)
    key = tuple(sorted(cfg.items()))
    if key not in _CACHE:
        _CACHE[key] = build(cfg)
    nc = _CACHE[key]
    NB = SEQ // 128
    half = SEQ // 2
    state_conv = np.asarray(state_conv, f32)
    state_ffn = np.asarray(state_ffn, f32)
    ck = np.ascontiguousarray(cache_k[0].reshape(NPOOL * PAGE, D))
    cv = np.ascontiguousarray(cache_v[0].reshape(NPOOL * PAGE, D))
    pvec_d = np.concatenate([
        np.asarray(conv_b_in, f32).reshape(2, D), np.asarray(conv_b_dw, f32).reshape(1, D),
        np.asarray(conv_ln_g, f32).reshape(1, D), np.asarray(conv_ln_b, f32).reshape(1, D),
        np.asarray(conv_w_dw, f32).reshape(CONV_W, D), np.asarray(norm_mix, f32)[0:1], np.asarray(norm_ffn, f32)[0:1],
        np.asarray(norm_mix, f32)[1:2], np.asarray(norm_ffn, f32)[1:2]], axis=0)
    pvec_f = np.concatenate([np.asarray(ffn_b_dw, f32).reshape(2, DFF),
                             np.asarray(ffn_w_dw, f32).reshape(2 * FFN_W, DFF)], axis=0)
    shared = dict(
        pvec_d=np.ascontiguousarray(pvec_d), pvec_f=np.ascontiguousarray(pvec_f),
        cache_k=ck, cache_v=cv, consts=_consts(), mask_new=_mask_new(NSEQ, H),
        norm_mix=np.asarray(norm_mix, f32), norm_ffn=np.asarray(norm_ffn, f32),
        norm_final=np.asarray(norm_final, f32).reshape(1, D),
        conv_w_in=np.asarray(conv_w_in, f32)[0], conv_b_in=np.asarray(conv_b_in, f32),
        conv_w_dw=np.asarray(conv_w_dw, f32)[0], conv_b_dw=np.asarray(conv_b_dw, f32),
        conv_ln_g=np.asarray(conv_ln_g, f32), conv_ln_b=np.asarray(conv_ln_b, f32),
        conv_w_out=np.asarray(conv_w_out, f32)[0], conv_b_out=np.asarray(conv_b_out, f32),
        attn_w_qkv=np.asarray(attn_w_qkv, f32)[0], attn_w_o=np.asarray(attn_w_o, f32)[0],
        attn_b=np.asarray(attn_b_logit, f32),
        ffn_w_gate=np.asarray(ffn_w_gate, f32), ffn_w_up=np.asarray(ffn_w_up, f32),
        ffn_w_dw=np.asarray(ffn_w_dw, f32), ffn_b_dw=np.asarray(ffn_b_dw, f32).reshape(2, 1, DFF),
        ffn_w_down=np.asarray(ffn_w_down, f32),
    )
    in_maps = []
    for c in range(n_cores):
        b, r = c // 2, c % 2
        if r == 1:
            xwin = x_prompt[b]
            nullb = np.zeros((128, NB), f32)
            valid = np.ones((1, 512), f32)
        else:
            xwin = np.concatenate([np.zeros((half, D), f32), x_prompt[b, :half]], axis=0)
            nullb = np.zeros((128, NB), f32)
            nullb[:, :NB // 2] = NEG
            valid = np.zeros((1, 512), f32)
        sl = slice(c * NSEQ, (c + 1) * NSEQ)
        m = dict(shared)
        m.update(
            xwin=np.ascontiguousarray(xwin), nullb=nullb, valid3=valid,
            x_s=np.ascontiguousarray(x_sample[sl].reshape(NSEQ * TQ, D)),
            st_conv=np.ascontiguousarray(state_conv[0, sl].reshape(NSEQ * 30, D)),
            st_ffn=np.ascontiguousarray(state_ffn[:, sl].reshape(2, NSEQ * 2, DFF)),
            ptab=np.ascontiguousarray(page_table[sl].reshape(1, NSEQ * NPG)),
        )
        in_maps.append(m)
    res = run_bass_kernel_spmd(nc, in_maps, core_ids=list(range(n_cores))).results

    y_p = np.zeros((B, SEQ, D), f32)
    k_p = np.zeros((1, B, SEQ, H, DH), f32)
    v_p = np.zeros((1, B, SEQ, H, DH), f32)
    sc_p = np.zeros((1, B, 30, D), f32)
    sf_p = np.zeros((2, B, 2, DFF), f32)
    y_s = np.zeros((DB, TQ, D), f32)
    k_s = np.zeros((1, DB, TQ, H, DH), f32)
    v_s = np.zeros((1, DB, TQ, H, DH), f32)
    sc_s = np.zeros((1, DB, 30, D), f32)
    sf_s = np.zeros((2, DB, 2, DFF), f32)
    for c in range(n_cores):
        b, r = c // 2, c % 2
        o = res[c]
        rows = slice(r * half, (r + 1) * half)
        y_p[b, rows] = o["y_p"]
        k_p[0, b, rows] = o["k_p"].reshape(half, H, DH)
        v_p[0, b, rows] = o["v_p"].reshape(half, H, DH)
        if r == 1:
            sc_p[0, b] = o["sc_p"]
            sf_p[:, b] = o["sf_p"]
        sl = slice(c * NSEQ, (c + 1) * NSEQ)
        y_s[sl] = o["y_s"].reshape(NSEQ, TQ, D)
        k_s[0, sl] = o["k_s"].reshape(NSEQ, TQ, H, DH)
        v_s[0, sl] = o["v_s"].reshape(NSEQ, TQ, H, DH)
        sc_s[0, sl] = o["sc_s"]
        sf_s[:, sl] = o["sf_s"].reshape(2, NSEQ, 2, DFF)
    return (y_p, y_s, sc_p, sf_p, k_p, v_p, sc_s, sf_s, k_s, v_s)
```

```python
import numpy as np
import ml_dtypes
import concourse.bass as bass
import concourse.mybir as mybir
from concourse.bass_utils import run_bass_kernel_spmd

F32 = mybir.dt.float32
BF16 = mybir.dt.bfloat16
I32 = mybir.dt.int32
AF = mybir.ActivationFunctionType
ALU = mybir.AluOpType
NEG = -30000.0
EPS = 1e-6
CONV_W = 31
FFN_W = 3
DH = 64
TQ = 4
PAGE = 128


class Trk:
    __slots__ = ("w", "r")

    def __init__(self):
        self.w = None
        self.r = {}


class Prog:
    ENG = ("pe", "act", "dve", "pool", "sp")

    def __init__(self, nc, ndma=24):
        self.nc = nc
        self.h = {"pe": nc.tensor, "act": nc.scalar, "dve": nc.vector, "pool": nc.gpsimd, "sp": nc.sync}
        self.sem = {e: nc.alloc_semaphore("s_" + e) for e in self.ENG}
        self.ndma = ndma
        for k in range(ndma):
            self.sem["d%d" % k] = nc.alloc_semaphore("s_d%d" % k)
        self.cnt = {e: 0 for e in self.ENG}
        self.dcnt = [0] * ndma
        self.dnext = 0
        self.lists = {e: [] for e in self.ENG}
        self.waited = {e: {} for e in self.ENG}

    def _deps(self, eng, reads, writes):
        need = {}
        for t in reads:
            if t.w is not None and need.get(t.w[0], 0) < t.w[1]:
                need[t.w[0]] = t.w[1]
        for t in writes:
            if t.w is not None and need.get(t.w[0], 0) < t.w[1]:
                need[t.w[0]] = t.w[1]
            for s, v in t.r.items():
                if need.get(s, 0) < v:
                    need[s] = v
        if eng == "pe":
            need.pop("pe", None)
        w = self.waited[eng]
        out = []
        for s, v in need.items():
            if w.get(s, 0) < v:
                w[s] = v
                out.append((s, v))
        return out

    def _mark(self, tok, reads, writes):
        s, v = tok
        for t in reads:
            if t.r.get(s, 0) < v:
                t.r[s] = v
        for t in writes:
            t.w = tok
            t.r = {}

    def op(self, eng, fn, reads=(), writes=()):
        waits = self._deps(eng, reads, writes)
        self.cnt[eng] += 1
        self.lists[eng].append((waits, fn, eng, 1))
        self._mark((eng, self.cnt[eng]), reads, writes)

    def dma(self, eng, fn, reads=(), writes=()):
        waits = self._deps(eng, reads, writes)
        k = self.dnext
        self.dnext = (k + 1) % self.ndma
        s = "d%d" % k
        prev = self.dcnt[k]
        if prev and self.waited[eng].get(s, 0) < prev:
            self.waited[eng][s] = prev
            waits.append((s, prev))
        self.dcnt[k] += 16
        self.lists[eng].append((waits, fn, s, 16))
        self._mark((s, self.dcnt[k]), reads, writes)

    def barrier(self):
        for e in self.ENG:
            waits = []
            w = self.waited[e]
            for f in self.ENG:
                if f != e and self.cnt[f] > w.get(f, 0):
                    w[f] = self.cnt[f]
                    waits.append((f, self.cnt[f]))
            if self.cnt[e] > w.get(e, 0):
                w[e] = self.cnt[e]
                waits.append((e, self.cnt[e]))
            for k in range(self.ndma):
                s = "d%d" % k
                if self.dcnt[k] > w.get(s, 0):
                    w[s] = self.dcnt[k]
                    waits.append((s, self.dcnt[k]))
            if waits:
                self.lists[e].append((waits, None, None, 0))

    def emit(self):
        nc = self.nc
        sem = self.sem
        lists = self.lists

        def run(e, lst):
            for waits, fn, s, inc in lst:
                for ws, wv in waits:
                    e.wait_ge(sem[ws], wv)
                if fn is not None:
                    fn(e).then_inc(sem[s], inc)

        with nc.Block() as block:
            @block.tensor
            def _(e):
                run(e, lists["pe"])

            @block.scalar
            def _(e):
                run(e, lists["act"])

            @block.vector
            def _(e):
                run(e, lists["dve"])

            @block.gpsimd
            def _(e):
                run(e, lists["pool"])

            @block.sync
            def _(e):
                run(e, lists["sp"])


class Ring:
    def __init__(self, aps):
        self.items = [(a, Trk()) for a in aps]
        self.i = 0

    def next(self):
        it = self.items[self.i]
        self.i = (self.i + 1) % len(self.items)
        return it


class Region:
    def __init__(self, nc, start, size, name):
        self.nc, self.start, self.size, self.name = nc, start, size, name
        self.cur = start
        self.n = 0

    def reset(self):
        self.cur = self.start

    def alloc(self, name, shape, dt):
        esz = 4 if dt in (F32, I32) else 2
        n = 1
        for s in shape[1:]:
            n *= s
        nbytes = (n * esz + 63) // 64 * 64
        off = self.cur
        self.cur += nbytes
        assert self.cur <= self.start + self.size, (self.name, name, self.cur - self.start, self.size)
        self.n += 1
        return self.nc.alloc_sbuf_tensor_at("%s_%s_%d" % (self.name, name, self.n), list(shape), dt, offset=off).ap()


def build(cfg):
    D, DFF, H = cfg["D"], cfg["DFF"], cfg["H"]
    WIN, NSEQ, NPG, NPOOL = cfg["WIN"], cfg["NSEQ"], cfg["NPG"], cfg["NPOOL"]
    KC, FC = D // 128, DFF // 128
    assert D == H * DH and KC * 2 == H
    TS = 512
    NT = WIN // TS
    NB = WIN // 128
    OWN_T0 = NT // 2
    NOWN = WIN // 2
    NTS = NSEQ * TQ
    CW = 512
    CWD = 256
    SCALE = DH ** -0.5
    HC = H * TQ
    G = max(1, min(NSEQ, 512 // HC))

    nc = bass.Bass("TRN2", target_bir_lowering=False)
    P = Prog(nc)

    def din(name, shape, dt=F32):
        return nc.dram_tensor(name, list(shape), dt, kind="ExternalInput").ap()

    def dout(name, shape, dt=F32):
        return nc.dram_tensor(name, list(shape), dt, kind="ExternalOutput").ap()

    def dscr(name, shape, dt=BF16):
        return nc.dram_tensor(name, list(shape), dt, kind="Internal").ap()

    xwin = din("xwin", [WIN, D])
    nullb = din("nullb", [128, NB])
    valid3 = din("valid3", [1, TS])
    x_s = din("x_s", [NTS, D])
    st_conv = din("st_conv", [NSEQ * 30, D])
    st_ffn = din("st_ffn", [2, NSEQ * 2, DFF])
    cache_k = din("cache_k", [NPOOL * PAGE, D])
    cache_v = din("cache_v", [NPOOL * PAGE, D])
    ptab = din("ptab", [1, NSEQ * NPG], I32)
    consts = din("consts", [128, 5 * 128])
    mask_new = din("mask_new", [NTS, NSEQ // G, G * HC])
    pvec_d = din("pvec_d", [9 + CONV_W, D])
    pvec_f = din("pvec_f", [8, DFF])
    norm_mix = din("norm_mix", [2, D])
    norm_ffn = din("norm_ffn", [2, D])
    norm_final = din("norm_final", [1, D])
    conv_w_in = din("conv_w_in", [D, 2 * D])
    conv_b_in = din("conv_b_in", [1, 2 * D])
    conv_w_dw = din("conv_w_dw", [CONV_W, D])
    conv_b_dw = din("conv_b_dw", [1, D])
    conv_ln_g = din("conv_ln_g", [1, D])
    conv_ln_b = din("conv_ln_b", [1, D])
    conv_w_out = din("conv_w_out", [D, D])
    conv_b_out = din("conv_b_out", [1, D])
    attn_w_qkv = din("attn_w_qkv", [D, 3 * D])
    attn_w_o = din("attn_w_o", [D, D])
    attn_b = din("attn_b", [1, H])
    ffn_w_gate = din("ffn_w_gate", [2, D, DFF])
    ffn_w_up = din("ffn_w_up", [2, D, DFF])
    ffn_w_dw = din("ffn_w_dw", [2, FFN_W, DFF])
    ffn_b_dw = din("ffn_b_dw", [2, 1, DFF])
    ffn_w_down = din("ffn_w_down", [2, DFF, D])

    y_p = dout("y_p", [NOWN, D])
    k_p = dout("k_p", [NOWN, D])
    v_p = dout("v_p", [NOWN, D])
    sc_p = dout("sc_p", [30, D])
    sf_p = dout("sf_p", [2, 2, DFF])
    y_s = dout("y_s", [NTS, D])
    k_s = dout("k_s", [NTS, D])
    v_s = dout("v_s", [NTS, D])
    sc_s = dout("sc_s", [NSEQ, 30, D])
    sf_s = dout("sf_s", [2, NSEQ * 2, DFF])

    w_in_fm = dscr("w_in_fm", [2 * KC, 128, KC * 128])
    w_out_tm = dscr("w_out_tm", [D // CW, 128, KC * CW])
    w_gate_fm = [dscr("w_gate_fm%d" % l, [FC, 128, KC * 128]) for l in range(2)]
    w_up_fm = [dscr("w_up_fm%d" % l, [FC, 128, KC * 128]) for l in range(2)]
    w_down_tm = [dscr("w_down_tm%d" % l, [D // CWD, 128, FC * CWD]) for l in range(2)]
    w_qk_fm = dscr("w_qk_fm", [2 * KC, 128, KC * 128])
    w_kv_tm = dscr("w_kv_tm", [2 * D // CW, 128, KC * CW])
    w_o_tm = dscr("w_o_tm", [D // CW, 128, KC * CW])
    kt_scr = dscr("kt_scr", [KC, 128, WIN])
    v_scr = dscr("v_scr", [WIN, D])
    trk_kt = Trk()
    trk_v = Trk()
    trk_w = Trk()

    base = (nc.sbuf_base + 63) // 64 * 64
    top = nc.sbuf_top
    PR = Region(nc, base, top - base, "pers")
    sb = PR.alloc

    cst_f = sb("cst_f", [128, 5 * 128], F32)
    ident_f = cst_f[:, 0:128]
    ident_b = sb("ident_b", [128, 128], BF16)
    negtri = sb("negtri", [128, 128], BF16)
    negones = sb("negones", [128, 128], BF16)
    maskneg = sb("maskneg", [128, 128], BF16)
    posones = sb("posones", [128, 128], BF16)
    ones_row = sb("ones_row", [1, 512], BF16)
    zero_row = sb("zero_row", [1, 512], BF16)
    b_in_c = sb("b_in_c", [128, 2 * KC], F32)
    w_dw_c = sb("w_dw_c", [128, KC, CONV_W], F32)
    b_dw_c = sb("b_dw_c", [128, KC], F32)
    ln_g_c = sb("ln_g_c", [128, KC], F32)
    ln_b_c = sb("ln_b_c", [128, KC], F32)
    fw_dw_c = sb("fw_dw_c", [128, 2, FC, FFN_W], F32)
    fb_dw_c = sb("fb_dw_c", [128, 2, FC], F32)
    gain_c = sb("gain_c", [128, 4, KC], F32)
    b_out_r = sb("b_out_r", [1, D], BF16)
    b_out_f = sb("b_out_f", [1, D], F32)
    gfin = sb("gfin", [128, D], F32)
    nullb_c = sb("nullb_c", [128, NB], F32)
    bh_bc = sb("bh_bc", [128, H], F32)
    kbias = sb("kbias", [128, NB, H], F32)
    cvec = sb("cvec", [128, KC], F32)
    valid_c = sb("valid_c", [128, TS], F32)
    hist_g = [sb("hist_g%d" % l, [128, FC, 2], F32) for l in range(2)]
    hist_glu = sb("hist_glu", [128, KC, 30], F32)
    t_hglu = Trk()
    epsc = sb("epsc", [128, 1], F32)
    stat = sb("stat", [128, 16], F32)
    lnv = sb("lnv", [128, 16], F32)
    rstd = sb("rstd", [128, 16], F32)
    t_const = Trk()
    t_stat = Trk()
    t_statb = [Trk() for _ in range(8)]
    t_hist = [Trk(), Trk()]

    xt = sb("xt", [128, 4, D], F32)
    t_xt = [Trk() for _ in range(4)]
    xhalo = sb("xhalo", [128, D], F32)
    t_xhalo = Trk()
    hT = sb("hT", [128, KC, TS], BF16)
    t_hT = Trk()
    xn_ring_aps = [sb("xn%d" % i, [128, D], BF16) for i in range(2)]
    junk = sb("junk", [128, D], BF16)
    t_junk = Trk()
    xn_ring = Ring(xn_ring_aps)
    wfm_ring = Ring([sb("wfm%d" % i, [128, KC, 128], BF16) for i in range(5)])
    wtm_ring = Ring([sb("wtm%d" % i, [128, max(FC * CWD, KC * CW)], BF16) for i in range(2)])

    R1_SZ = 36 * 1024
    R2_SZ = FC * TS * 2
    R3_SZ = 13 * 1024
    r1s = PR.cur
    R1 = Region(nc, r1s, R1_SZ, "r1")
    R2 = Region(nc, r1s + R1_SZ, R2_SZ, "r2")
    R12 = Region(nc, r1s, R1_SZ + R2_SZ, "r12")
    R3 = Region(nc, r1s + R1_SZ + R2_SZ, R3_SZ, "r3")
    r4s = r1s + R1_SZ + R2_SZ + R3_SZ
    assert r4s < top, (r4s, top)
    R4 = Region(nc, r4s, top - r4s, "r4")

    EXTW = 30 + TS
    EW = 30 + TQ
    EXTA = max(EXTW, NSEQ * EW)
    ext_full = R1.alloc("ext", [128, KC, EXTA], F32)
    ext = ext_full[:, :, 0:EXTW]
    t_ext = [Trk() for _ in range(KC)]
    yacc = R1.alloc("yacc", [128, KC, TS], F32)
    t_yacc = [Trk() for _ in range(KC)]
    sc22 = R2.alloc("sc22", [128, FC, TS], BF16)
    t_sc22 = [Trk() for _ in range(FC)]
    mean_t = R3.alloc("mean_t", [128, TS], F32)
    rstd_t = R3.alloc("rstd_t", [128, TS], F32)
    tmpa_ring = Ring([R3.alloc("tmp_a%d" % i, [128, TS], F32) for i in range(4)])
    t_mean = Trk()
    t_rstd = Trk()
    R3.reset()
    extg_ring = Ring([R3.alloc("extg%d" % i, [128, 2 + TS], F32) for i in range(2)])
    gc_ring = Ring([R3.alloc("gc%d" % i, [128, TS], F32) for i in range(2)])
    sil_ring = Ring([R3.alloc("sil%d" % i, [128, TS], F32) for i in range(2)])
    R2.reset()
    QT = R2.alloc("QT", [128, KC, TS], BF16)
    t_QT = Trk()
    attT = R2.alloc("attT", [128, KC, TS], BF16)
    t_attT = [Trk() for _ in range(KC)]
    R1.reset()
    R3.reset()
    KTt = R3.alloc("KTt", [128, KC, TS], BF16)
    t_KTt = Trk()
    kvtm_ring = Ring([R1.alloc("kvtm%d" % i, [128, 2 * D], F32) for i in range(4)])
    vbf_ring = Ring([R3.alloc("vbf%d" % i, [128, D], BF16) for i in range(2)])
    R1.reset()
    Kp_ring = Ring([R1.alloc("Kp%d" % i, [128, WIN], BF16) for i in range(2)])
    Vp_ring = Ring([R1.alloc("Vp%d" % i, [128, NB, 128], BF16) for i in range(2)])
    NPEC = KC // 2
    dg_bytes = NPEC * CONV_W * 128 * 2
    eb_bytes = (NPEC * EXTW * 2 + 63) // 64 * 64
    RDG = Region(nc, (top - dg_bytes - eb_bytes) // 64 * 64, dg_bytes + eb_bytes, "rdg")
    DG = RDG.alloc("DG", [128, NPEC * CONV_W, 128], BF16)
    extb = RDG.alloc("extb", [128, NPEC, EXTW], BF16)
    t_extb = [Trk() for _ in range(NPEC)]
    R34 = Region(nc, R3.start, RDG.start - R3.start, "r34")
    e_ring = Ring([R34.alloc("e%d" % i, [128, 2, TS], F32) for i in range(2)])
    sp_ring = Ring([R34.alloc("sp%d" % i, [128, 2, TS], BF16) for i in range(3)])
    S_ring = Ring([R34.alloc("S%d" % i, [128, 2, TS], BF16) for i in range(2)])
    W_ring = Ring([R34.alloc("W%d" % i, [128, 2, TS], BF16) for i in range(3)])
    R1.reset()
    yout_ring = Ring([R1.alloc("yout%d" % i, [128, D], F32) for i in range(2)])
    R1.reset()
    R2.reset()
    stg_f = Ring([R1.alloc("stg_f%d" % i, [128, 3 * D], F32) for i in range(3)])
    stg_b = Ring([R2.alloc("stg_b%d" % i, [128, 3 * D], BF16) for i in range(3)])

    ps_all = nc.alloc_psum_tensor("ps_all", [128, 8 * 512], F32).ap()
    banks = [(ps_all[:, i * 512:(i + 1) * 512], Trk()) for i in range(8)]
    bank_i = {}
    pzpair_ring = Ring([ps_all[:, 2 * i * 512:(2 * i + 2) * 512].rearrange("p (b n) -> p b n", b=2) for i in range(3)])

    def nbank(lo=0, hi=8):
        i = bank_i.get((lo, hi), lo)
        bank_i[(lo, hi)] = i + 1 if i + 1 < hi else lo
        return banks[i]

    def mm(out, lhsT, rhs, start, stop, reads, writes):
        P.op("pe", lambda e: e.matmul(out, lhsT, rhs, start=start, stop=stop), reads, writes)

    def act(out, in_, func, reads, writes, bias=None, scale=None, accum=None):
        kw = {}
        if bias is not None:
            kw["bias"] = bias
        if scale is not None:
            kw["scale"] = scale
        if accum is not None:
            kw["accum_out"] = accum
        P.op("act", lambda e: e.activation(out, in_, func, **kw), reads, writes)

    def tcopy(eng, out, in_, reads, writes):
        if eng == "act":
            P.op(eng, lambda e: e.activation(out, in_, AF.Copy), reads, writes)
        else:
            P.op(eng, lambda e: e.tensor_copy(out, in_), reads, writes)

    def tt(eng, out, a, b, op, reads, writes):
        P.op(eng, lambda e: e.tensor_tensor(out, a, b, op), reads, writes)

    def ts(eng, out, a, s1, s2, op0, op1, reads, writes):
        if op1 is None:
            P.op(eng, lambda e: e.tensor_scalar(out, a, s1, None, op0), reads, writes)
        else:
            P.op(eng, lambda e: e.tensor_scalar(out, a, s1, s2, op0, op1), reads, writes)

    def stt(eng, out, a, s, b, op0, op1, reads, writes):
        P.op(eng, lambda e: e.scalar_tensor_tensor(out, a, s, b, op0, op1), reads, writes)

    def dma(eng, out, in_, reads, writes, **kw):
        P.dma(eng, lambda e: e.dma_start(out=out, in_=in_, **kw), reads, writes)

    def memset(eng, ap, val, writes):
        P.op(eng, lambda e: e.memset(ap, val), (), writes)

    dma("sp", cst_f, consts, (), (t_const,))
    tcopy("dve", ident_b, cst_f[:, 0:128], (t_const,), (t_const,))
    tcopy("dve", negtri, cst_f[:, 128:256], (t_const,), (t_const,))
    tcopy("dve", negones, cst_f[:, 256:384], (t_const,), (t_const,))
    tcopy("dve", maskneg, cst_f[:, 384:512], (t_const,), (t_const,))
    tcopy("dve", posones, cst_f[:, 512:640], (t_const,), (t_const,))
    memset("dve", ones_row, 1.0, (t_const,))
    memset("dve", zero_row, 0.0, (t_const,))
    memset("dve", epsc, EPS, (t_const,))

    NPD = 9 + CONV_W
    prow, t_prow = stg_f.items[0]
    prow2, t_prow2 = stg_f.items[1]
    dma("sp", prow[0:NPD, 0:D], pvec_d, (), (t_prow,))
    dma("sp", prow2[0:8, 0:DFF], pvec_f, (), (t_prow2,))
    for kc in range(KC):
        pb, tpb = nbank()
        P.op("pe", lambda e, kc=kc, pb=pb: e.transpose(pb[:, 0:NPD], prow[0:NPD, kc * 128:(kc + 1) * 128],
                                                      ident_f[0:NPD, 0:NPD]), (t_prow, t_const), (tpb,))
        tcopy("dve", b_in_c[:, kc:kc + 1], pb[:, 0:1], (tpb,), (t_const,))
        tcopy("dve", b_in_c[:, KC + kc:KC + kc + 1], pb[:, 1:2], (tpb,), (t_const,))
        tcopy("dve", b_dw_c[:, kc:kc + 1], pb[:, 2:3], (tpb,), (t_const,))
        tcopy("dve", ln_g_c[:, kc:kc + 1], pb[:, 3:4], (tpb,), (t_const,))
        tcopy("dve", ln_b_c[:, kc:kc + 1], pb[:, 4:5], (tpb,), (t_const,))
        tcopy("dve", w_dw_c[:, kc, :], pb[:, 5:5 + CONV_W], (tpb,), (t_const,))
        tcopy("dve", gain_c[:, :, kc], pb[:, 5 + CONV_W:9 + CONV_W], (tpb,), (t_const,))
    for c in range(FC):
        pb, tpb = nbank()
        P.op("pe", lambda e, c=c, pb=pb: e.transpose(pb[:, 0:8], prow2[0:8, c * 128:(c + 1) * 128],
                                                    ident_f[0:8, 0:8]), (t_prow2, t_const), (tpb,))
        tcopy("dve", fb_dw_c[:, :, c], pb[:, 0:2], (tpb,), (t_const,))
        tcopy("dve", fw_dw_c[:, :, c, :], pb[:, 2:8].rearrange("p (l j) -> p l j", j=FFN_W), (tpb,), (t_const,))
    dma("sp", b_out_f, conv_b_out, (), (t_const,))
    tcopy("dve", b_out_r, b_out_f, (t_const,), (t_const,))
    dma("sp", gfin, norm_final.partition_broadcast(128), (), (t_const,))
    dma("sp", nullb_c, nullb, (), (t_const,))
    dma("sp", bh_bc, attn_b.partition_broadcast(128), (), (t_const,))
    dma("sp", valid_c, valid3.partition_broadcast(128), (), (t_const,))
    bhp = bh_bc.rearrange("p (k t) -> p k t", t=2)
    tcopy("dve", cvec[0:64, :], bhp[0:64, :, 0], (t_const,), (t_const,))
    tcopy("dve", cvec[64:128, :], bhp[64:128, :, 1], (t_const,), (t_const,))
    act(cvec, cvec, AF.Exp, (t_const,), (t_const,))
    for kb in range(NB):
        ts("dve", kbias[:, kb, :], bh_bc, nullb_c[:, kb:kb + 1], None, ALU.add, None, (t_const,), (t_const,))
    for l in range(2):
        memset("dve", hist_g[l], 0.0, (t_hist[l],))
    for c in range(KC - NPEC, KC):
        for j in range(CONV_W):
            ts(("dve", "pool")[j % 2], DG[:, (c - (KC - NPEC)) * CONV_W + j, :], ident_f, w_dw_c[:, c, j:j + 1], None,
               ALU.mult, None, (t_const,), (t_const,))

    cast_rr = [0]

    def convert(src, kin, n, gain_idx, dests):
        kin_c = kin // 128
        for kc in range(kin_c):
            sf, tf = stg_f.next()
            sbf, tb = stg_b.next()
            dma("sp", sf[:, 0:n], src[kc * 128:(kc + 1) * 128, :], (), (tf,))
            if gain_idx is not None:
                eng = ("act", "act", "dve")[cast_rr[0] % 3]
            else:
                eng = ("dve", "act", "pool")[cast_rr[0] % 3]
            cast_rr[0] += 1
            if gain_idx is None:
                if eng == "act":
                    act(sbf[:, 0:n], sf[:, 0:n], AF.Copy, (tf,), (tb,))
                else:
                    tcopy(eng, sbf[:, 0:n], sf[:, 0:n], (tf,), (tb,))
            else:
                gsc = gain_c[:, gain_idx, kc:kc + 1]
                if eng == "act":
                    act(sbf[:, 0:n], sf[:, 0:n], AF.Copy, (tf, t_const), (tb,), scale=gsc)
                else:
                    ts(eng, sbf[:, 0:n], sf[:, 0:n], gsc, None, ALU.mult, None, (tf, t_const), (tb,))
            for kind, scr, c0, ncols in dests:
                w = 128 if kind == "fm" else kind
                dst = scr.rearrange("s p (k n) -> p s k n", k=kin_c)[:, :, kc, :]
                srcv = sbf[:, c0:c0 + ncols].rearrange("p (s n) -> p s n", n=w)
                dma("act", dst, srcv, (tb,), (trk_w,))

    convert(conv_w_in, D, 2 * D, 0, [("fm", w_in_fm, 0, 2 * D)])
    convert(conv_w_out, D, D, None, [(CW, w_out_tm, 0, D)])
    for l in range(2):
        convert(ffn_w_gate[l], D, DFF, 1 + 2 * l, [("fm", w_gate_fm[l], 0, DFF)])
        convert(ffn_w_up[l], D, DFF, 1 + 2 * l, [("fm", w_up_fm[l], 0, DFF)])
        convert(ffn_w_down[l], DFF, D, None, [(CWD, w_down_tm[l], 0, D)])
    convert(attn_w_qkv, D, 3 * D, 2, [("fm", w_qk_fm, 0, 2 * D), (CW, w_kv_tm, D, 2 * D)])
    convert(attn_w_o, D, D, None, [(CW, w_o_tm, 0, D)])
    P.barrier()

    CW_DEFAULT = CW
    def load_fm(scr, oc):
        w, tw = wfm_ring.next()
        dma("sp", w, scr[oc].rearrange("p (k n) -> p k n", k=KC), (trk_w,), (tw,))
        return w, tw

    def load_tm(scr, cs, kin_c, cw):
        w, tw = wtm_ring.next()
        wv = w[:, 0:kin_c * cw].rearrange("p (k n) -> p k n", k=kin_c)
        dma("sp", wv, scr[cs].rearrange("p (k n) -> p k n", k=kin_c), (trk_w,), (tw,))
        return wv, tw

    def row_stats(xa, tx, n, b):
        tsb = t_statb[b]
        act(junk[:n, :], xa, AF.Square, (tx,), (t_junk, tsb), accum=stat[:n, b:b + 1])
        act(lnv[:n, b:b + 1], stat[:n, b:b + 1], AF.Ln, (tsb, t_const), (tsb,), bias=epsc[:n, :], scale=1.0 / D)
        act(rstd[:n, b:b + 1], lnv[:n, b:b + 1], AF.Exp, (tsb,), (tsb,), scale=-0.5)

    def rms_to_hT(blocks):
        for b, (xa, tx, n) in enumerate(blocks):
            row_stats(xa, tx, n, b)
            xb, txn = xn_ring.next()
            ts("dve", xb[:n, :], xa, rstd[:n, b:b + 1], None, ALU.mult, None, (tx, t_statb[b]), (txn,))
            pb, tpb = nbank()
            pbv = pb.bitcast(BF16)
            for kc in range(KC):
                P.op("pe", lambda e, kc=kc, n=n, xb=xb, pbv=pbv: e.transpose(
                    pbv[:, kc * 128:kc * 128 + n], xb[:n, kc * 128:(kc + 1) * 128], ident_b[:n, :n]),
                    (txn, t_const), (tpb,))
            src = pbv[:, 0:KC * 128].rearrange("p (k n) -> p k n", k=KC)[:, :, 0:n]
            tcopy("dve", hT[:, :, b * 128:b * 128 + n], src, (tpb,), (t_hT,))

    def linear_fm(scr, oc_list, N, consume, rhs=None, trhs=None):
        rhs = hT if rhs is None else rhs
        trhs = t_hT if trhs is None else trhs
        for i, oc in enumerate(oc_list):
            w, tw = load_fm(scr, oc)
            pb, tpb = nbank()
            for kc in range(KC):
                mm(pb[:, 0:N], w[:, kc, :], rhs[:, kc, 0:N], kc == 0, kc == KC - 1, (tw, trhs), (tpb,))
            consume(i, oc, pb[:, 0:N], tpb)

    def linear_tm(scr, ncols, kin_c, lhs, tlhs_fn, blocks, consume, bias_row=None, cw=None):
        CW = cw or CW_DEFAULT
        for cs in range(ncols // CW):
            w, tw = load_tm(scr, cs, kin_c, CW)
            for b, n in blocks:
                pb, tpb = nbank()
                for kc in range(kin_c):
                    mm(pb[:n, 0:CW], lhs[:, kc, b * 128:b * 128 + n], w[:, kc, :], kc == 0,
                       kc == kin_c - 1 and bias_row is None, (tw, tlhs_fn(kc)), (tpb,))
                if bias_row is not None:
                    mm(pb[:n, 0:CW], ones_row[0:1, 0:n], bias_row[0:1, cs * CW:(cs + 1) * CW], False, True,
                       (t_const,), (tpb,))
                consume(cs * CW, CW, b, n, pb[:n, 0:CW], tpb)

    def resid_add(xblocks):
        def f(c0, cw, b, n, pb, tpb):
            xa, tx, _ = xblocks[b]
            tt("dve", xa[:, c0:c0 + cw], xa[:, c0:c0 + cw], pb, ALU.add, (tx, tpb), (tx,))
        return f

    def blist(xblocks):
        return [(b, n) for b, (_, _, n) in enumerate(xblocks)]

    def conv_module(xblocks, N, t_extv, glu_dst, tap_src, view, valid=None, use_pe=False):
        rms_to_hT(xblocks)
        for c in range(KC):
            res = {}

            def cons(i, oc, pb, tpb, res=res):
                res[i] = (pb, tpb)
            linear_fm(w_in_fm, [c, c + KC], N, cons)
            (pa, tpa), (pg, tpg) = res[0], res[1]
            sg, tsg = tmpa_ring.next()
            act(sg[:, 0:N], pg, AF.Sigmoid, (tpg, t_const), (tsg,), bias=b_in_c[:, KC + c:KC + c + 1])
            stt("dve", glu_dst(c), view(pa), b_in_c[:, c:c + 1], view(sg[:, 0:N]), ALU.add, ALU.mult,
                (tpa, tsg, t_const), (t_extv[c],))
            if valid is not None:
                tt("dve", glu_dst(c), glu_dst(c), view(valid[:, 0:N]), ALU.mult, (t_extv[c], t_const), (t_extv[c],))
        ndve = KC if not use_pe else KC - NPEC
        for c in range(ndve, KC):
            tcopy("act" if c % 2 == 0 else "pool", extb[:, c - ndve, :], ext[:, c, :], (t_extv[c],), (t_extb[c - ndve],))
        for j in range(CONV_W):
            for c in range(ndve):
                yv = view(yacc[:, c, 0:N])
                if j == 0:
                    ts("dve", yv, tap_src(c, 0), w_dw_c[:, c, 0:1], b_dw_c[:, c:c + 1], ALU.mult, ALU.add,
                       (t_extv[c], t_const), (t_yacc[c],))
                else:
                    stt("dve", yv, tap_src(c, j), w_dw_c[:, c, j:j + 1], yv, ALU.mult, ALU.add,
                        (t_extv[c], t_const, t_yacc[c]), (t_yacc[c],))
        for c in range(ndve, KC):
            py, tpy = nbank()
            for j in range(CONV_W):
                mm(py[:, 0:N], DG[:, (c - ndve) * CONV_W + j, :], extb[:, c - ndve, j:j + N], j == 0, j == CONV_W - 1,
                   (t_const, t_extb[c - ndve]), (tpy,))
            act(yacc[:, c, 0:N], py[:, 0:N], AF.Identity, (tpy, t_const), (t_yacc[c],), bias=b_dw_c[:, c:c + 1])
        for c in range(KC):
            tcopy("dve" if c % 2 == 0 else "pool", sc22[:, c, 0:N], yacc[:, c, 0:N], (t_yacc[c],), (t_sc22[c],))
            act(sc22[:, KC + c, 0:N], yacc[:, c, 0:N], AF.Square, (t_yacc[c],), (t_sc22[KC + c],))
        pm, tpm = nbank()
        pq, tpq = nbank()
        for c in range(KC):
            mm(pm[:, 0:N], posones, sc22[:, c, 0:N], c == 0, c == KC - 1, (t_const, t_sc22[c]), (tpm,))
        for c in range(KC):
            mm(pq[:, 0:N], posones, sc22[:, KC + c, 0:N], c == 0, c == KC - 1, (t_const, t_sc22[KC + c]), (tpq,))
        ts("dve", mean_t[:, 0:N], pm[:, 0:N], 1.0 / D, None, ALU.mult, None, (tpm,), (t_mean,))
        m2, tm2 = tmpa_ring.next()
        tt("dve", m2[:, 0:N], mean_t[:, 0:N], mean_t[:, 0:N], ALU.mult, (t_mean,), (tm2,))
        stt("dve", m2[:, 0:N], pq[:, 0:N], 1.0 / D, m2[:, 0:N], ALU.mult, ALU.subtract, (tpq, tm2), (tm2,))
        act(m2[:, 0:N], m2[:, 0:N], AF.Ln, (tm2, t_const), (tm2,), bias=epsc[:, :])
        act(rstd_t[:, 0:N], m2[:, 0:N], AF.Exp, (tm2,), (t_rstd,), scale=-0.5)
        for c in range(KC):
            d1, td1 = tmpa_ring.next()
            tt("dve", d1[:, 0:N], yacc[:, c, 0:N], mean_t[:, 0:N], ALU.subtract, (t_yacc[c], t_mean), (td1,))
            tt("dve", d1[:, 0:N], d1[:, 0:N], rstd_t[:, 0:N], ALU.mult, (td1, t_rstd), (td1,))
            act(hT[:, c, 0:N], d1[:, 0:N], AF.Silu, (td1, t_const), (t_hT,),
                bias=ln_b_c[:, c:c + 1], scale=ln_g_c[:, c:c + 1])
        linear_tm(w_out_tm, D, KC, hT, lambda kc: t_hT, blist(xblocks), resid_add(xblocks),
                  bias_row=b_out_r)

    def ffn(l, xblocks, N, hist_src, gate_only=False, valid=None, hist_out=None):
        rms_to_hT(xblocks)
        for c in range(FC):
            eg, teg = extg_ring.next()
            res = {}

            def consg(i, oc, pb, tpb, res=res):
                res["g"] = (pb, tpb)
            linear_fm(w_gate_fm[l], [c], N, consg)
            pg, tpg = res["g"]
            hist_src(c, eg, teg)
            if valid is not None:
                tt("dve", eg[:, 2:2 + N], pg, valid[:, 0:N], ALU.mult, (tpg, t_const), (teg,))
            else:
                act(eg[:, 2:2 + N], pg, AF.Copy, (tpg,), (teg,))
            if hist_out is not None:
                hist_out(c, eg, teg)
            if gate_only:
                continue

            def consu(i, oc, pb, tpb, res=res):
                res["u"] = (pb, tpb)
            linear_fm(w_up_fm[l], [c], N, consu)
            pu, tpu = res["u"]
            gc, tgc = gc_ring.next()
            for j in range(FFN_W):
                if j == 0:
                    ts("dve", gc[:, 0:N], eg[:, 0:N], fw_dw_c[:, l, c, 0:1], None, ALU.mult, None,
                       (teg, t_const), (tgc,))
                else:
                    stt("dve", gc[:, 0:N], eg[:, j:j + N], fw_dw_c[:, l, c, j:j + 1], gc[:, 0:N], ALU.mult, ALU.add,
                        (teg, t_const, tgc), (tgc,))
            sl, tsl = sil_ring.next()
            act(sl[:, 0:N], gc[:, 0:N], AF.Silu, (tgc, t_const), (tsl,), bias=fb_dw_c[:, l, c:c + 1])
            tt("dve", sc22[:, c, 0:N], sl[:, 0:N], pu, ALU.mult, (tsl, tpu), (t_sc22[c],))
        if gate_only:
            return
        linear_tm(w_down_tm[l], D, FC, sc22, lambda kc: t_sc22[kc], blist(xblocks),
                  resid_add(xblocks), cw=CWD)

    def final_out(xblocks, out_rows):
        for b, (xa, tx, n) in enumerate(xblocks):
            row_stats(xa, tx, n, b)
            yo, tyo = yout_ring.next()
            stt("dve", yo[:n, :], xa, rstd[:n, b:b + 1], gfin[:n, :], ALU.mult, ALU.mult, (tx, t_statb[b], t_const), (tyo,))
            dma("sp", out_rows(b, n), yo[:n, :], (tyo,), ())

    def attention(nq, nkb_full, ndiag):
        nkb = nkb_full + ndiag
        nkeys = nkb * 128
        for pr in range(KC):
            Kp, tKp = Kp_ring.next()
            Vp, tVp = Vp_ring.next()
            dma("sp", Kp[:, 0:nkeys], kt_scr[pr, :, 0:nkeys], (trk_kt,), (tKp,))
            dma("sp", Vp[:, 0:nkb, :], v_scr[0:nkeys, pr * 128:(pr + 1) * 128].rearrange("(k p) d -> p k d", p=128),
                (trk_v,), (tVp,))
            po, tpo = nbank(6, 8)
            state = dict(S=None, first=True)

            def stage_a(kb):
                jd = kb - nkb_full
                c0 = 128 * jd if jd >= 0 else 0
                pz2, tpz = pzpair_ring.next()
                e2, te = e_ring.next()
                for hh in range(2):
                    r0 = hh * 64
                    mm(pz2[:, hh, c0:nq], Kp[r0:r0 + 64, kb * 128:(kb + 1) * 128], QT[r0:r0 + 64, pr, c0:nq],
                       True, False, (tKp, t_QT), (tpz,))
                    if jd >= 0:
                        mm(pz2[:, hh, c0:c0 + 128], ident_b, maskneg, False, False, (t_const,), (tpz,))
                for hh in range(2):
                    ts("dve", pz2[:, hh, c0:nq], pz2[:, hh, c0:nq], bh_bc[:, 2 * pr + hh:2 * pr + hh + 1], None,
                       ALU.add, None, (tpz, t_const), (tpz,))
                act(e2[:, :, c0:nq], pz2[:, :, c0:nq], AF.Exp, (tpz, t_const), (te,), bias=nullb_c[:, kb:kb + 1])
                sp2, tsp = sp_ring.next()
                act(sp2[:, :, c0:nq], e2[:, :, c0:nq], AF.Ln, (te,), (tsp,), bias=1.0)
                return (pz2, tpz, sp2, tsp, c0)

            def stage_b(kb, a):
                pz2, tpz, sp2, tsp, c0 = a
                S_prev = state["S"]
                for hh in range(2):
                    mm(pz2[:, hh, c0:nq], negtri, sp2[:, hh, c0:nq], False, S_prev is None, (t_const, tsp), (tpz,))
                    if S_prev is not None:
                        Sa, tSa, s_c0 = S_prev
                        mm(pz2[:, hh, s_c0:nq], negones, Sa[:, hh, s_c0:nq], False, True, (t_const, tSa), (tpz,))
                W2, tW = W_ring.next()
                act(W2[:, :, c0:nq], pz2[:, :, c0:nq], AF.Exp, (tpz, t_const), (tW,), bias=nullb_c[:, kb:kb + 1])
                if kb > 0:
                    Sn, tSn = S_ring.next()
                    if S_prev is None:
                        tcopy("pool", Sn[:, :, c0:nq], sp2[:, :, c0:nq], (tsp,), (tSn,))
                    else:
                        Sa, tSa, s_c0 = S_prev
                        if s_c0 > c0:
                            tcopy("pool", Sn[:, :, c0:s_c0], sp2[:, :, c0:s_c0], (tsp,), (tSn,))
                        tt("pool", Sn[:, :, s_c0:nq], Sa[:, :, s_c0:nq], sp2[:, :, s_c0:nq], ALU.add, (tSa, tsp), (tSn,))
                    state["S"] = (Sn, tSn, c0)
                return (W2, tW, c0)

            def stage_c(kb, wv):
                W2, tW, c0 = wv
                for hh in range(2):
                    r0 = hh * 64
                    if state["first"] and c0 > 0:
                        mm(po[r0:r0 + 64, 0:nq], zero_row[0:1, 0:64], zero_row[0:1, 0:nq], True, False,
                           (t_const,), (tpo,))
                    mm(po[r0:r0 + 64, c0:nq], Vp[:, kb, r0:r0 + 64], W2[:, hh, c0:nq],
                       state["first"] and c0 == 0, kb == 0, (tVp, tW), (tpo,))
                state["first"] = False

            kbs = list(range(nkb - 1, -1, -1))
            pa, pw = {}, {}
            ns = len(kbs)
            for i in range(ns + 2):
                if i < ns:
                    pa[i] = stage_a(kbs[i])
                j = i - 1
                if 0 <= j < ns:
                    pw[j] = stage_b(kbs[j], pa.pop(j))
                k = j - 1
                if 0 <= k < ns:
                    stage_c(kbs[k], pw.pop(k))
            tcopy("dve", attT[:, pr, 0:nq], po[:, 0:nq], (tpo,), (t_attT[pr],))

    def layer1(xblocks, N, q0, nkb_full, ndiag, own_tile, hist_src, hist_out, halo=False):
        rms_to_hT(xblocks)

        def consq(i, oc, pb, tpb):
            if oc < KC:
                act(QT[:, oc, 0:N], pb, AF.Copy, (tpb,), (t_QT,), scale=SCALE)
            else:
                tcopy("dve", KTt[:, oc - KC, 0:N], pb, (tpb,), (t_KTt,))
        if halo:
            linear_fm(w_qk_fm, list(range(KC)), N, consq)
        else:
            ocs = list(range(2 * KC)) if own_tile is not None else list(range(KC, 2 * KC))
            linear_fm(w_qk_fm, ocs, N, consq)
            dma("sp", kt_scr[:, :, q0:q0 + N].rearrange("c p n -> p c n"), KTt[:, :, 0:N], (t_KTt,), (trk_kt,))
            kvb = {b: kvtm_ring.next() for b, n in blist(xblocks)}

            def conskv(c0, cw, b_, n_, pb, tpb):
                kv, tkv = kvb[b_]
                tcopy("dve" if (c0 // cw) % 2 == 0 else "act", kv[:n_, c0:c0 + cw], pb, (tpb,), (tkv,))
            linear_tm(w_kv_tm, 2 * D, KC, hT, lambda kc: t_hT, blist(xblocks), conskv)
            for b, n in blist(xblocks):
                kv, tkv = kvb[b]
                vb, tvb = vbf_ring.next()
                tcopy("pool", vb[:n, :], kv[:n, D:2 * D], (tkv,), (tvb,))
                dma("sp", v_scr[q0 + b * 128:q0 + b * 128 + n, :], vb[:n, :], (tvb,), (trk_v,))
                if own_tile is not None:
                    r = own_tile * TS + b * 128
                    dma("sp", k_p[r:r + n, :], kv[:n, 0:D], (tkv,), ())
                    dma("sp", v_p[r:r + n, :], kv[:n, D:2 * D], (tkv,), ())
        if own_tile is None and not halo:
            return
        P.barrier()
        attention(N, nkb_full, ndiag)
        P.barrier()
        linear_tm(w_o_tm, D, KC, attT, lambda kc: t_attT[kc], blist(xblocks), resid_add(xblocks))
        if halo:
            ffn(1, xblocks, N, hist_src, gate_only=True, valid=valid_c[:, TS - 128:TS], hist_out=hist_out)
            return
        ffn(1, xblocks, N, hist_src, hist_out=hist_out)

    memset("pool", hist_glu, 0.0, (t_hglu,))

    def hist_from(l):
        def f(c, eg, teg):
            tcopy("pool", eg[:, 0:2], hist_g[l][:, c, :], (t_hist[l],), (teg,))
        return f

    def hist_to(l, N, export):
        def f(c, eg, teg):
            tcopy("pool", hist_g[l][:, c, :], eg[:, N:N + 2], (teg,), (t_hist[l],))
            if export and c == FC - 1:
                for tt_ in range(2):
                    dma("pool", sf_p[l, tt_:tt_ + 1, :].rearrange("o (c p) -> p (o c)", p=128), hist_g[l][:, :, tt_],
                        (t_hist[l],), (), allow_slow_non_contiguous=True)
        return f

    for t in range(NT):
        own = t - OWN_T0 if t >= OWN_T0 else None
        last = t == NT - 1
        xblocks = [(xt[:, b, :], t_xt[b], 128) for b in range(4)]
        for b in range(4):
            dma("sp", xt[:, b, :], xwin[t * TS + b * 128:t * TS + (b + 1) * 128, :], (), (t_xt[b],))
        vmask = valid_c if t == OWN_T0 - 1 else None
        for c in range(KC):
            tcopy("pool", ext[:, c, 0:30], hist_glu[:, c, :], (t_hglu,), (t_ext[c],))
        conv_module(xblocks, TS, t_ext, lambda c: ext[:, c, 30:30 + TS], lambda c, j: ext[:, c, j:j + TS],
                    lambda a: a, valid=vmask, use_pe=True)
        if last:
            P.barrier()
            pb0, tp0 = nbank()
            pb1, tp1 = nbank()
            hk = KC // 2
            for c in range(KC):
                pbx, tpx = (pb0, tp0) if c < hk else (pb1, tp1)
                cc = c % hk
                P.op("pe", lambda e, c=c, pbx=pbx, cc=cc: e.transpose(
                    pbx[0:30, cc * 128:(cc + 1) * 128], ext[:, c, TS:TS + 30], ident_f),
                    (t_ext[c], t_const), (tpx,))
            tcopy("dve", xhalo[0:30, 0:hk * 128], pb0[0:30, 0:hk * 128], (tp0,), (t_xhalo,))
            tcopy("dve", xhalo[0:30, hk * 128:2 * hk * 128], pb1[0:30, 0:hk * 128], (tp1,), (t_xhalo,))
            dma("sp", sc_p[:, :], xhalo[0:30, :], (t_xhalo,), ())
        for c in range(KC):
            tcopy("pool", hist_glu[:, c, :], ext[:, c, TS:TS + 30], (t_ext[c],), (t_hglu,))
        P.barrier()
        ffn(0, xblocks, TS, hist_from(0), valid=vmask, hist_out=hist_to(0, TS, last))
        P.barrier()
        if t == OWN_T0 - 1:
            tcopy("pool", xhalo, xt[:, 3, :], (t_xt[3],), (t_xhalo,))
        if own is None:
            layer1(xblocks, TS, t * TS, 0, 0, None, None, None)
            P.barrier()
            if t == OWN_T0 - 1:
                hb = [(xhalo, t_xhalo, 128)]
                layer1(hb, 128, OWN_T0 * TS - 128, OWN_T0 * 4 - 1, 1, None, hist_from(1), hist_to(1, 128, False),
                       halo=True)
                P.barrier()
        else:
            layer1(xblocks, TS, t * TS, t * 4, 4, own, hist_from(1), hist_to(1, TS, last))
            P.barrier()
            final_out(xblocks, lambda b, n: y_p[own * TS + b * 128:own * TS + b * 128 + n, :])
            P.barrier()

    N = NTS
    xs = xt[0:N, 0, :]
    t_xs = t_xt[0]
    xsb = [(xs, t_xs, N)]
    stage = xt[:, 1, :]
    t_stage = t_xt[1]
    glu_tm = xt[:, 2, :]
    t_glutm = t_xt[2]
    kvs = xt[:, 2:4, :].rearrange("p a d -> p (a d)")
    t_kvs = t_xt[3]
    dma("sp", xs, x_s, (), (t_xs,))
    ext_s = ext_full[:, :, 0:NSEQ * EW].rearrange("p k (s j) -> p k s j", j=EW)
    t_exts = [Trk() for _ in range(KC)]
    gl_c = R4.alloc("gl_c", [128, KC, N], F32)
    t_glc = Trk()
    SPB = 4
    for s0 in range(0, NSEQ, SPB):
        ns = min(SPB, NSEQ - s0)
        rows = ns * 30
        dma("sp", stage[0:rows, :], st_conv[s0 * 30:s0 * 30 + rows, :], (), (t_stage,))
        for c in range(KC):
            pb, tpb = nbank()
            P.op("pe", lambda e, c=c, pb=pb, rows=rows: e.transpose(
                pb[:, 0:rows], stage[0:rows, c * 128:(c + 1) * 128], ident_f[0:rows, 0:rows]),
                (t_stage, t_const), (tpb,))
            tcopy("dve", ext_s[:, c, s0:s0 + ns, 0:30], pb[:, 0:rows].rearrange("p (s j) -> p s j", j=30),
                  (tpb,), (t_exts[c],))
        for si in range(ns):
            dma("sp", sc_s[s0 + si, 0:26, :], stage[si * 30 + 4:si * 30 + 30, :], (t_stage,), ())

    def view_s(a):
        return a.rearrange("p (s i) -> p s i", i=TQ)

    conv_module(xsb, N, t_exts, lambda c: ext_s[:, c, :, 30:30 + TQ], lambda c, j: ext_s[:, c, :, j:j + TQ], view_s)
    for c in range(KC):
        tcopy("pool", view_s(gl_c[:, c, :]), ext_s[:, c, :, 30:30 + TQ], (t_exts[c],), (t_glc,))
    pb0, tp0 = nbank()
    pb1, tp1 = nbank()
    hk = KC // 2
    for c in range(KC):
        pbx, tpx = (pb0, tp0) if c < hk else (pb1, tp1)
        cc = c % hk
        P.op("pe", lambda e, c=c, pbx=pbx, cc=cc: e.transpose(
            pbx[0:N, cc * 128:(cc + 1) * 128], gl_c[:, c, :], ident_f), (t_glc, t_const), (tpx,))
    tcopy("dve", glu_tm[0:N, 0:hk * 128], pb0[0:N, 0:hk * 128], (tp0,), (t_glutm,))
    tcopy("dve", glu_tm[0:N, hk * 128:2 * hk * 128], pb1[0:N, 0:hk * 128], (tp1,), (t_glutm,))
    for s in range(NSEQ):
        dma("sp", sc_s[s, 26:30, :], glu_tm[s * TQ:(s + 1) * TQ, :], (t_glutm,), ())
    P.barrier()

    hs = [R4.alloc("hs%d" % l, [128, FC, NSEQ, 2], F32) for l in range(2)]
    t_hs = [Trk(), Trk()]
    gnew = [R4.alloc("gnew%d" % l, [128, FC, NSEQ * 2], F32) for l in range(2)]
    t_gnew = [Trk(), Trk()]
    R1.reset()
    stg2 = R1.alloc("stg2", [128, DFF], F32)
    t_stg2 = Trk()
    RS = NSEQ * 2
    for l in range(2):
        dma("sp", stg2[0:RS, :], st_ffn[l], (), (t_stg2,))
        for c in range(FC):
            pb, tpb = nbank()
            P.op("pe", lambda e, c=c, pb=pb: e.transpose(
                pb[:, 0:RS], stg2[0:RS, c * 128:(c + 1) * 128], ident_f[0:RS, 0:RS]), (t_stg2, t_const), (tpb,))
            tcopy("dve", hs[l][:, c, :, :], pb[:, 0:RS].rearrange("p (s j) -> p s j", j=2), (tpb,), (t_hs[l],))
    P.barrier()

    def ffn_s(l):
        rms_to_hT(xsb)
        for c in range(FC):
            eg, teg = extg_ring.next()
            egv = eg[:, 0:NSEQ * 6].rearrange("p (s j) -> p s j", j=6)
            res = {}

            def consg(i, oc, pb, tpb, res=res):
                res["g"] = (pb, tpb)
            linear_fm(w_gate_fm[l], [c], N, consg)
            pg, tpg = res["g"]
            tcopy("pool", egv[:, :, 0:2], hs[l][:, c, :, :], (t_hs[l],), (teg,))
            tcopy("dve", egv[:, :, 2:2 + TQ], view_s(pg), (tpg,), (teg,))
            tcopy("pool", gnew[l][:, c, :].rearrange("p (s j) -> p s j", j=2), egv[:, :, 4:6], (teg,), (t_gnew[l],))

            def consu(i, oc, pb, tpb, res=res):
                res["u"] = (pb, tpb)
            linear_fm(w_up_fm[l], [c], N, consu)
            pu, tpu = res["u"]
            gc, tgc = gc_ring.next()
            dst = view_s(gc[:, 0:N])
            for j in range(FFN_W):
                src = egv[:, :, j:j + TQ]
                if j == 0:
                    ts("dve", dst, src, fw_dw_c[:, l, c, 0:1], None, ALU.mult, None, (teg, t_const), (tgc,))
                else:
                    stt("dve", dst, src, fw_dw_c[:, l, c, j:j + 1], dst, ALU.mult, ALU.add,
                        (teg, t_const, tgc), (tgc,))
            sl, tsl = sil_ring.next()
            act(sl[:, 0:N], gc[:, 0:N], AF.Silu, (tgc, t_const), (tsl,), bias=fb_dw_c[:, l, c:c + 1])
            tt("dve", sc22[:, c, 0:N], sl[:, 0:N], pu, ALU.mult, (tsl, tpu), (t_sc22[c],))
        linear_tm(w_down_tm[l], D, FC, sc22, lambda kc: t_sc22[kc], [(0, N)], resid_add(xsb), cw=CWD)
        for c0 in range(0, FC, 4):
            pb, tpb = nbank()
            nn = min(4, FC - c0)
            for cc in range(nn):
                P.op("pe", lambda e, c=c0 + cc, cc=cc, pb=pb: e.transpose(
                    pb[0:RS, cc * 128:(cc + 1) * 128], gnew[l][:, c, :], ident_f), (t_gnew[l], t_const), (tpb,))
            tcopy("dve", stg2[0:RS, c0 * 128:(c0 + nn) * 128], pb[0:RS, 0:nn * 128], (tpb,), (t_stg2,))
        dma("sp", sf_s[l], stg2[0:RS, :], (t_stg2,), ())

    ffn_s(0)
    P.barrier()

    rms_to_hT(xsb)
    QTs = R4.alloc("QTs", [128, KC, N], BF16)
    KTn = R4.alloc("KTn", [128, KC, N], BF16)
    t_QTs, t_KTn = Trk(), Trk()

    def consq_s(i, oc, pb, tpb):
        if oc < KC:
            act(QTs[:, oc, :], pb, AF.Copy, (tpb,), (t_QTs,), scale=SCALE)
        else:
            tcopy("dve", KTn[:, oc - KC, :], pb, (tpb,), (t_KTn,))
    linear_fm(w_qk_fm, list(range(2 * KC)), N, consq_s)

    def conskv_s(c0, cw, b, n, pb, tpb):
        tcopy("dve", kvs[:n, c0:c0 + cw], pb, (tpb,), (t_kvs,))
    linear_tm(w_kv_tm, 2 * D, KC, hT, lambda kc: t_hT, [(0, N)], conskv_s)
    dma("sp", k_s[:, :], kvs[0:N, 0:D], (t_kvs,), ())
    dma("sp", v_s[:, :], kvs[0:N, D:2 * D], (t_kvs,), ())
    Vn = R4.alloc("Vn", [128, D], BF16)
    t_Vn = Trk()
    tcopy("pool", Vn[0:N, :], kvs[0:N, D:2 * D], (t_kvs,), (t_Vn,))
    Qblk = R4.alloc("Qblk", [128, NSEQ, KC, 2 * TQ], BF16)
    t_Qblk = Trk()
    memset("pool", Qblk, 0.0, (t_Qblk,))
    for pr in range(KC):
        src = QTs[:, pr, :].rearrange("p (s i) -> p s i", i=TQ)
        tcopy("pool", Qblk[0:64, :, pr, 0:TQ], src[0:64], (t_QTs, t_Qblk), (t_Qblk,))
        tcopy("pool", Qblk[64:128, :, pr, TQ:2 * TQ], src[64:128], (t_QTs, t_Qblk), (t_Qblk,))
    NG = NSEQ // G
    GC = G * HC
    brow_f = R4.alloc("brow_f", [1, GC], F32)
    brow = R4.alloc("brow", [1, GC], BF16)
    mnew_f = R4.alloc("mnew_f", [N, NG, GC], F32)
    mnew = R4.alloc("mnew", [N, NG, GC], BF16)
    t_brow = Trk()
    for i in range(TQ):
        dst = brow_f[0:1, :].rearrange("o (s h i) -> o s h i", h=H, i=TQ)[:, :, :, i]
        for s in range(G):
            tcopy("dve", dst[:, s, :], bh_bc[0:1, :], (t_const,), (t_brow,))
    tcopy("dve", brow, brow_f, (t_brow,), (t_brow,))
    dma("sp", mnew_f, mask_new, (), (t_brow,))
    tcopy("dve", mnew, mnew_f, (t_brow,), (t_brow,))
    NI = NSEQ * NPG
    pt_b = R4.alloc("pt_b", [128, NI], I32)
    iot = R4.alloc("iot", [128, NI], I32)
    c128 = R4.alloc("c128", [128, NI], I32)
    idx = R4.alloc("idx", [128, NI], I32)
    t_idx = Trk()
    dma("sp", pt_b, ptab.partition_broadcast(128), (), (t_idx,))
    P.op("pool", lambda e: e.iota(iot, [[0, NI]], base=0, channel_multiplier=1), (), (t_idx,))
    P.op("pool", lambda e: e.iota(c128, [[0, NI]], base=PAGE, channel_multiplier=0), (), (t_idx,))
    tt("pool", idx, pt_b, c128, ALU.mult, (t_idx,), (t_idx,))
    tt("pool", idx, idx, iot, ALU.add, (t_idx,), (t_idx,))
    e_s = Ring([R4.alloc("e_s%d" % i, [128, 512], F32) for i in range(2)])
    sp_s = Ring([R4.alloc("sp_s%d" % i, [128, 512], BF16) for i in range(2)])
    S_s = Ring([R4.alloc("S_s%d" % i, [128, 512], BF16) for i in range(2)])
    W_s = Ring([R4.alloc("W_s%d" % i, [128, 512], BF16) for i in range(2)])
    attTs = R4.alloc("attTs", [128, KC, N], BF16)
    t_attTs = Trk()
    P.barrier()
    R123 = Region(nc, R1.start, R1_SZ + R2_SZ + R3_SZ, "r123")
    kpg_ring = Ring([R123.alloc("kpg%d" % i, [128, D], BF16) for i in range(6)])
    ktp_ring = Ring([R123.alloc("ktp%d" % i, [128, KC, 128], BF16) for i in range(G + 3)])
    vpg_ring = Ring([R123.alloc("vpg%d" % i, [128, D], BF16) for i in range(2 * G)])

    for gi in range(NG):
        s0 = gi * G
        ng = G
        NCOL = ng * HC
        po, tpo = nbank(6, 8)
        mm(po[:, 0:ng * KC * 2 * TQ], zero_row[0:1, 0:128], zero_row[0:1, 0:ng * KC * 2 * TQ], True, False,
           (t_const,), (tpo,))
        S_prev = None
        for blk in range(NPG, -1, -1):
            newb = blk == NPG
            nk = N if newb else 128
            kts, vps = [], []
            if not newb:
                for si in range(ng):
                    s = s0 + si
                    col = s * NPG + blk
                    kp_, tkp = kpg_ring.next()
                    P.dma("pool", lambda e, kp_=kp_, col=col: e.indirect_dma_start(
                        out=kp_, out_offset=None, in_=cache_k,
                        in_offset=bass.IndirectOffsetOnAxis(ap=idx[:, col:col + 1], axis=0)), (t_idx,), (tkp,))
                    vp_, tvp = vpg_ring.next()
                    P.dma("pool", lambda e, vp_=vp_, col=col: e.indirect_dma_start(
                        out=vp_, out_offset=None, in_=cache_v,
                        in_offset=bass.IndirectOffsetOnAxis(ap=idx[:, col:col + 1], axis=0)), (t_idx,), (tvp,))
                    pbT, tpT = nbank(4, 6)
                    pbv = pbT.bitcast(BF16)
                    for c in range(KC):
                        P.op("pe", lambda e, c=c, pbv=pbv, kp_=kp_: e.transpose(
                            pbv[:, c * 128:(c + 1) * 128], kp_[:, c * 128:(c + 1) * 128], ident_b),
                            (tkp, t_const), (tpT,))
                    kt_, tkt = ktp_ring.next()
                    tcopy("dve", kt_, pbv[:, 0:KC * 128].rearrange("p (k n) -> p k n", k=KC), (tpT,), (tkt,))
                    kts.append((kt_, tkt))
                    vps.append((vp_, tvp))
            pz, tpz = nbank(0, 4)
            mm(pz[0:nk, 0:NCOL], ones_row[0:1, 0:nk], brow[0:1, 0:NCOL], True, False, (t_const, t_brow), (tpz,))
            for si in range(ng):
                s = s0 + si
                for pr in range(KC):
                    cols = slice(si * HC + pr * 2 * TQ, si * HC + (pr + 1) * 2 * TQ)
                    if newb:
                        lhsT = KTn[:, pr, 0:N]
                        rd = (t_KTn, t_Qblk)
                    else:
                        lhsT = kts[si][0][:, pr, :]
                        rd = (kts[si][1], t_Qblk)
                    mm(pz[0:nk, cols], lhsT, Qblk[:, s, pr, :], False, False, rd, (tpz,))
            if newb:
                mm(pz[0:nk, 0:NCOL], ident_b[0:N, 0:N], mnew[0:N, gi, 0:NCOL], False, False, (t_const, t_brow), (tpz,))
            e_t, te = e_s.next()
            act(e_t[0:nk, 0:NCOL], pz[0:nk, 0:NCOL], AF.Exp, (tpz,), (te,))
            sp_t, tsp = sp_s.next()
            act(sp_t[0:nk, 0:NCOL], e_t[0:nk, 0:NCOL], AF.Ln, (te,), (tsp,), bias=1.0)
            mm(pz[0:nk, 0:NCOL], negtri[0:nk, 0:nk], sp_t[0:nk, 0:NCOL], False, S_prev is None, (t_const, tsp), (tpz,))
            if S_prev is not None:
                Sa, tSa = S_prev
                mm(pz[0:nk, 0:NCOL], negones[:, 0:nk], Sa[:, 0:NCOL], False, True, (t_const, tSa), (tpz,))
            W_t, tW = W_s.next()
            act(W_t[0:nk, 0:NCOL], pz[0:nk, 0:NCOL], AF.Exp, (tpz,), (tW,))
            if blk > 0:
                Sn, tSn = S_s.next()
                if S_prev is None:
                    memset("pool", Sn[:, 0:NCOL], 0.0, (tSn,))
                    tcopy("pool", Sn[0:nk, 0:NCOL], sp_t[0:nk, 0:NCOL], (tsp, tSn), (tSn,))
                else:
                    Sa, tSa = S_prev
                    tt("pool", Sn[:, 0:NCOL], Sa[:, 0:NCOL], sp_t[:, 0:NCOL], ALU.add, (tSa, tsp), (tSn,))
                S_prev = (Sn, tSn)
            for si in range(ng):
                s = s0 + si
                for pr in range(KC):
                    wc = slice(si * HC + pr * 2 * TQ, si * HC + (pr + 1) * 2 * TQ)
                    oc_ = slice((si * KC + pr) * 2 * TQ, (si * KC + pr + 1) * 2 * TQ)
                    if newb:
                        lhsT = Vn[0:N, pr * 128:(pr + 1) * 128]
                        rd = (t_Vn, tW)
                    else:
                        lhsT = vps[si][0][:, pr * 128:(pr + 1) * 128]
                        rd = (vps[si][1], tW)
                    mm(po[:, oc_], lhsT, W_t[0:nk, wc], False, blk == 0 and si == ng - 1 and pr == KC - 1, rd, (tpo,))
        for si in range(ng):
            s = s0 + si
            pov = po[:, si * KC * 2 * TQ:(si + 1) * KC * 2 * TQ].rearrange("p (k j) -> p k j", j=2 * TQ)
            tcopy("dve", attTs[0:64, :, s * TQ:(s + 1) * TQ], pov[0:64, :, 0:TQ], (tpo,), (t_attTs,))
            tcopy("dve", attTs[64:128, :, s * TQ:(s + 1) * TQ], pov[64:128, :, TQ:2 * TQ], (tpo,), (t_attTs,))
    P.barrier()
    linear_tm(w_o_tm, D, KC, attTs, lambda kc: t_attTs, [(0, N)], resid_add(xsb))
    ffn_s(1)
    P.barrier()
    final_out(xsb, lambda b, n: y_s[0:n, :])

    P.barrier()
    P.emit()
    return nc


_CACHE = {}


def _consts():
    c = np.zeros((128, 5 * 128), np.float32)
    k = np.arange(128)[:, None]
    q = np.arange(128)[None, :]
    c[:, 0:128] = np.eye(128, dtype=np.float32)
    c[:, 128:256] = -(k >= q).astype(np.float32)
    c[:, 256:384] = -1.0
    c[:, 384:512] = np.where(k >= q, NEG, 0.0)
    c[:, 512:640] = 1.0
    return c


def _mask_new(nseq, h):
    hc = h * TQ
    g = max(1, min(nseq, 512 // hc))
    ng = nseq // g
    m = np.full((nseq * TQ, ng, g * hc), NEG, np.float32)
    for s in range(nseq):
        gi, sl = s // g, s % g
        for j in range(TQ):
            for i in range(TQ):
                if j < i:
                    m[s * TQ + j, gi, sl * hc + i:(sl + 1) * hc:TQ] = 0.0
    return m


def kernel(x_prompt, x_sample, state_conv, state_ffn, cache_k, cache_v, page_table,
           norm_mix, norm_ffn, norm_final, conv_w_in, conv_b_in, conv_w_dw, conv_b_dw, conv_ln_g,
           conv_ln_b, conv_w_out, conv_b_out, attn_w_qkv, attn_w_o, attn_b_logit, ffn_w_gate, ffn_w_up,
           ffn_w_dw, ffn_b_dw, ffn_w_down):
    f32 = np.float32
    x_prompt = np.asarray(x_prompt, f32)
    B, SEQ, D = x_prompt.shape
    x_sample = np.asarray(x_sample, f32)
    DB = x_sample.shape[0]
    n_cores = 8
    assert B * 2 == n_cores and DB % n_cores == 0
    NSEQ = DB // n_cores
    page_table = np.asarray(page_table, np.int32)
    NPG = page_table.shape[1]
    cache_k = np.asarray(cache_k, f32)
    cache_v = np.asarray(cache_v, f32)
    NPOOL = cache_k.shape[1]
    H = np.asarray(attn_b_logit).shape[1]
    DFF = np.asarray(ffn_w_gate).shape[2]
    cfg = dict(D=D, DFF=DFF, H=H, WIN=SEQ, NSEQ=NSEQ, NPG=NPG, NPOOL=NPOOL)
    key = tuple(sorted(cfg.items()))
    if key not in _CACHE:
        _CACHE[key] = build(cfg)
    nc = _CACHE[key]
    NB = SEQ // 128
    half = SEQ // 2
    state_conv = np.asarray(state_conv, f32)
    state_ffn = np.asarray(state_ffn, f32)
    ck = np.ascontiguousarray(cache_k[0].reshape(NPOOL * PAGE, D))
    cv = np.ascontiguousarray(cache_v[0].reshape(NPOOL * PAGE, D))
    pvec_d = np.concatenate([
        np.asarray(conv_b_in, f32).reshape(2, D), np.asarray(conv_b_dw, f32).reshape(1, D),
        np.asarray(conv_ln_g, f32).reshape(1, D), np.asarray(conv_ln_b, f32).reshape(1, D),
        np.asarray(conv_w_dw, f32).reshape(CONV_W, D), np.asarray(norm_mix, f32)[0:1], np.asarray(norm_ffn, f32)[0:1],
        np.asarray(norm_mix, f32)[1:2], np.asarray(norm_ffn, f32)[1:2]], axis=0)
    pvec_f = np.concatenate([np.asarray(ffn_b_dw, f32).reshape(2, DFF),
                             np.asarray(ffn_w_dw, f32).reshape(2 * FFN_W, DFF)], axis=0)
    shared = dict(
        pvec_d=np.ascontiguousarray(pvec_d), pvec_f=np.ascontiguousarray(pvec_f),
        cache_k=ck, cache_v=cv, consts=_consts(), mask_new=_mask_new(NSEQ, H),
        norm_mix=np.asarray(norm_mix, f32), norm_ffn=np.asarray(norm_ffn, f32),
        norm_final=np.asarray(norm_final, f32).reshape(1, D),
        conv_w_in=np.asarray(conv_w_in, f32)[0], conv_b_in=np.asarray(conv_b_in, f32),
        conv_w_dw=np.asarray(conv_w_dw, f32)[0], conv_b_dw=np.asarray(conv_b_dw, f32),
        conv_ln_g=np.asarray(conv_ln_g, f32), conv_ln_b=np.asarray(conv_ln_b, f32),
        conv_w_out=np.asarray(conv_w_out, f32)[0], conv_b_out=np.asarray(conv_b_out, f32),
        attn_w_qkv=np.asarray(attn_w_qkv, f32)[0], attn_w_o=np.asarray(attn_w_o, f32)[0],
        attn_b=np.asarray(attn_b_logit, f32),
        ffn_w_gate=np.asarray(ffn_w_gate, f32), ffn_w_up=np.asarray(ffn_w_up, f32),
        ffn_w_dw=np.asarray(ffn_w_dw, f32), ffn_b_dw=np.asarray(ffn_b_dw, f32).reshape(2, 1, DFF),
        ffn_w_down=np.asarray(ffn_w_down, f32),
    )
    in_maps = []
    for c in range(n_cores):
        b, r = c // 2, c % 2
        if r == 1:
            xwin = x_prompt[b]
            nullb = np.zeros((128, NB), f32)
            valid = np.ones((1, 512), f32)
        else:
            xwin = np.concatenate([np.zeros((half, D), f32), x_prompt[b, :half]], axis=0)
            nullb = np.zeros((128, NB), f32)
            nullb[:, :NB // 2] = NEG
            valid = np.zeros((1, 512), f32)
        sl = slice(c * NSEQ, (c + 1) * NSEQ)
        m = dict(shared)
        m.update(
            xwin=np.ascontiguousarray(xwin), nullb=nullb, valid3=valid,
            x_s=np.ascontiguousarray(x_sample[sl].reshape(NSEQ * TQ, D)),
            st_conv=np.ascontiguousarray(state_conv[0, sl].reshape(NSEQ * 30, D)),
            st_ffn=np.ascontiguousarray(state_ffn[:, sl].reshape(2, NSEQ * 2, DFF)),
            ptab=np.ascontiguousarray(page_table[sl].reshape(1, NSEQ * NPG)),
        )
        in_maps.append(m)
    res = run_bass_kernel_spmd(nc, in_maps, core_ids=list(range(n_cores))).results

    y_p = np.zeros((B, SEQ, D), f32)
    k_p = np.zeros((1, B, SEQ, H, DH), f32)
    v_p = np.zeros((1, B, SEQ, H, DH), f32)
    sc_p = np.zeros((1, B, 30, D), f32)
    sf_p = np.zeros((2, B, 2, DFF), f32)
    y_s = np.zeros((DB, TQ, D), f32)
    k_s = np.zeros((1, DB, TQ, H, DH), f32)
    v_s = np.zeros((1, DB, TQ, H, DH), f32)
    sc_s = np.zeros((1, DB, 30, D), f32)
    sf_s = np.zeros((2, DB, 2, DFF), f32)
    for c in range(n_cores):
        b, r = c // 2, c % 2
        o = res[c]
        rows = slice(r * half, (r + 1) * half)
        y_p[b, rows] = o["y_p"]
        k_p[0, b, rows] = o["k_p"].reshape(half, H, DH)
        v_p[0, b, rows] = o["v_p"].reshape(half, H, DH)
        if r == 1:
            sc_p[0, b] = o["sc_p"]
            sf_p[:, b] = o["sf_p"]
        sl = slice(c * NSEQ, (c + 1) * NSEQ)
        y_s[sl] = o["y_s"].reshape(NSEQ, TQ, D)
        k_s[0, sl] = o["k_s"].reshape(NSEQ, TQ, H, DH)
        v_s[0, sl] = o["v_s"].reshape(NSEQ, TQ, H, DH)
        sc_s[0, sl] = o["sc_s"]
        sf_s[:, sl] = o["sf_s"].reshape(2, NSEQ, 2, DFF)
    return (y_p, y_s, sc_p, sf_p, k_p, v_p, sc_s, sf_s, k_s, v_s)
```
